# Optimizing a Trainium2 kernel written in Bass

```python
import jax, jax.numpy as jnp
from jax import lax
import numpy as np

D_MODEL = 2048
BATCH = 4
SEQ = 8192
DEPTH = 2

N_MEM = 256
EPS = 1e-6

SC_WIDTH = 1024
SC_KERNEL = 3

SSD_HEADS = 32
SSD_HEAD_DIM = 64
SSD_WIDTH = SSD_HEADS * SSD_HEAD_DIM
SSD_STATE = 128
SSD_GROUPS = 4
SSD_CONV = 4
SSD_CHUNK = 128
SSD_CONV_DIM = SSD_WIDTH + 2 * SSD_GROUPS * SSD_STATE
DT_MIN = 0.001
DT_MAX = 0.1

MLA_HEADS = 16
MLA_Q_RANK = 768
MLA_KV_RANK = 512
MLA_NOPE = 128
MLA_ROPE = 64
MLA_V = 128
MLA_WIDTH = MLA_HEADS * MLA_V
ROPE_THETA = 10000.0
Q_BLOCK = 128

MEM_HEADS = 4
MEM_HEAD_DIM = 128
MEM_WIDTH = MEM_HEADS * MEM_HEAD_DIM

EVEN_SPLITS = (SC_WIDTH, SC_WIDTH, SC_WIDTH, SC_WIDTH,
               SSD_WIDTH, SSD_CONV_DIM, SSD_HEADS, MEM_WIDTH, MEM_WIDTH)
EVEN_IN = sum(EVEN_SPLITS)
EVEN_OUT = SC_WIDTH + SSD_WIDTH + MEM_WIDTH
ODD_SPLITS = (MLA_Q_RANK, MLA_KV_RANK, MLA_ROPE, MLA_WIDTH, MEM_WIDTH, MEM_WIDTH)
ODD_IN = sum(ODD_SPLITS)
ODD_OUT = MLA_WIDTH + MEM_WIDTH

kernel_name = "hybrid_shortconv_ssd_mla_memory"


def _split(t, sizes):
    idx = [int(i) for i in np.cumsum(sizes)[:-1]]
    return jnp.split(t, idx, axis=-1)


def rms_norm(x, w):
    xf = x.astype(jnp.float32)
    y = xf * lax.rsqrt(jnp.mean(xf * xf, axis=-1, keepdims=True) + EPS)
    return (y * w.astype(jnp.float32)).astype(x.dtype)


def causal_depthwise_conv(u, w):
    k_w, c = w.shape
    return lax.conv_general_dilated(
        u, w.reshape(k_w, 1, c).astype(u.dtype), window_strides=(1,),
        padding=[(k_w - 1, 0)], dimension_numbers=("NWC", "WIO", "NWC"),
        feature_group_count=c)


def rope(t, pos):
    half = MLA_ROPE // 2
    inv = ROPE_THETA ** (-jnp.arange(half, dtype=jnp.float32) / half)
    ang = pos.astype(jnp.float32)[..., None] * inv
    cos = jnp.cos(ang)[:, :, None, :]
    sin = jnp.sin(ang)[:, :, None, :]
    t1 = t[..., :half].astype(jnp.float32)
    t2 = t[..., half:].astype(jnp.float32)
    return jnp.concatenate([t1 * cos - t2 * sin, t1 * sin + t2 * cos], axis=-1).astype(t.dtype)


def memory_attention(q, mem_n, w_mk, w_mv):
    b_, l_, _ = q.shape
    qh = q.reshape(b_, l_, MEM_HEADS, MEM_HEAD_DIM)
    k = (mem_n @ w_mk).reshape(b_, -1, MEM_HEADS, MEM_HEAD_DIM)
    v = (mem_n @ w_mv).reshape(b_, -1, MEM_HEADS, MEM_HEAD_DIM)
    s = jnp.einsum('bqhd,bkhd->bhqk', qh, k).astype(jnp.float32) * (MEM_HEAD_DIM ** -0.5)
    p = jax.nn.softmax(s, axis=-1).astype(v.dtype)
    o = jnp.einsum('bhqk,bkhd->bqhd', p, v)
    return o.reshape(b_, l_, MEM_WIDTH)


def ssd_chunked(xs, dt, a, bm, cm):
    b_, l_, h_, p_ = xs.shape
    nc = l_ // SSD_CHUNK
    hpg = h_ // SSD_GROUPS
    xdt = (xs * dt[..., None]).reshape(b_, nc, SSD_CHUNK, SSD_GROUPS, hpg, p_)
    bm = bm.reshape(b_, nc, SSD_CHUNK, SSD_GROUPS, SSD_STATE)
    cm = cm.reshape(b_, nc, SSD_CHUNK, SSD_GROUPS, SSD_STATE)
    a_cum = jnp.cumsum((dt * a).reshape(b_, nc, SSD_CHUNK, SSD_GROUPS, hpg), axis=2)
    seg = a_cum[:, :, :, None] - a_cum[:, :, None, :]
    causal = (jnp.arange(SSD_CHUNK)[:, None] >= jnp.arange(SSD_CHUNK)[None, :])[:, :, None, None]
    decay = jnp.exp(jnp.where(causal, seg, -jnp.inf))
    cb = jnp.einsum('bcqgn,bckgn->bcqkg', cm, bm)
    y_diag = jnp.einsum('bcqkg,bcqkgh,bckghp->bcqghp', cb, decay, xdt)
    decay_to_end = jnp.exp(a_cum[:, :, -1:] - a_cum)
    states = jnp.einsum('bckgn,bckgh,bckghp->bcghpn', bm, decay_to_end, xdt)
    chunk_decay = jnp.exp(a_cum[:, :, -1])

    def step(s, inp):
        st, dec = inp
        return s * dec[..., None, None] + st, s

    init = jnp.zeros((b_, SSD_GROUPS, hpg, p_, SSD_STATE), jnp.float32)
    _, s_enter = lax.scan(step, init, (jnp.moveaxis(states, 1, 0), jnp.moveaxis(chunk_decay, 1, 0)))
    s_enter = jnp.moveaxis(s_enter, 0, 1)
    y_off = jnp.einsum('bcqgn,bcghpn,bcqgh->bcqghp', cm, s_enter, jnp.exp(a_cum))
    return (y_diag + y_off).reshape(b_, l_, h_, p_)


def causal_block_attention(q, k, v):
    b_, l_, h_, dk = q.shape
    nb = l_ // Q_BLOCK
    scale = dk ** -0.5
    qb = jnp.moveaxis(q.reshape(b_, nb, Q_BLOCK, h_, dk), 1, 0)
    k_pos = jnp.arange(l_)

    def one_block(args):
        q_blk, i = args
        s = jnp.einsum('bqhd,bkhd->bhqk', q_blk, k).astype(jnp.float32) * scale
        q_pos = i * Q_BLOCK + jnp.arange(Q_BLOCK)
        s = jnp.where(k_pos[None, :] <= q_pos[:, None], s, -jnp.inf)
        p = jax.nn.softmax(s, axis=-1).astype(v.dtype)
        return jnp.einsum('bhqk,bkhv->bqhv', p, v)

    o = lax.map(one_block, (qb, jnp.arange(nb)))
    return jnp.moveaxis(o, 0, 1).reshape(b_, l_, h_ * v.shape[-1])


def even_layer(x, mem_n, norm_w, w_in, sc_conv_w, ssd_conv_w, ssd_conv_b, ssd_dt_bias,
               ssd_a_log, ssd_d, ssd_norm_w, w_mk, w_mv, w_out):
    b_, l_, _ = x.shape
    h = rms_norm(x, norm_w)
    proj = h @ w_in
    sc_b, sc_c, sc_v, sc_z, ssd_z, ssd_xbc, ssd_dt, mq, mz = _split(proj, EVEN_SPLITS)
    y_a = sc_b * causal_depthwise_conv(sc_c * sc_v, sc_conv_w)
    y_a = y_a * jax.nn.silu(sc_z)
    xbc = jax.nn.silu(causal_depthwise_conv(ssd_xbc, ssd_conv_w) + ssd_conv_b)
    xs, bm, cm = _split(xbc, (SSD_WIDTH, SSD_GROUPS * SSD_STATE, SSD_GROUPS * SSD_STATE))
    xs = xs.reshape(b_, l_, SSD_HEADS, SSD_HEAD_DIM).astype(jnp.float32)
    bm = bm.reshape(b_, l_, SSD_GROUPS, SSD_STATE).astype(jnp.float32)
    cm = cm.reshape(b_, l_, SSD_GROUPS, SSD_STATE).astype(jnp.float32)
    dt = jax.nn.softplus(ssd_dt.astype(jnp.float32) + ssd_dt_bias.astype(jnp.float32))
    a = -jnp.exp(ssd_a_log.astype(jnp.float32))
    y_b = ssd_chunked(xs, dt, a, bm, cm) + xs * ssd_d.astype(jnp.float32)[:, None]
    y_b = y_b.reshape(b_, l_, SSD_WIDTH) * jax.nn.silu(ssd_z.astype(jnp.float32))
    y_b = y_b.reshape(b_, l_, SSD_GROUPS, SSD_WIDTH // SSD_GROUPS)
    y_b = y_b * lax.rsqrt(jnp.mean(y_b * y_b, axis=-1, keepdims=True) + EPS)
    y_b = (y_b.reshape(b_, l_, SSD_WIDTH) * ssd_norm_w.astype(jnp.float32)).astype(x.dtype)
    y_m = memory_attention(mq, mem_n, w_mk, w_mv) * jax.nn.silu(mz)
    return x + jnp.concatenate([y_a, y_b, y_m], axis=-1) @ w_out


def odd_layer(x, mem_n, positions, norm_w, w_in, mla_q_norm_w, mla_kv_norm_w, mla_w_uq,
              mla_w_ukv, w_mk, w_mv, w_out):
    b_, l_, _ = x.shape
    h = rms_norm(x, norm_w)
    proj = h @ w_in
    c_q, c_kv, k_r, z_c, mq, mz = _split(proj, ODD_SPLITS)
    q = (rms_norm(c_q, mla_q_norm_w) @ mla_w_uq).reshape(b_, l_, MLA_HEADS, MLA_NOPE + MLA_ROPE)
    kv = (rms_norm(c_kv, mla_kv_norm_w) @ mla_w_ukv).reshape(b_, l_, MLA_HEADS, MLA_NOPE + MLA_V)
    q_nope, q_rope = q[..., :MLA_NOPE], rope(q[..., MLA_NOPE:], positions)
    k_nope, v = kv[..., :MLA_NOPE], kv[..., MLA_NOPE:]
    k_rope = jnp.broadcast_to(rope(k_r[:, :, None, :], positions), (b_, l_, MLA_HEADS, MLA_ROPE))
    q_full = jnp.concatenate([q_nope, q_rope], axis=-1)
    k_full = jnp.concatenate([k_nope, k_rope], axis=-1)
    y_c = causal_block_attention(q_full, k_full, v) * jax.nn.silu(z_c)
    y_m = memory_attention(mq, mem_n, w_mk, w_mv) * jax.nn.silu(mz)
    return x + jnp.concatenate([y_c, y_m], axis=-1) @ w_out


def setup_inputs(seed: int = 0) -> dict:
    key = jax.random.key(seed)
    ks = jax.random.split(key, 32)
    f32 = jnp.float32

    def dense(k, fan_in, fan_out):
        return jax.random.normal(k, (fan_in, fan_out), f32) * fan_in ** -0.5

    def gain(k, n):
        return 1.0 + 0.05 * jax.random.normal(k, (n,), f32)

    x = jax.random.normal(ks[0], (BATCH, SEQ, D_MODEL), f32)
    mem = jax.random.normal(ks[1], (BATCH, N_MEM, D_MODEL), f32)
    offsets = jax.random.randint(ks[2], (BATCH, 1), 0, 4096, dtype=jnp.int32)
    positions = offsets + jnp.arange(SEQ, dtype=jnp.int32)[None, :]
    dt0 = jnp.exp(jax.random.uniform(ks[3], (SSD_HEADS,), f32) * (np.log(DT_MAX) - np.log(DT_MIN)) + np.log(DT_MIN))
    return {
        "x": x,
        "mem": mem,
        "positions": positions,
        "mem_norm_w": gain(ks[4], D_MODEL),
        "norm0_w": gain(ks[5], D_MODEL),
        "w_in0": dense(ks[6], D_MODEL, EVEN_IN),
        "sc_conv_w": jax.random.normal(ks[7], (SC_KERNEL, SC_WIDTH), f32) * SC_KERNEL ** -0.5,
        "ssd_conv_w": jax.random.normal(ks[8], (SSD_CONV, SSD_CONV_DIM), f32) * SSD_CONV ** -0.5,
        "ssd_conv_b": 0.01 * jax.random.normal(ks[9], (SSD_CONV_DIM,), f32),
        "ssd_dt_bias": dt0 + jnp.log(-jnp.expm1(-dt0)),
        "ssd_a_log": jnp.log(jax.random.uniform(ks[10], (SSD_HEADS,), f32, 1.0, 16.0)),
        "ssd_d": 1.0 + 0.1 * jax.random.normal(ks[11], (SSD_HEADS,), f32),
        "ssd_norm_w": gain(ks[12], SSD_WIDTH),
        "mem_k0": dense(ks[13], D_MODEL, MEM_WIDTH),
        "mem_v0": dense(ks[14], D_MODEL, MEM_WIDTH),
        "w_out0": dense(ks[15], EVEN_OUT, D_MODEL),
        "norm1_w": gain(ks[16], D_MODEL),
        "w_in1": dense(ks[17], D_MODEL, ODD_IN),
        "mla_q_norm_w": gain(ks[18], MLA_Q_RANK),
        "mla_kv_norm_w": gain(ks[19], MLA_KV_RANK),
        "mla_w_uq": dense(ks[20], MLA_Q_RANK, MLA_HEADS * (MLA_NOPE + MLA_ROPE)),
        "mla_w_ukv": dense(ks[21], MLA_KV_RANK, MLA_HEADS * (MLA_NOPE + MLA_V)),
        "mem_k1": dense(ks[22], D_MODEL, MEM_WIDTH),
        "mem_v1": dense(ks[23], D_MODEL, MEM_WIDTH),
        "w_out1": dense(ks[24], ODD_OUT, D_MODEL),
        "final_norm_w": gain(ks[25], D_MODEL),
    }


def reference(x, mem, positions, mem_norm_w, norm0_w, w_in0, sc_conv_w, ssd_conv_w, ssd_conv_b,
              ssd_dt_bias, ssd_a_log, ssd_d, ssd_norm_w, mem_k0, mem_v0, w_out0, norm1_w, w_in1,
              mla_q_norm_w, mla_kv_norm_w, mla_w_uq, mla_w_ukv, mem_k1, mem_v1, w_out1, final_norm_w):
    mem_n = rms_norm(mem, mem_norm_w)
    even_params = (norm0_w, w_in0, sc_conv_w, ssd_conv_w, ssd_conv_b, ssd_dt_bias, ssd_a_log,
                   ssd_d, ssd_norm_w, mem_k0, mem_v0, w_out0)
    odd_params = (norm1_w, w_in1, mla_q_norm_w, mla_kv_norm_w, mla_w_uq, mla_w_ukv,
                  mem_k1, mem_v1, w_out1)
    layer_params = [even_params, odd_params]
    for layer in range(DEPTH):
        if layer % 2 == 0:
            x = even_layer(x, mem_n, *layer_params[layer])
        else:
            x = odd_layer(x, mem_n, positions, *layer_params[layer])
    return rms_norm(x, final_norm_w)
```

```python
import os
import numpy as np
import ml_dtypes
from contextlib import ExitStack
import concourse.bass as bass
import concourse.mybir as mybir
from concourse.bass_utils import run_bass_kernel_spmd

F32 = mybir.dt.float32
BF16 = mybir.dt.bfloat16
I32 = mybir.dt.int32
AF = mybir.ActivationFunctionType
ALU = mybir.AluOpType

D = 2048
KT = 16
NMEM = 256
EPS = 1e-6
EVEN_IN = 5136
EVEN_OUT = 3584
EVEN_OWN = 1792
ODD_IN = 2880
ODD_OUT = 2560
ODD_OWN = 1280
C_SCB, C_SCC, C_SCV, C_SCZ = 0, 512, 1024, 1536
C_SSDZ = 2048
C_XBC = 3072
C_DT = 4608
C_MQ0 = 4624
C_MZ0 = 4880
C_CQ, C_CKV, C_KR, C_ZC, C_MQ1, C_MZ1 = 0, 768, 1280, 1344, 2368, 2624

ENGS = ("pe", "act", "dve", "pool", "sp")


class _Op:
    __slots__ = ("eng", "fn", "deps", "is_dma", "slot", "dma_val", "sig", "needed", "prev_dma", "ndma")

    def __init__(self, eng, fn, is_dma=False, slot=None, ndma=1):
        self.eng = eng
        self.fn = fn
        self.deps = []
        self.is_dma = is_dma
        self.slot = slot
        self.dma_val = 0
        self.sig = 0
        self.needed = False
        self.prev_dma = None
        self.ndma = ndma


class Sched:
    def __init__(self, nc, es):
        self.nc = nc
        self.es = es
        self.q = {e: [] for e in ENGS}
        self.last_w = {}
        self.readers = {}
        self.slot_last = {}
        self.slot_tot = {}
        self.slot_sem = {}
        self.sigcnt = {e: 0 for e in ENGS}
        self.esem = {e: es.enter_context(nc.semaphore("s_" + e)) for e in ENGS}
        self.waited = {e: {} for e in ENGS}
        self.nops = 0
        self.cc_sem = es.enter_context(nc.semaphore("s_cc"))
        self.cc_cnt = 0
        self.log = None
        self.cc_scratch = es.enter_context(nc.sbuf_tensor("cc_scratch", [128, 4], F32))

    def collective(self, fns):
        self.flush()
        sem = self.cc_sem
        base = self.cc_cnt
        self.cc_cnt += len(fns)
        with self.nc.Block() as block:
            @block.gpsimd
            def _(e):
                for i, fn in enumerate(fns):
                    fn(e).then_inc(sem, 1)
                    e.wait_ge(sem, base + i + 1)
        scr = self.cc_scratch
        self.op("pool", lambda e: e.memset(scr[:], 0.0), w=["cc_scratch"])
        self.flush()

    def sb(self, es, name, shape, dtype):
        self.nsb = getattr(self, "nsb", 0) + 1
        return es.enter_context(self.nc.sbuf_tensor("%s_%d" % (name, self.nsb), list(shape), dtype))

    def _track(self, o, r, w):
        deps = []
        for k in r:
            d = self.last_w.get(k)
            if d is not None:
                deps.append(d)
        for k in w:
            d = self.last_w.get(k)
            if d is not None:
                deps.append(d)
            deps.extend(self.readers.get(k, ()))
        for k in r:
            self.readers.setdefault(k, []).append(o)
        for k in w:
            self.last_w[k] = o
            self.readers[k] = []
        seen = set()
        for d in deps:
            if d is o or id(d) in seen:
                continue
            seen.add(id(d))
            o.deps.append(d)

    def op(self, eng, fn, r=(), w=()):
        o = _Op(eng, fn)
        self._track(o, r, w)
        self.q[eng].append(o)
        self.nops += 1
        return o

    def dma(self, eng, fn, r=(), w=(), slot=None, n=1):
        if slot is None:
            slot = w[0] if len(w) else r[0]
        o = _Op(eng, fn, is_dma=True, slot=slot, ndma=n)
        self._track(o, r, w)
        o.prev_dma = self.slot_last.get(slot)
        self.slot_last[slot] = o
        self.slot_tot[slot] = self.slot_tot.get(slot, 0) + 16 * n
        o.dma_val = self.slot_tot[slot]
        if slot not in self.slot_sem:
            self.slot_sem[slot] = self.es.enter_context(self.nc.semaphore("d%d" % len(self.slot_sem)))
        self.q[eng].append(o)
        self.nops += 1
        return o

    def flush(self):
        nc = self.nc
        lasts = [self.q[e][-1] for e in ENGS if self.q[e]]
        dmas = [o for o in self.slot_last.values()]
        for e in ENGS:
            b = _Op(e, None)
            b.deps = [d for d in lasts if d.eng != e or d.is_dma] + [d for d in dmas]
            self.q[e].append(b)
        for e in ENGS:
            for o in self.q[e]:
                for d in o.deps:
                    if d.is_dma:
                        continue
                    if d.eng == o.eng and d.eng == "pe":
                        continue
                    if d.sig == 0:
                        d.needed = True
        for e in ENGS:
            for o in self.q[e]:
                if o.needed and o.sig == 0:
                    self.sigcnt[e] += 1
                    o.sig = self.sigcnt[e]
        sched = self

        def emit(eng, e):
            waited = sched.waited[eng]

            def wait(sem, val, key):
                if waited.get(key, 0) >= val:
                    return
                waited[key] = val
                if sched.log is not None:
                    sched.log.append((eng, "W", key, val))
                e.wait_ge(sem, val)

            for o in sched.q[eng]:
                for d in o.deps:
                    if d.is_dma:
                        wait(sched.slot_sem[d.slot], d.dma_val, ("d", d.slot))
                    else:
                        if d.eng == eng and eng == "pe":
                            continue
                        wait(sched.esem[d.eng], d.sig, ("e", d.eng))
                if o.fn is None:
                    continue
                if sched.log is not None:
                    sched.log.append((eng, "DMA" if o.is_dma else "OP", o.slot, o.dma_val if o.is_dma else o.sig, o.needed))
                if o.is_dma:
                    if o.prev_dma is not None:
                        wait(sched.slot_sem[o.slot], o.prev_dma.dma_val, ("d", o.slot))
                    ins = o.fn(e)
                    assert len(ins) == o.ndma, (len(ins), o.ndma)
                    for i in ins:
                        i.then_inc(sched.slot_sem[o.slot], 16)
                else:
                    ins = o.fn(e)
                    if o.needed:
                        ins.then_inc(sched.esem[eng], 1)

        with nc.Block() as block:
            @block.tensor
            def _(e):
                emit("pe", e)

            @block.scalar
            def _(e):
                emit("act", e)

            @block.vector
            def _(e):
                emit("dve", e)

            @block.gpsimd
            def _(e):
                emit("pool", e)

            @block.sync
            def _(e):
                emit("sp", e)

        for e in ENGS:
            self.q[e] = []
        self.last_w = {}
        self.readers = {}
        self.slot_last = {}


class K:
    pass


def build_program(NTOK, debug=(), stop_after=99):
    nc = bass.Bass("TRN2", target_bir_lowering=False)
    NG = NTOK // 512
    NCH = NTOK // 128

    def din(name, shape, dt=F32):
        return nc.dram_tensor(name, list(shape), dt, kind="ExternalInput").ap()

    def dscr(name, shape, dt=BF16):
        return nc.dram_tensor(name, list(shape), dt).ap()

    xT = din("xT", [D, NTOK])
    memT = din("memT", [D, NMEM])
    pos = din("pos", [1, NTOK], I32)
    w_in0 = din("w_in0", [D, EVEN_IN])
    w_out0 = din("w_out0", [EVEN_OUT, 1024])
    xTo = din("xTo", [1024, NTOK])
    w_in1 = din("w_in1", [D, ODD_IN])
    w_out1 = din("w_out1", [ODD_OUT, 1024])
    w_uq = din("w_uq", [768, 1536])
    w_ukv = din("w_ukv", [512, 2048])
    mk = [din("mem_k0", [D, 256]), din("mem_k1", [D, 256])]
    mv = [din("mem_v0", [D, 256]), din("mem_v1", [D, 256])]
    vec16 = din("vec16", [128, 5, 16])
    dcol = din("dcol", [128, 8])
    sccw = din("sccw", [128, 4, 3])
    ssdcw = din("ssdcw", [128, 12, 4])
    ssdcb = din("ssdcb", [128, 12])
    hvec = din("hvec", [16, 2])
    qnw = din("qnw", [128, 6])
    kvnw = din("kvnw", [128, 4])
    cmat = din("cmat", [128, 4, 128])
    esel = din("esel", [16, 16, 128])
    invf = din("invf", [64, 1])
    amask = din("amask", [128, 4, 512], BF16)

    outT = nc.dram_tensor("outT", [1024, NTOK], F32, kind="ExternalOutput").ap()
    H2 = NTOK // 2

    win0_b = dscr("win0_b", [D, EVEN_IN])
    wout0_b = dscr("wout0_b", [EVEN_OUT, 1024])
    win1_b = dscr("win1_b", [D, ODD_IN])
    wout1_b = dscr("wout1_b", [ODD_OUT, 1024])
    wuq_b = dscr("wuq_b", [768, 1536])
    wukv_b = dscr("wukv_b", [512, 2048])
    mk_b = [dscr("mk0_b", [D, 256]), dscr("mk1_b", [D, 256])]
    mv_b = [dscr("mv0_b", [D, 256]), dscr("mv1_b", [D, 256])]
    hT_d = dscr("hT_d", [D, NTOK])
    yall_d = dscr("yall_d", [EVEN_OUT, NTOK])
    yall1_d = dscr("yall1_d", [ODD_OUT, NTOK])
    yown_d = dscr("yown_d", [EVEN_OWN, NTOK])
    yown1_d = dscr("yown1_d", [ODD_OWN, NTOK])
    x1T_d = dscr("x1T_d", [2, D, H2], F32)
    x1own_d = dscr("x1own_d", [2, 1024, H2], F32)
    x2own_d = dscr("x2own_d", [1024, NTOK], F32)
    ssown_d = dscr("ssown_d", [1, NTOK], F32)
    ssall_d = dscr("ssall_d", [2, NTOK], F32)

    def x1g_ap(r0, r1, t0, tw):
        return x1T_d[t0 // H2, r0:r1, (t0 % H2):(t0 % H2) + tw]

    def x1o_ap(r0, r1, t0, tw):
        return x1own_d[t0 // H2, r0:r1, (t0 % H2):(t0 % H2) + tw]
    cqn_d = dscr("cqn_d", [768, NTOK])
    lat_d = dscr("lat_d", [576, NTOK])
    zc_d = dscr("zc_d", [1024, NTOK])
    cs_d = dscr("cs_d", [2, 64, NTOK], F32)

    dbg = {}
    dbgx = {}
    if "sc_dbg" in [d[0] for d in debug]:
        debug = [d for d in debug if d[0] != "sc_dbg"]
        dbgx["sc_dbg"] = True
        for nm, shp in (("work0", [128, 514]), ("work1", [128, 514]), ("acc", [128, 512]), ("acc2", [128, 512]), ("zs", [128, 512]), ("csb", [128, 512]), ("carry", [128, 8, 2]), ("cw", [128, 8, 3])):
            dbgx[nm] = nc.dram_tensor("dbgx_" + nm, shp, F32, kind="ExternalOutput").ap()
    for name, shape, dt in debug:
        dbg[name] = nc.dram_tensor("dbg_" + name, list(shape), dt, kind="ExternalOutput").ap()

    with ExitStack() as es:
        S = Sched(nc, es)
        ps = [es.enter_context(nc.psum_tensor("ps%d" % i, [128, 512], F32)) for i in range(8)]
        cm_f = S.sb(es, "cm_f", [128, 4, 128], F32)
        cm_b = S.sb(es, "cm_b", [128, 4, 128], BF16)
        v16 = S.sb(es, "v16", [128, 5, 16], F32)
        S.dma("sp", lambda e: [e.dma_start(out=cm_f[:], in_=cmat[:, :, :])], w=["cm_f"])
        S.dma("sp", lambda e: [e.dma_start(out=v16[:], in_=vec16[:, :, :])], w=["v16"])
        S.op("dve", lambda e: e.tensor_copy(out=cm_b[:], in_=cm_f[:]), r=["cm_f"], w=["cm_b"])
        ident_f, ones_f, tri_f = cm_f[:, 0, :], cm_f[:, 1, :], cm_f[:, 2, :]
        ident_b, ones_b, tri_b = cm_b[:, 0, :], cm_b[:, 1, :], cm_b[:, 2, :]
        CONST = ["cm_f", "cm_b", "v16"]

        CW = 2568

        def make_cast(pes):
            fbuf = [S.sb(pes, "cf%d" % i, [128, CW], F32) for i in range(3)]
            bbuf = [S.sb(pes, "cb%d" % i, [128, CW], BF16) for i in range(3)]
            it = [0]

            def cast_item(dst, src, r0, c0, cw):
                i = it[0] % 3
                it[0] += 1
                fb, bb = fbuf[i], bbuf[i]
                S.dma("sp", lambda e: [e.dma_start(out=fb[:, :cw], in_=src[r0:r0 + 128, c0:c0 + cw])], w=[("cf", i)])
                eng = ("dve", "act", "pool")[it[0] % 3]
                if eng == "act":
                    S.op("act", lambda e: e.activation(out=bb[:, :cw], in_=fb[:, :cw], func=AF.Copy), r=[("cf", i)], w=[("cb", i)])
                else:
                    S.op(eng, lambda e: e.tensor_copy(out=bb[:, :cw], in_=fb[:, :cw]), r=[("cf", i)], w=[("cb", i)])
                S.dma("pool", lambda e: [e.dma_start(out=dst[r0:r0 + 128, c0:c0 + cw], in_=bb[:, :cw])], r=[("cb", i)], w=[("wdram", id(dst), r0, c0)], slot=("cbst", i))
            return cast_item

        def cast_items(dst, src, rows, cols):
            return [(dst, src, r0, c0, min(CW, cols - c0)) for r0 in range(0, rows, 128) for c0 in range(0, cols, CW)]

        items_a = cast_items(win0_b, w_in0, D, EVEN_IN)
        for l in range(2):
            items_a += cast_items(mk_b[l], mk[l], D, 256) + cast_items(mv_b[l], mv[l], D, 256)
        items_b = (cast_items(wout0_b, w_out0, EVEN_OUT, 1024) + cast_items(win1_b, w_in1, D, ODD_IN) + cast_items(wout1_b, w_out1, ODD_OUT, 1024)
                   + cast_items(wuq_b, w_uq, 768, 1536) + cast_items(wukv_b, w_ukv, 512, 2048))

        def cast_phase():
            with ExitStack() as pes:
                ci = make_cast(pes)
                for item in items_a:
                    ci(*item)
                S.flush()

        cast_phase()

        def rstd_from_ss(ss_ps, n, out_ap, tmp_ap, keys_r, key_tmp, key_out):
            S.op("act", lambda e: e.activation(out=tmp_ap, in_=ss_ps, func=AF.Sqrt, scale=1.0 / n, bias=EPS), r=keys_r, w=[key_tmp])
            S.op("dve", lambda e: e.reciprocal(out=out_ap, in_=tmp_ap), r=[key_tmp], w=[key_out])

        def wload(wt, wkey, wb, kts, c0, width):
            S.dma("sp", lambda e: [e.dma_start(out=wt[:, kt, :width], in_=wb[kt * 128:(kt + 1) * 128, c0:c0 + width]) for kt in range(kts)],
                  w=[wkey], n=kts)

        def ld_kt(tile_, dr, r0, t0, nk, wkey, tw=512, eng="sp", r=()):
            src_ = dr if callable(dr) else (lambda a, b_, t_, w_: dr[a:b_, t_:t_ + w_])
            S.dma(eng, lambda e: [e.dma_start(out=tile_[:, kt, :], in_=src_(r0 + kt * 128, r0 + (kt + 1) * 128, t0, tw)) for kt in range(nk)], r=list(r), w=[wkey], n=nk)

        def st_kt(tile_, dr, r0, t0, nk, rkey, wkey, slot, tw=512, eng="pool"):
            S.dma(eng, lambda e: [e.dma_start(out=dr[r0 + kt * 128:r0 + (kt + 1) * 128, t0:t0 + tw], in_=tile_[:, kt, :]) for kt in range(nk)], r=[rkey], w=[(wkey, r0, t0)], slot=slot, n=nk)

        kmT = [S.sb(es, "kmT%d" % l, [128, 2, NMEM], BF16) for l in range(2)]
        vm = [S.sb(es, "vm%d" % l, [128, 2, 256], BF16) for l in range(2)]

        def mem_phase():
            with ExitStack() as pes:
                mt = S.sb(pes, "mt", [128, KT, NMEM], F32)
                sq = S.sb(pes, "msq", [128, NMEM], F32)
                rt = S.sb(pes, "mrt", [128, NMEM], F32)
                rs = S.sb(pes, "mrs", [128, NMEM], F32)
                mn = S.sb(pes, "mn", [128, KT, NMEM], BF16)
                wt = [S.sb(pes, "mw%d" % i, [128, KT, 256], BF16) for i in range(2)]
                ld_kt(mt, memT, 0, 0, KT, "mt", tw=NMEM)
                for kt in range(KT):
                    S.op("act", lambda e, kt=kt: e.activation(out=sq[:], in_=mt[:, kt, :], func=AF.Square), r=["mt"], w=["msq"])
                    S.op("pe", lambda e, kt=kt: e.matmul(ps[0][:, :NMEM], lhsT=ones_f, rhs=sq[:], start=(kt == 0), stop=(kt == KT - 1)), r=["msq", "cm_f"], w=["ps0"])
                rstd_from_ss(ps[0][:, :NMEM], D, rs[:], rt[:], ["ps0"], "mrt", "mrs")
                for kt in range(KT):
                    S.op("dve", lambda e, kt=kt: e.scalar_tensor_tensor(out=mn[:, kt, :], in0=mt[:, kt, :], scalar=v16[:, 0, kt:kt + 1], in1=rs[:], op0=ALU.mult, op1=ALU.mult),
                         r=["mt", "mrs", "v16"], w=["mn"])
                for l in range(2):
                    wload(wt[0], "mw0", mk_b[l], KT, 0, 256)
                    wload(wt[1], "mw1", mv_b[l], KT, 0, 256)
                    for h in range(2):
                        def f(e, h=h):
                            for kt in range(KT):
                                ins = e.matmul(ps[1][:, :NMEM], lhsT=wt[0][:, kt, h * 128:(h + 1) * 128], rhs=mn[:, kt, :], start=(kt == 0), stop=(kt == KT - 1))
                            return ins
                        S.op("pe", f, r=["mw0", "mn"], w=["ps1"])
                        S.op("dve", lambda e, h=h, l=l: e.tensor_copy(out=kmT[l][:, h, :], in_=ps[1][:, :NMEM]), r=["ps1"], w=["kmT%d" % l])
                    for mb in range(2):
                        def f(e, mb=mb):
                            for kt in range(KT):
                                ins = e.matmul(ps[2][:, 0:256], lhsT=mn[:, kt, mb * 128:(mb + 1) * 128], rhs=wt[1][:, kt, :], start=(kt == 0), stop=(kt == KT - 1))
                            return ins
                        S.op("pe", f, r=["mw1", "mn"], w=["ps2"])
                        S.op("dve", lambda e, mb=mb, l=l: e.tensor_copy(out=vm[l][:, mb, :], in_=ps[2][:, 0:256]), r=["ps2"], w=["vm%d" % l])
                S.flush()

        mem_phase()

        def norm_phase(src, which):
            with ExitStack() as pes:
                xt = [S.sb(pes, "nx%d" % i, [128, KT, 512], F32) for i in range(2)]
                sq = [S.sb(pes, "nsq%d" % i, [128, 512], BF16) for i in range(2)]
                rt = S.sb(pes, "nrt", [128, 512], F32)
                rs = S.sb(pes, "nrs", [128, 512], F32)
                ht = [S.sb(pes, "nh%d" % i, [128, KT, 512], BF16) for i in range(2)]
                for g in range(NG):
                    b = g % 2
                    t0 = g * 512
                    ld_kt(xt[b], src, 0, t0, KT, ("nx", b))
                    for kt in range(KT):
                        sb_ = kt % 2
                        S.op("act", lambda e, b=b, kt=kt, sb_=sb_: e.activation(out=sq[sb_][:], in_=xt[b][:, kt, :], func=AF.Square), r=[("nx", b)], w=[("nsq", sb_)])
                        S.op("pe", lambda e, kt=kt, sb_=sb_: e.matmul(ps[0][:, :], lhsT=ones_b, rhs=sq[sb_][:], start=(kt == 0), stop=(kt == KT - 1)), r=[("nsq", sb_), "cm_b"], w=["ps0"])
                    rstd_from_ss(ps[0][:, :], D, rs[:], rt[:], ["ps0"], "nrt", "nrs")
                    for kt in range(KT):
                        S.op("dve", lambda e, b=b, kt=kt: e.scalar_tensor_tensor(out=ht[b][:, kt, :], in0=xt[b][:, kt, :], scalar=v16[:, which, kt:kt + 1], in1=rs[:], op0=ALU.mult, op1=ALU.mult),
                             r=[("nx", b), "nrs", "v16"], w=[("nh", b)])
                    st_kt(ht[b], hT_d, 0, t0, KT, ("nh", b), "hT_d", ("nhst", b))
                S.flush()

        norm_phase(xT, 1)


        TWO_PI = 6.283185307179586
        PI = 3.141592653589793
        bank = [0]

        def nb():
            bank[0] = (bank[0] + 1) % 4
            return bank[0]

        def mm_acc(psap, pskey, lhs_fn, rhs_fn, kts, rkeys):
            def f(e):
                for kt in range(kts):
                    ins = e.matmul(psap, lhsT=lhs_fn(kt), rhs=rhs_fn(kt), start=(kt == 0), stop=(kt == kts - 1))
                return ins
            S.op("pe", f, r=rkeys, w=[pskey])

        def load_ht(ht, b, t0):
            ld_kt(ht[b], hT_d, 0, t0, KT, ("ht", b))

        def mem_attn_group(l, ht, b, wts, wb, cq, cz, ym, ymkey, tmp):
            qs, pT, zs, rl, osb = tmp
            for half in range(1):
                wload(wts[half], ("mw", half), wb, KT, cq + half * 256, 256)
                wload(wts[2 + half], ("mw", 2 + half), wb, KT, cz + half * 256, 256)
            for h in range(2):
                wq = wts[h // 2]
                wz = wts[2 + h // 2]
                co = (h % 2) * 128
                bq = nb()
                mm_acc(ps[bq][:, :], "ps%d" % bq, lambda kt, wq=wq, co=co: wq[:, kt, co:co + 128], lambda kt: ht[b][:, kt, :], KT, [("mw", h // 2), ("ht", b)])
                S.op("act", lambda e, bq=bq: e.activation(out=qs[:], in_=ps[bq][:, :], func=AF.Copy, scale=128.0 ** -0.5), r=["ps%d" % bq], w=["mqs"])
                for mb in range(2):
                    bs = nb()
                    S.op("pe", lambda e, bs=bs, mb=mb, h=h: e.matmul(ps[bs][:, :], lhsT=kmT[l][:, h, mb * 128:(mb + 1) * 128], rhs=qs[:], start=True, stop=True),
                         r=["mqs", "kmT%d" % l], w=["ps%d" % bs])
                    S.op("act", lambda e, bs=bs, mb=mb: e.activation(out=pT[mb][:], in_=ps[bs][:, :], func=AF.Exp), r=["ps%d" % bs], w=[("mpT", mb)])
                for mb in range(2):
                    S.op("pe", lambda e, mb=mb, h=h: e.matmul(ps[4][:, :], lhsT=vm[l][:, mb, h * 128:(h + 1) * 128], rhs=pT[mb][:], start=(mb == 0), stop=(mb == 1)),
                         r=[("mpT", mb), "vm%d" % l], w=["ps4"])
                for mb in range(2):
                    S.op("pe", lambda e, mb=mb: e.matmul(ps[5][:, :], lhsT=ones_b, rhs=pT[mb][:], start=(mb == 0), stop=(mb == 1)),
                         r=[("mpT", mb), "cm_b"], w=["ps5"])
                bz = nb()
                mm_acc(ps[bz][:, :], "ps%d" % bz, lambda kt, wz=wz, co=co: wz[:, kt, co:co + 128], lambda kt: ht[b][:, kt, :], KT, [("mw", 2 + h // 2), ("ht", b)])
                S.op("act", lambda e, bz=bz: e.activation(out=zs[:], in_=ps[bz][:, :], func=AF.Silu), r=["ps%d" % bz], w=["mzs"])
                S.op("dve", lambda e: e.reciprocal(out=rl[:], in_=ps[5][:, :]), r=["ps5"], w=["mrl"])
                S.op("dve", lambda e: e.tensor_tensor(out=osb[:], in0=ps[4][:, :], in1=rl[:], op=ALU.mult), r=["ps4", "mrl"], w=["mosb"])
                S.op("pool", lambda e, h=h: e.tensor_tensor(out=ym[:, h, :], in0=osb[:], in1=zs[:], op=ALU.mult), r=["mosb", "mzs"], w=[ymkey])

        def mem_tmp(pes):
            return (S.sb(pes, "mqs", [128, 512], BF16), [S.sb(pes, "mpT%d" % i, [128, 512], BF16) for i in range(2)],
                    S.sb(pes, "mzs", [128, 512], F32), S.sb(pes, "mrl", [128, 512], F32), S.sb(pes, "mosb", [128, 512], F32))

        def sc_mem_phase():
            with ExitStack() as pes:
                ht = [S.sb(pes, "ht%d" % i, [128, KT, 512], BF16) for i in range(2)]
                wts = [S.sb(pes, "wt%d" % i, [128, KT, 256], BF16) for i in range(4)]
                csb = S.sb(pes, "csb", [128, 512], F32)
                work = [S.sb(pes, "work%d" % i, [128, 514], F32) for i in range(2)]
                acc = S.sb(pes, "acc", [128, 512], F32)
                acc2 = S.sb(pes, "acc2", [128, 512], F32)
                zs = S.sb(pes, "zs", [128, 512], F32)
                carry = S.sb(pes, "carry", [128, 4, 2], F32)
                cw = S.sb(pes, "cw", [128, 4, 3], F32)
                ya = [S.sb(pes, "ya%d" % i, [128, 4, 512], BF16) for i in range(2)]
                ym = [S.sb(pes, "ym%d" % i, [128, 2, 512], BF16) for i in range(2)]
                mtmp = mem_tmp(pes)
                ci = make_cast(pes)
                per_g = (len(items_b) + NG - 1) // NG
                S.dma("sp", lambda e: [e.dma_start(out=cw[:], in_=sccw[:, :, :])], w=["cw"])
                S.op("pool", lambda e: e.memset(carry[:], 0.0), w=["carry"])
                for g in range(NG):
                    b = g % 2
                    t0 = g * 512
                    load_ht(ht, b, t0)
                    for j in range(4):
                        jj, co = j // 2, (j % 2) * 128
                        if j % 2 == 0:
                            for fi, c_off in enumerate((C_SCC, C_SCV, C_SCB, C_SCZ)):
                                wload(wts[fi], ("mw", fi), win0_b, KT, c_off + jj * 256, 256)
                        wk = work[j % 2]
                        wkk = ("work", j % 2)
                        bc, bv = nb(), nb()
                        mm_acc(ps[bc][:, :], "ps%d" % bc, lambda kt, co=co: wts[0][:, kt, co:co + 128], lambda kt, b=b: ht[b][:, kt, :], KT, [("mw", 0), ("ht", b)])
                        mm_acc(ps[bv][:, :], "ps%d" % bv, lambda kt, co=co: wts[1][:, kt, co:co + 128], lambda kt, b=b: ht[b][:, kt, :], KT, [("mw", 1), ("ht", b)])
                        S.op("act", lambda e, bc=bc: e.activation(out=csb[:], in_=ps[bc][:, :], func=AF.Copy), r=["ps%d" % bc], w=["csb"])
                        S.op("pool", lambda e, wk=wk, j=j: e.tensor_copy(out=wk[:, 0:2], in_=carry[:, j, :]), r=["carry"], w=[wkk])
                        S.op("dve", lambda e, wk=wk, bv=bv: e.tensor_tensor(out=wk[:, 2:514], in0=csb[:], in1=ps[bv][:, :], op=ALU.mult), r=["csb", "ps%d" % bv, wkk], w=[wkk])
                        S.op("pool", lambda e, wk=wk, j=j: e.tensor_copy(out=carry[:, j, :], in_=wk[:, 512:514]), r=[wkk], w=["carry"])
                        S.op("dve", lambda e, wk=wk, j=j: e.tensor_scalar(out=acc[:], in0=wk[:, 2:514], scalar1=cw[:, j, 2:3], scalar2=None, op0=ALU.mult), r=[wkk, "cw"], w=["acc"])
                        S.op("dve", lambda e, wk=wk, j=j: e.scalar_tensor_tensor(out=acc[:], in0=wk[:, 1:513], scalar=cw[:, j, 1:2], in1=acc[:], op0=ALU.mult, op1=ALU.add), r=[wkk, "cw", "acc"], w=["acc"])
                        S.op("dve", lambda e, wk=wk, j=j: e.scalar_tensor_tensor(out=acc[:], in0=wk[:, 0:512], scalar=cw[:, j, 0:1], in1=acc[:], op0=ALU.mult, op1=ALU.add), r=[wkk, "cw", "acc"], w=["acc"])
                        bb, bz = nb(), nb()
                        mm_acc(ps[bb][:, :], "ps%d" % bb, lambda kt, co=co: wts[2][:, kt, co:co + 128], lambda kt, b=b: ht[b][:, kt, :], KT, [("mw", 2), ("ht", b)])
                        mm_acc(ps[bz][:, :], "ps%d" % bz, lambda kt, co=co: wts[3][:, kt, co:co + 128], lambda kt, b=b: ht[b][:, kt, :], KT, [("mw", 3), ("ht", b)])
                        S.op("act", lambda e, bz=bz: e.activation(out=zs[:], in_=ps[bz][:, :], func=AF.Silu), r=["ps%d" % bz], w=["zs"])
                        S.op("dve", lambda e, bb=bb: e.tensor_tensor(out=acc2[:], in0=acc[:], in1=ps[bb][:, :], op=ALU.mult), r=["acc", "ps%d" % bb], w=["acc2"])
                        S.op("pool", lambda e, j=j, b=b: e.tensor_tensor(out=ya[b][:, j, :], in0=acc2[:], in1=zs[:], op=ALU.mult), r=["acc2", "zs"], w=[("ya", b)])
                    st_kt(ya[b], yown_d, 0, t0, 4, ("ya", b), "yown_d", ("yast", b))
                    for item in items_b[g * per_g:(g + 1) * per_g]:
                        ci(*item)
                    mem_attn_group(0, ht, b, wts, win0_b, C_MQ0, C_MZ0, ym[b], ("ym", b), mtmp)
                    st_kt(ym[b], yown_d, 1536, t0, 2, ("ym", b), "yown_d", ("ymst", b))
                S.flush()

        sc_mem_phase()

        def ssd_phase():
            with ExitStack() as pes:
                ht = [S.sb(pes, "sht", [128, KT, 512], BF16)]
                wts = [S.sb(pes, "swt%d" % i, [128, KT, 256], BF16) for i in range(3)]
                zs_t = S.sb(pes, "zs_t", [128, 8, 512], BF16)
                xs_t = S.sb(pes, "xs_t", [128, 8, 512], BF16)
                bm_t = S.sb(pes, "bm_t", [128, 2, 512], BF16)
                cm_t = S.sb(pes, "cm_t", [128, 2, 512], BF16)
                yb_t = S.sb(pes, "yb_t", [128, 8, 512], BF16)
                dtT = S.sb(pes, "dtT", [16, 512], F32)
                dte_ = S.sb(pes, "dtexp", [16, 512], F32)
                dtaT = S.sb(pes, "dtaT", [16, 512], F32)
                acT = S.sb(pes, "acT", [16, 512], F32)
                ones32 = S.sb(pes, "ones32", [16, 128], F32)
                hv = S.sb(pes, "hv", [16, 2], F32)
                aneg = S.sb(pes, "aneg", [16, 1], F32)
                es_sel = S.sb(pes, "es_sel", [16, 16, 128], F32)
                xcar = S.sb(pes, "xcar", [128, 12, 3], F32)
                xcw = S.sb(pes, "xcw", [128, 12, 4], F32)
                xcb = S.sb(pes, "xcb", [128, 12], F32)
                dc = S.sb(pes, "dc", [128, 8], F32)
                wk2 = [S.sb(pes, "wk2_%d" % i, [128, 515], F32) for i in range(2)]
                acc3 = S.sb(pes, "acc3", [128, 512], F32)
                tok32 = S.sb(pes, "tok32", [128, 32], F32)
                xs_tok = S.sb(pes, "xs_tok", [128, 16, 64], BF16)
                bm_tok = S.sb(pes, "bm_tok", [128, 256], BF16)
                seg = [S.sb(pes, "seg%d" % i_, [128, 8, 128], F32) for i_ in range(2)]
                es_ = [S.sb(pes, "es%d" % i_, [128, 8, 128], F32) for i_ in range(2)]
                ea_ = [S.sb(pes, "ea%d" % i_, [128, 8, 128], F32) for i_ in range(2)]
                cbm = [S.sb(pes, "cbm%d" % i_, [128, 128], F32) for i_ in range(2)]
                MT = [S.sb(pes, "MT%d" % i_, [128, 8, 128], BF16) for i_ in range(2)]
                ChT = [S.sb(pes, "ChT%d" % i_, [128, 8, 128], BF16) for i_ in range(2)]
                t8 = [S.sb(pes, "t8%d" % i_, [128, 8], F32) for i_ in range(2)]
                dte8 = [S.sb(pes, "dte8%d" % i_, [128, 8], F32) for i_ in range(2)]
                cd8 = [S.sb(pes, "cd8%d" % i_, [128, 8], F32) for i_ in range(2)]
                xdt_pad = [S.sb(pes, "xdt_pad%d" % i_, [128, 8, 128], BF16) for i_ in range(2)]
                xdt_f = [S.sb(pes, "xdt_f%d" % i_, [128, 8, 64], F32) for i_ in range(2)]
                xdt_end = [S.sb(pes, "xdt_end%d" % i_, [128, 8, 64], BF16) for i_ in range(2)]
                S_f = S.sb(pes, "S_f", [128, 16, 64], F32)
                S_pad = [S.sb(pes, "S_pad%d" % i_, [128, 8, 128], BF16) for i_ in range(2)]
                yv = [S.sb(pes, "yv%d" % i_, [128, 4, 128], F32) for i_ in range(2)]
                yz = [S.sb(pes, "yz%d" % i_, [128, 4, 128], F32) for i_ in range(2)]
                sq4 = [S.sb(pes, "sq4%d" % i_, [128, 4, 128], BF16) for i_ in range(2)]
                rt4 = [S.sb(pes, "rt4%d" % i_, [128, 128], F32) for i_ in range(2)]
                rs4 = [S.sb(pes, "rs4%d" % i_, [128, 128], F32) for i_ in range(2)]
                for t_, src in ((hv, hvec), (es_sel, esel), (xcw, ssdcw), (xcb, ssdcb), (dc, dcol)):
                    S.dma("sp", lambda e, t_=t_, src=src: [e.dma_start(out=t_[:], in_=src)], w=[("c", id(t_))])
                S.op("pool", lambda e: e.memset(xcar[:], 0.0), w=["xcar"])
                S.op("pool", lambda e: e.memset(S_f[:], 0.0), w=[("S_f", 0), ("S_f", 1)])
                for i_ in range(2):
                    S.op("pool", lambda e, i_=i_: e.memset(xdt_pad[i_][:], 0.0), w=[("xdt_pad", i_)])
                    S.op("pool", lambda e, i_=i_: e.memset(S_pad[i_][:], 0.0), w=[("S_pad", i_)])
                S.op("pool", lambda e: e.memset(ones32[:], 1.0), w=["ones32"])
                S.op("act", lambda e: e.activation(out=aneg[:], in_=hv[:, 1:2], func=AF.Exp), r=[("c", id(hv))], w=["aneg0"])
                S.op("dve", lambda e: e.tensor_scalar(out=aneg[:], in0=aneg[:], scalar1=-1.0, scalar2=None, op0=ALU.mult), r=["aneg0"], w=["aneg"])
                S.flush()
                psT = ps[6][:].bitcast(BF16)
                for g in range(NG):
                    t0 = g * 512
                    ld_kt(ht[0], hT_d, 0, t0, KT, ("ht", 0))
                    wi = 0
                    for t in range(8):
                        if t % 2 == 0:
                            wi = (wi + 1) % 3
                            wload(wts[wi], ("sw", wi), win0_b, KT, C_SSDZ + t * 128, 256)
                        co = (t % 2) * 128
                        bz = nb()
                        mm_acc(ps[bz][:, :], "ps%d" % bz, lambda kt, wi=wi, co=co: wts[wi][:, kt, co:co + 128], lambda kt: ht[0][:, kt, :], KT, [("sw", wi), ("ht", 0)])
                        S.op("act", lambda e, bz=bz, t=t: e.activation(out=zs_t[:, t, :], in_=ps[bz][:, :], func=AF.Silu), r=["ps%d" % bz], w=["zs_t"])
                    for t in range(12):
                        if t % 2 == 0:
                            wi = (wi + 1) % 3
                            wload(wts[wi], ("sw", wi), win0_b, KT, C_XBC + t * 128, 256)
                        co = (t % 2) * 128
                        bx = nb()
                        wk = wk2[t % 2]
                        wkk = ("wk2", t % 2)
                        mm_acc(ps[bx][:, :], "ps%d" % bx, lambda kt, wi=wi, co=co: wts[wi][:, kt, co:co + 128], lambda kt: ht[0][:, kt, :], KT, [("sw", wi), ("ht", 0)])
                        S.op("pool", lambda e, wk=wk, t=t: e.tensor_copy(out=wk[:, 0:3], in_=xcar[:, t, :]), r=["xcar"], w=[wkk])
                        S.op("act", lambda e, wk=wk, bx=bx: e.activation(out=wk[:, 3:515], in_=ps[bx][:, :], func=AF.Copy), r=["ps%d" % bx, wkk], w=[wkk])
                        S.op("pool", lambda e, wk=wk, t=t: e.tensor_copy(out=xcar[:, t, :], in_=wk[:, 512:515]), r=[wkk], w=["xcar"])
                        S.op("dve", lambda e, wk=wk, t=t: e.tensor_scalar(out=acc3[:], in0=wk[:, 3:515], scalar1=xcw[:, t, 3:4], scalar2=None, op0=ALU.mult), r=[wkk], w=["acc3"])
                        for k_ in range(3):
                            S.op("dve", lambda e, wk=wk, t=t, k_=k_: e.scalar_tensor_tensor(out=acc3[:], in0=wk[:, k_:k_ + 512], scalar=xcw[:, t, k_:k_ + 1], in1=acc3[:], op0=ALU.mult, op1=ALU.add), r=[wkk, "acc3"], w=["acc3"])
                        if t < 8:
                            dst_ = xs_t[:, t, :]
                            dk = "xs_t"
                        elif t < 10:
                            dst_ = bm_t[:, t - 8, :]
                            dk = "bm_t"
                        else:
                            dst_ = cm_t[:, t - 10, :]
                            dk = "cm_t"
                        S.op("act", lambda e, dst_=dst_, t=t: e.activation(out=dst_, in_=acc3[:], func=AF.Silu, bias=xcb[:, t:t + 1]), r=["acc3"], w=[dk])
                    wi = (wi + 1) % 3
                    wload(wts[wi], ("sw", wi), win0_b, KT, C_DT, 16)
                    bd = nb()
                    mm_acc(ps[bd][0:16, :], "ps%d" % bd, lambda kt, wi=wi: wts[wi][:, kt, 0:16], lambda kt: ht[0][:, kt, :], KT, [("sw", wi), ("ht", 0)])
                    S.op("act", lambda e, bd=bd: e.activation(out=dte_[:], in_=ps[bd][0:16, :], func=AF.Exp, bias=hv[:, 0:1]), r=["ps%d" % bd], w=["dtexp"])
                    S.op("act", lambda e: e.activation(out=dtT[:], in_=dte_[:], func=AF.Ln, bias=1.0), r=["dtexp"], w=["dtT"])
                    S.op("dve", lambda e: e.tensor_scalar(out=dtaT[:], in0=dtT[:], scalar1=aneg[:, 0:1], scalar2=None, op0=ALU.mult), r=["dtT"], w=["dtaT"])
                    for c in range(4):
                        q0 = c * 128
                        S.op("dve", lambda e, q0=q0: e.tensor_tensor_scan(out=acT[:, q0:q0 + 128], data0=ones32[:], data1=dtaT[:, q0:q0 + 128], initial=0.0, op0=ALU.mult, op1=ALU.add), r=["dtaT"], w=["acT"])
                        S.op("pe", lambda e, q0=q0: e.transpose(out=ps[7][:, 0:16], in_=dtT[:, q0:q0 + 128], identity=ident_f[0:16, 0:16]), r=["dtT"], w=["ps7"])
                        S.op("pe", lambda e, q0=q0: e.transpose(out=ps[7][:, 16:32], in_=acT[:, q0:q0 + 128], identity=ident_f[0:16, 0:16]), r=["acT", "ps7"], w=["ps7"])
                        S.op("dve", lambda e: e.tensor_copy(out=tok32[:], in_=ps[7][:, 0:32]), r=["ps7"], w=["tok32"])
                        for q4 in range(2):
                            def f(e, q4=q4, q0=q0):
                                for i in range(4):
                                    ins = e.transpose(out=psT[:, i * 128:(i + 1) * 128], in_=xs_t[:, q4 * 4 + i, q0:q0 + 128], identity=ident_b)
                                return ins
                            S.op("pe", f, r=["xs_t"], w=["ps6"])
                            S.op("act", lambda e, q4=q4: e.activation(out=xs_tok[:, q4 * 8:(q4 + 1) * 8, :].rearrange("p h d -> p (h d)"), in_=psT[:, 0:512], func=AF.Copy), r=["ps6"], w=["xs_tok"])
                        def f(e, q0=q0):
                            for i in range(2):
                                ins = e.transpose(out=psT[:, i * 128:(i + 1) * 128], in_=bm_t[:, i, q0:q0 + 128], identity=ident_b)
                            return ins
                        S.op("pe", f, r=["bm_t"], w=["ps6"])
                        S.op("act", lambda e: e.activation(out=bm_tok[:], in_=psT[:, 0:256], func=AF.Copy), r=["ps6"], w=["bm_tok"])
                        def st1(gh, q0):
                            ba = 2 * gh
                            def f(e):
                                for hh in range(8):
                                    ins = e.matmul(ps[ba + hh // 4][:, (hh % 4) * 128:(hh % 4 + 1) * 128], lhsT=es_sel[:, gh * 8 + hh, :], rhs=acT[:, q0:q0 + 128], start=True, stop=True)
                                return ins
                            S.op("pe", f, r=["acT"], w=["ps%d" % ba, "ps%d" % (ba + 1)])

                        def st2(gh, q0):
                            ba = 2 * gh
                            for hh in range(8):
                                S.op("dve", lambda e, hh=hh: e.tensor_scalar(out=seg[gh][:, hh, :], in0=ps[ba + hh // 4][:, (hh % 4) * 128:(hh % 4 + 1) * 128], scalar1=tok32[:, 16 + gh * 8 + hh:17 + gh * 8 + hh], scalar2=0.0, op0=ALU.subtract, op1=ALU.min),
                                     r=["ps%d" % ba, "ps%d" % (ba + 1), "tok32"], w=[("seg", gh)])

                        def st3(gh, q0):
                            ba = 2 * gh
                            S.op("act", lambda e: e.activation(out=es_[gh][:], in_=seg[gh][:], func=AF.Exp), r=[("seg", gh)], w=[("es", gh)])
                            for hb in range(2):
                                S.op("act", lambda e, hb=hb: e.activation(out=ea_[gh][:, hb * 4:(hb + 1) * 4, :].rearrange("p h q -> p (h q)"), in_=ps[ba + hb][:, :], func=AF.Exp), r=["ps%d" % ba, "ps%d" % (ba + 1)], w=[("ea", gh)])
                            for hb in range(2):
                                S.op("dve", lambda e, hb=hb: e.tensor_tensor(out=t8[gh][:, hb * 4:(hb + 1) * 4], in0=ps[ba + hb][:, :].rearrange("p (h q) -> p h q", q=128)[:, :, 127], in1=tok32[:, 16 + gh * 8 + hb * 4:16 + gh * 8 + hb * 4 + 4], op=ALU.subtract),
                                     r=["ps%d" % ba, "ps%d" % (ba + 1), "tok32"], w=[("t8", gh)])
                            S.op("act", lambda e: e.activation(out=dte8[gh][:], in_=t8[gh][:], func=AF.Exp), r=[("t8", gh)], w=[("dte8", gh)])
                            S.op("act", lambda e: e.activation(out=cd8[gh][:], in_=ea_[gh][:, :, 127], func=AF.Copy), r=[("ea", gh)], w=[("cd8", gh)])

                        def st4(gh, q0):
                            cbo = ps[4][:, gh * 128:(gh + 1) * 128]
                            S.op("pe", lambda e: e.matmul(cbo, lhsT=bm_t[:, gh, q0:q0 + 128], rhs=cm_t[:, gh, q0:q0 + 128], start=True, stop=True), r=["bm_t", "cm_t"], w=["ps4"])
                            S.op("dve", lambda e: e.tensor_tensor(out=cbm[gh][:], in0=cbo, in1=tri_f, op=ALU.mult), r=["ps4"], w=[("cbm", gh)])
                            S.op("dve", lambda e: e.tensor_tensor(out=MT[gh][:], in0=es_[gh][:], in1=cbm[gh][:].unsqueeze(1).broadcast_to([128, 8, 128]), op=ALU.mult), r=[("es", gh), ("cbm", gh)], w=[("MT", gh)])
                            S.op("pool", lambda e: e.tensor_tensor(out=ChT[gh][:], in0=ea_[gh][:], in1=cm_t[:, gh, q0:q0 + 128].unsqueeze(1).broadcast_to([128, 8, 128]), op=ALU.mult), r=[("ea", gh), "cm_t"], w=[("ChT", gh)])

                        def st5(gh, q0):
                            S.op("dve", lambda e: e.tensor_tensor(out=xdt_f[gh][:], in0=xs_tok[:, gh * 8:(gh + 1) * 8, :], in1=tok32[:, gh * 8:(gh + 1) * 8].unsqueeze(2).broadcast_to([128, 8, 64]), op=ALU.mult), r=["xs_tok", "tok32"], w=[("xdt_f", gh)])
                            for ev in range(2):
                                S.op("pool", lambda e, ev=ev: e.tensor_copy(out=xdt_pad[gh][:, ev::2, ev * 64:(ev + 1) * 64], in_=xdt_f[gh][:, ev::2, :]), r=[("xdt_f", gh)], w=[("xdt_pad", gh)])
                                S.op("act", lambda e, ev=ev: e.activation(out=S_pad[gh][:, ev::2, ev * 64:(ev + 1) * 64], in_=S_f[:, gh * 8 + ev:(gh + 1) * 8:2, :], func=AF.Copy), r=[("S_f", gh)], w=[("S_pad", gh)])
                            S.op("dve", lambda e: e.tensor_tensor(out=xdt_end[gh][:], in0=xdt_f[gh][:], in1=dte8[gh][:].unsqueeze(2).broadcast_to([128, 8, 64]), op=ALU.mult), r=[("xdt_f", gh), ("dte8", gh)], w=[("xdt_end", gh)])

                        def st6(gh, q0):
                            by = 5 + gh
                            def f(e):
                                for t in range(4):
                                    o_ = ps[by][:, t * 128:(t + 1) * 128]
                                    e.matmul(o_, lhsT=xdt_pad[gh][:, 2 * t, :], rhs=MT[gh][:, 2 * t, :], start=True, stop=False)
                                    e.matmul(o_, lhsT=xdt_pad[gh][:, 2 * t + 1, :], rhs=MT[gh][:, 2 * t + 1, :], start=False, stop=False)
                                    e.matmul(o_, lhsT=S_pad[gh][:, 2 * t, :], rhs=ChT[gh][:, 2 * t, :], start=False, stop=False)
                                    ins = e.matmul(o_, lhsT=S_pad[gh][:, 2 * t + 1, :], rhs=ChT[gh][:, 2 * t + 1, :], start=False, stop=True)
                                return ins
                            S.op("pe", f, r=[("xdt_pad", gh), ("MT", gh), ("S_pad", gh), ("ChT", gh)], w=["ps%d" % by])
                            S.op("pe", lambda e: e.matmul(ps[7][:, :], lhsT=bm_tok[:, gh * 128:(gh + 1) * 128], rhs=xdt_end[gh][:].rearrange("p h d -> p (h d)"), start=True, stop=True), r=["bm_tok", ("xdt_end", gh)], w=["ps7"])
                            S.op("dve", lambda e: e.tensor_tensor(out=S_f[:, gh * 8:(gh + 1) * 8, :], in0=S_f[:, gh * 8:(gh + 1) * 8, :], in1=cd8[gh][:].unsqueeze(2).broadcast_to([128, 8, 64]), op=ALU.mult), r=[("S_f", gh), ("cd8", gh), ("S_pad", gh)], w=[("S_f", gh)])
                            S.op("dve", lambda e: e.tensor_tensor(out=S_f[:, gh * 8:(gh + 1) * 8, :].rearrange("p h d -> p (h d)"), in0=ps[7][:, :], in1=S_f[:, gh * 8:(gh + 1) * 8, :].rearrange("p h d -> p (h d)"), op=ALU.add), r=[("S_f", gh), "ps7"], w=[("S_f", gh)])

                        def st7(gh, q0):
                            by = 5 + gh
                            for t in range(4):
                                ct = gh * 4 + t
                                S.op("dve", lambda e, t=t, ct=ct: e.scalar_tensor_tensor(out=yv[gh][:, t, :], in0=xs_t[:, ct, q0:q0 + 128], scalar=dc[:, ct:ct + 1], in1=ps[by][:, t * 128:(t + 1) * 128], op0=ALU.mult, op1=ALU.add), r=["xs_t", "ps%d" % by], w=[("yv", gh)])
                            S.op("dve", lambda e: e.tensor_tensor(out=yz[gh][:], in0=yv[gh][:], in1=zs_t[:, gh * 4:(gh + 1) * 4, q0:q0 + 128], op=ALU.mult), r=[("yv", gh), "zs_t"], w=[("yz", gh)])
                            S.op("act", lambda e: e.activation(out=sq4[gh][:], in_=yz[gh][:], func=AF.Square), r=[("yz", gh)], w=[("sq4", gh)])
                            no_ = ps[4][:, 256 + gh * 128:256 + (gh + 1) * 128]
                            def f(e):
                                for t in range(4):
                                    ins = e.matmul(no_, lhsT=ones_b, rhs=sq4[gh][:, t, :], start=(t == 0), stop=(t == 3))
                                return ins
                            S.op("pe", f, r=[("sq4", gh)], w=["ps4"])
                            rstd_from_ss(no_, 512, rs4[gh][:], rt4[gh][:], ["ps4"], ("rt4", gh), ("rs4", gh))
                            for t in range(4):
                                ct = gh * 4 + t
                                S.op("dve", lambda e, t=t, ct=ct: e.scalar_tensor_tensor(out=yb_t[:, ct, q0:q0 + 128], in0=yz[gh][:, t, :], scalar=v16[:, 4, ct:ct + 1], in1=rs4[gh][:], op0=ALU.mult, op1=ALU.mult), r=[("yz", gh), ("rs4", gh)], w=["yb_t"])

                        for st in (st1, st2, st3, st4, st5, st6, st7):
                            for gh in range(2):
                                st(gh, q0)
                    st_kt(yb_t, yown_d, 512, t0, 8, "yb_t", "yown_d", "ybst")
                S.flush()

        if stop_after >= 1.5:
            ssd_phase()

        def out_phase(wb, kto, resid, dst, final, ysrc):
            with ExitStack() as pes:
                yt = [S.sb(pes, "oy%d" % i, [128, kto, 512], BF16) for i in range(2)]
                wts = [S.sb(pes, "ow%d" % i, [128, kto, 256], BF16) for i in range(2)]
                xr = [S.sb(pes, "oxr%d" % i, [128, 512], F32) for i in range(3)]
                ot = [S.sb(pes, "oo%d" % i, [128, 512], F32) for i in range(3)]
                sq = [S.sb(pes, "fsq%d" % i, [128, 512], BF16) for i in range(2)]
                ssr = S.sb(pes, "ssr", [1, 512], F32)
                cnt = 0
                for g in range(NG):
                    b = g % 2
                    t0 = g * 512
                    ld_kt(yt[b], ysrc, 0, t0, kto, ("oy", b), r=["yall_d"])
                    for oc in range(8):
                        if oc % 2 == 0:
                            wi = (oc // 2) % 2
                            wload(wts[wi], ("ow", wi), wb, kto, oc * 128, 256)
                        wt_ = wts[(oc // 2) % 2]
                        co = (oc % 2) * 128
                        i3 = cnt % 3
                        cnt += 1
                        S.dma("sp", lambda e, i3=i3, oc=oc, t0=t0: [e.dma_start(out=xr[i3][:], in_=resid(oc * 128, (oc + 1) * 128, t0, 512))], r=["resid"], w=[("oxr", i3)])
                        bo = nb()
                        mm_acc(ps[bo][:, :], "ps%d" % bo, lambda kt, wt_=wt_, co=co: wt_[:, kt, co:co + 128], lambda kt, b=b: yt[b][:, kt, :], kto, [("ow", (oc // 2) % 2), ("oy", b)])
                        S.op("dve", lambda e, i3=i3, bo=bo: e.tensor_tensor(out=ot[i3][:], in0=ps[bo][:, :], in1=xr[i3][:], op=ALU.add), r=["ps%d" % bo, ("oxr", i3)], w=[("oo", i3)])
                        S.dma(os.environ.get("STQ", "pool"), lambda e, i3=i3, oc=oc, t0=t0: [e.dma_start(out=dst(oc * 128, (oc + 1) * 128, t0, 512), in_=ot[i3][:])], r=[("oo", i3)], w=[("resid_out", oc, t0)], slot=("oost", i3))
                        if final:
                            sb_ = oc % 2
                            S.op("act", lambda e, i3=i3, sb_=sb_: e.activation(out=sq[sb_][:], in_=ot[i3][:], func=AF.Square), r=[("oo", i3)], w=[("fsq", sb_)])
                            S.op("pe", lambda e, oc=oc, sb_=sb_: e.matmul(ps[6][:, :], lhsT=ones_b, rhs=sq[sb_][:], start=(oc == 0), stop=(oc == 7)), r=[("fsq", sb_), "cm_b"], w=["ps6"])
                    if final:
                        S.op("act", lambda e: e.activation(out=ssr[:], in_=ps[6][0:1, :], func=AF.Copy), r=["ps6"], w=["ssr"])
                        S.dma("pool", lambda e, t0=t0: [e.dma_start(out=ssown_d[0:1, t0:t0 + 512], in_=ssr[:])], r=["ssr"], w=[("ssown_d", t0)], slot="ssst")
                S.flush()

        def final_norm_phase():
            with ExitStack() as pes:
                ss2 = [S.sb(pes, "ss2_%d" % i, [2, 512], F32) for i in range(2)]
                rt = S.sb(pes, "frt", [128, 512], F32)
                rs = [S.sb(pes, "frs%d" % i, [128, 512], F32) for i in range(2)]
                xr = [S.sb(pes, "fx%d" % i, [128, 8, 512], F32) for i in range(2)]
                ot = [S.sb(pes, "fo%d" % i, [128, 8, 512], F32) for i in range(2)]
                for g in range(NG):
                    b = g % 2
                    t0 = g * 512
                    S.dma("sp", lambda e, b=b, t0=t0: [e.dma_start(out=ss2[b][:], in_=ssall_d[0:2, t0:t0 + 512])], w=[("ss2", b)])
                    ld_kt(xr[b], x2own_d, 0, t0, 8, ("fx", b))
                    S.op("pe", lambda e, b=b: e.matmul(ps[6][:, :], lhsT=ones_f[0:2, :], rhs=ss2[b][:], start=True, stop=True), r=[("ss2", b)], w=["ps6"])
                    rstd_from_ss(ps[6][:, :], D, rs[b][:], rt[:], ["ps6"], "frt", ("frs", b))
                    for oc in range(8):
                        S.op("dve", lambda e, b=b, oc=oc: e.scalar_tensor_tensor(out=ot[b][:, oc, :], in0=xr[b][:, oc, :], scalar=v16[:, 3, oc:oc + 1], in1=rs[b][:], op0=ALU.mult, op1=ALU.mult), r=[("fx", b), ("frs", b), "v16"], w=[("fo", b)])
                    st_kt(ot[b], outT, 0, t0, 8, ("fo", b), "final_out", ("fost", b))
                S.flush()

        RG = [[0, 1], [2, 3], [4, 5], [6, 7]]
        if stop_after >= 2:
            S.collective([(lambda e, j=j: e.collective_compute("AllGather", ALU.bypass, replica_groups=RG, ins=[yown_d[j * 128:(j + 1) * 128, :].opt()], outs=[yall_d[j * 256:(j + 1) * 256, :].opt()])) for j in range(14)])
            out_phase(wout0_b, 28, lambda a, b_, t_, w_: xTo[a:b_, t_:t_ + w_], x1o_ap, False, yall_d)
            S.collective([(lambda e, th=th, j=j: e.collective_compute("AllGather", ALU.bypass, replica_groups=RG, ins=[x1own_d[th, j * 128:(j + 1) * 128, :].opt()], outs=[x1T_d[th, j * 256:(j + 1) * 256, :].opt()])) for th in range(2) for j in range(8)])

        def rope_apply(pes_tiles, t_ps, cos2, sin2, out_bf, okey, scale, rkeys, cskeys=("cs",)):
            t_sb, o1, o2 = pes_tiles
            S.op("act", lambda e: e.activation(out=t_sb[:], in_=t_ps, func=AF.Copy), r=rkeys, w=["t_sb"])
            S.op("pe", lambda e: e.matmul(ps[7][0:64, :], lhsT=cm_f[0:64, 3, 0:64], rhs=t_sb[:], start=True, stop=True), r=["t_sb"], w=["ps7"])
            S.op("dve", lambda e: e.tensor_tensor(out=o1[:], in0=t_sb[:], in1=cos2, op=ALU.mult), r=["t_sb"] + list(cskeys), w=["o1"])
            S.op("dve", lambda e: e.tensor_tensor(out=o2[:], in0=ps[7][0:64, :], in1=sin2, op=ALU.mult), r=["ps7"] + list(cskeys), w=["o2"])
            S.op("dve", lambda e: e.scalar_tensor_tensor(out=out_bf, in0=o1[:], scalar=scale, in1=o2[:], op0=ALU.mult, op1=ALU.add), r=["o1", "o2"], w=[okey])

        def l1_in_phase():
            with ExitStack() as pes:
                ht2 = [S.sb(pes, "ht%d" % i, [128, KT, 512], BF16) for i in range(2)]
                wts = [S.sb(pes, "wt%d" % i, [128, KT, 256], BF16) for i in range(4)]
                cq = S.sb(pes, "cq", [128, 6, 512], F32)
                sq = [S.sb(pes, "lsq%d" % i, [128, 512], BF16) for i in range(2)]
                rt = S.sb(pes, "lrt", [128, 512], F32)
                rs = S.sb(pes, "lrs", [128, 512], F32)
                cqn_t = S.sb(pes, "cqn_t", [128, 6, 512], BF16)
                zc_t = S.sb(pes, "zc_t", [128, 8, 512], BF16)
                ym = S.sb(pes, "ym", [128, 2, 512], BF16)
                nwq = S.sb(pes, "nwq", [128, 6], F32)
                nwk = S.sb(pes, "nwk", [128, 4], F32)
                ivf = S.sb(pes, "ivf", [64, 1], F32)
                posi = S.sb(pes, "posi", [1, 512], I32)
                posf = S.sb(pes, "posf", [1, 512], F32)
                a1 = S.sb(pes, "a1", [64, 512], F32)
                a2 = S.sb(pes, "a2", [64, 512], F32)
                a3 = S.sb(pes, "a3", [64, 512], F32)
                ki = S.sb(pes, "ki", [64, 512], I32)
                cs = S.sb(pes, "cs", [64, 2, 512], F32)
                rtl = (S.sb(pes, "t_sb", [64, 512], F32), S.sb(pes, "o1", [64, 512], F32), S.sb(pes, "o2", [64, 512], F32))
                krb = S.sb(pes, "krb", [64, 512], BF16)
                mtmp = mem_tmp(pes)
                xt1 = S.sb(pes, "l1x", [128, KT, 512], F32)
                nsq = [S.sb(pes, "l1nsq%d" % i, [128, 512], BF16) for i in range(2)]
                nrt = S.sb(pes, "l1nrt", [128, 512], F32)
                nrs = S.sb(pes, "l1nrs", [128, 512], F32)
                for t_, src in ((nwq, qnw), (nwk, kvnw), (ivf, invf)):
                    S.dma("sp", lambda e, t_=t_, src=src: [e.dma_start(out=t_[:], in_=src)], w=[("c", id(t_))])
                S.flush()
                for g in range(NG):
                    t0 = g * 512
                    hb_ = g % 2
                    ht = [ht2[hb_]]
                    hk = ("ht", hb_)
                    ld_kt(xt1, x1g_ap, 0, t0, KT, "l1x")
                    for kt in range(KT):
                        sb_ = kt % 2
                        S.op("act", lambda e, kt=kt, sb_=sb_: e.activation(out=nsq[sb_][:], in_=xt1[:, kt, :], func=AF.Square), r=["l1x"], w=[("l1nsq", sb_)])
                        S.op("pe", lambda e, kt=kt, sb_=sb_: e.matmul(ps[6][:, :], lhsT=ones_b, rhs=nsq[sb_][:], start=(kt == 0), stop=(kt == KT - 1)), r=[("l1nsq", sb_), "cm_b"], w=["ps6"])
                    rstd_from_ss(ps[6][:, :], D, nrs[:], nrt[:], ["ps6"], "l1nrt", "l1nrs")
                    for kt in range(KT):
                        S.op("dve", lambda e, kt=kt, htg=ht[0]: e.scalar_tensor_tensor(out=htg[:, kt, :], in0=xt1[:, kt, :], scalar=v16[:, 2, kt:kt + 1], in1=nrs[:], op0=ALU.mult, op1=ALU.mult),
                             r=["l1x", "l1nrs", "v16"], w=[hk])
                    for (c0, nt, nw, dst, r0) in ((C_CQ, 6, nwq, cqn_d, 0), (C_CKV, 4, nwk, lat_d, 0)):
                        for t in range(nt):
                            if t % 2 == 0:
                                wload(wts[(t // 2) % 4], ("mw", (t // 2) % 4), win1_b, KT, c0 + t * 128, 256)
                            wt_ = wts[(t // 2) % 4]
                            co = (t % 2) * 128
                            bq = nb()
                            mm_acc(ps[bq][:, :], "ps%d" % bq, lambda kt, wt_=wt_, co=co: wt_[:, kt, co:co + 128], lambda kt, htg=ht[0]: htg[:, kt, :], KT, [("mw", (t // 2) % 4), hk])
                            S.op("act", lambda e, bq=bq, t=t: e.activation(out=cq[:, t, :], in_=ps[bq][:, :], func=AF.Copy), r=["ps%d" % bq], w=["cq"])
                            sb_ = t % 2
                            S.op("act", lambda e, t=t, sb_=sb_: e.activation(out=sq[sb_][:], in_=cq[:, t, :], func=AF.Square), r=["cq"], w=[("lsq", sb_)])
                            S.op("pe", lambda e, t=t, sb_=sb_, nt=nt: e.matmul(ps[6][:, :], lhsT=ones_b, rhs=sq[sb_][:], start=(t == 0), stop=(t == nt - 1)), r=[("lsq", sb_)], w=["ps6"])
                        rstd_from_ss(ps[6][:, :], nt * 128, rs[:], rt[:], ["ps6"], "lrt", "lrs")
                        for t in range(nt):
                            S.op("dve", lambda e, t=t, nw=nw: e.scalar_tensor_tensor(out=cqn_t[:, t, :], in0=cq[:, t, :], scalar=nw[:, t:t + 1], in1=rs[:], op0=ALU.mult, op1=ALU.mult), r=["cq", "lrs"], w=["cqn_t"])
                        st_kt(cqn_t, dst, 0, t0, nt, "cqn_t", ("d", id(dst)), "cqnst")
                    S.dma("sp", lambda e, t0=t0: [e.dma_start(out=posi[:], in_=pos[0:1, t0:t0 + 512])], w=["posi"])
                    S.op("dve", lambda e: e.tensor_copy(out=posf[:], in_=posi[:]), r=["posi"], w=["posf"])
                    S.op("pe", lambda e: e.matmul(ps[5][0:64, :], lhsT=ones_f[0:1, 0:64], rhs=posf[:], start=True, stop=True), r=["posf"], w=["ps5"])
                    C1 = 6.28125
                    C2 = TWO_PI - C1
                    PIS = 3.1415925
                    S.op("dve", lambda e: e.tensor_scalar(out=a1[:], in0=ps[5][0:64, :], scalar1=ivf[:, 0:1], scalar2=None, op0=ALU.mult), r=["ps5"], w=["a1"])
                    S.op("dve", lambda e: e.tensor_scalar(out=a2[:], in0=a1[:], scalar1=1.0 / TWO_PI, scalar2=None, op0=ALU.mult), r=["a1"], w=["a2"])
                    S.op("dve", lambda e: e.tensor_copy(out=ki[:], in_=a2[:]), r=["a2"], w=["ki"])
                    S.op("dve", lambda e: e.tensor_copy(out=a2[:], in_=ki[:]), r=["ki"], w=["a2"])
                    S.op("dve", lambda e: e.scalar_tensor_tensor(out=a1[:], in0=a2[:], scalar=-C1, in1=a1[:], op0=ALU.mult, op1=ALU.add), r=["a2", "a1"], w=["a1"])
                    S.op("dve", lambda e: e.scalar_tensor_tensor(out=a1[:], in0=a2[:], scalar=-C2, in1=a1[:], op0=ALU.mult, op1=ALU.add), r=["a2", "a1"], w=["a1"])

                    def fold(t_, key):
                        S.op("dve", lambda e: e.tensor_scalar(out=a2[:], in0=t_[:], scalar1=PI, scalar2=-TWO_PI, op0=ALU.is_gt, op1=ALU.mult), r=[key], w=["a2"])
                        S.op("dve", lambda e: e.tensor_tensor(out=t_[:], in0=t_[:], in1=a2[:], op=ALU.add), r=[key, "a2"], w=[key])
                        S.op("dve", lambda e: e.tensor_scalar(out=a2[:], in0=t_[:], scalar1=-PI, scalar2=TWO_PI, op0=ALU.is_lt, op1=ALU.mult), r=[key], w=["a2"])
                        S.op("dve", lambda e: e.tensor_tensor(out=t_[:], in0=t_[:], in1=a2[:], op=ALU.add), r=[key, "a2"], w=[key])
                        S.op("dve", lambda e: e.tensor_scalar(out=t_[:], in0=t_[:], scalar1=PIS, scalar2=-PIS, op0=ALU.min, op1=ALU.max), r=[key], w=[key])
                    fold(a1, "a1")
                    S.op("act", lambda e: e.activation(out=cs[:, 1, :], in_=a1[:], func=AF.Sin), r=["a1"], w=["cs"])
                    S.op("dve", lambda e: e.tensor_scalar(out=a3[:], in0=a1[:], scalar1=PI / 2, scalar2=None, op0=ALU.add), r=["a1"], w=["a3"])
                    fold(a3, "a3")
                    S.op("act", lambda e: e.activation(out=cs[:, 0, :], in_=a3[:], func=AF.Sin), r=["a3"], w=["cs"])
                    S.dma("pool", lambda e, t0=t0: [e.dma_start(out=cs_d[c_, :, t0:t0 + 512], in_=cs[:, c_, :]) for c_ in range(2)], r=["cs"], w=[("cs_d", t0)], slot="csst", n=2)
                    wload(wts[0], ("mw", 0), win1_b, KT, C_KR, 64)
                    bk = nb()
                    mm_acc(ps[bk][0:64, :], "ps%d" % bk, lambda kt: wts[0][:, kt, 0:64], lambda kt, htg=ht[0]: htg[:, kt, :], KT, [("mw", 0), hk])
                    rope_apply(rtl, ps[bk][0:64, :], cs[:, 0, :], cs[:, 1, :], krb[:], "krb", 1.0, ["ps%d" % bk])
                    S.dma("pool", lambda e, t0=t0: [e.dma_start(out=lat_d[512:576, t0:t0 + 512], in_=krb[:])], r=["krb"], w=[("lat_kr", t0)], slot="krst")
                    for t in range(8):
                        if t % 2 == 0:
                            wload(wts[(t // 2) % 4], ("mw", (t // 2) % 4), win1_b, KT, C_ZC + t * 128, 256)
                        wt_ = wts[(t // 2) % 4]
                        co = (t % 2) * 128
                        bz = nb()
                        mm_acc(ps[bz][:, :], "ps%d" % bz, lambda kt, wt_=wt_, co=co: wt_[:, kt, co:co + 128], lambda kt, htg=ht[0]: htg[:, kt, :], KT, [("mw", (t // 2) % 4), hk])
                        S.op("act", lambda e, bz=bz, t=t: e.activation(out=zc_t[:, t, :], in_=ps[bz][:, :], func=AF.Silu), r=["ps%d" % bz], w=["zc_t"])
                    st_kt(zc_t, zc_d, 0, t0, 8, "zc_t", "zc_d", "zcst")
                    mem_attn_group(1, ht2, hb_, wts, win1_b, C_MQ1, C_MZ1, ym, "ym1", mtmp)
                    st_kt(ym, yown1_d, 1024, t0, 2, "ym1", "yown1_d", "ym1st")
                S.flush()

        if stop_after >= 4:
            l1_in_phase()

        def mla_phase():
            with ExitStack() as pes:
                ckvn = S.sb(pes, "ckvn", [128, 4, NTOK], BF16)
                kr = S.sb(pes, "kr", [128, NTOK], BF16)
                KhT = S.sb(pes, "KhT", [128, NTOK], BF16)
                Vh = S.sb(pes, "Vh", [128, NCH, 128], BF16)
                mk_b_ = S.sb(pes, "mk_b", [128, 4, 512], BF16)
                wkv = S.sb(pes, "wkv", [128, 4, 256], BF16)
                wq = S.sb(pes, "wq", [128, 6, 192], BF16)
                cqg = [S.sb(pes, "cqg%d" % i, [128, 6, 512], BF16) for i in range(2)]
                csg = [S.sb(pes, "csg%d" % i, [64, 2, 512], F32) for i in range(2)]
                zcg = [S.sb(pes, "zcg%d" % i, [128, 512], BF16) for i in range(2)]
                Qn = S.sb(pes, "Qn", [128, 512], BF16)
                Qr = S.sb(pes, "Qr", [128, 512], BF16)
                lacc = S.sb(pes, "lacc", [128, 512], F32)
                pT = [S.sb(pes, "pT%d" % i, [128, 512], BF16) for i in range(6)]
                rl = S.sb(pes, "arl", [128, 512], F32)
                osb = S.sb(pes, "aosb", [128, 512], F32)
                yc = [S.sb(pes, "yc%d" % i, [128, 512], BF16) for i in range(2)]
                rtl = (S.sb(pes, "t_sb", [64, 512], F32), S.sb(pes, "o1", [64, 512], F32), S.sb(pes, "o2", [64, 512], F32))
                S.dma("sp", lambda e: [e.dma_start(out=mk_b_[:], in_=amask[:, :, :])], w=["mk_b"])
                for rt_ in range(4):
                    S.dma("sp", lambda e, rt_=rt_: [e.dma_start(out=ckvn[:, rt_, :], in_=lat_d[rt_ * 128:(rt_ + 1) * 128, :])], w=[("ckvn", rt_)])
                S.op("pool", lambda e: e.memset(kr[64:128, :], 0.0), w=["kr_pad"])
                S.op("pool", lambda e: e.memset(Qr[64:128, :], 0.0), w=["Qr_pad"])
                S.dma("sp", lambda e: [e.dma_start(out=kr[0:64, :], in_=lat_d[512:576, :])], w=["kr"])
                S.flush()
                sc_ = 192.0 ** -0.5
                it = 0
                for h in range(8):
                    S.dma("sp", lambda e, h=h: [e.dma_start(out=wkv[:, kt, :], in_=wukv_b[kt * 128:(kt + 1) * 128, h * 256:(h + 1) * 256]) for kt in range(4)], w=["wkv"], n=4)
                    S.dma("sp", lambda e, h=h: [e.dma_start(out=wq[:, kt, :], in_=wuq_b[kt * 128:(kt + 1) * 128, h * 192:(h + 1) * 192]) for kt in range(6)], w=["wq"], n=6)
                    for kg in range(NG):
                        bk = nb()
                        mm_acc(ps[bk][:, :], "ps%d" % bk, lambda kt: wkv[:, kt, 0:128], lambda kt, kg=kg: ckvn[:, kt, kg * 512:(kg + 1) * 512], 4, ["wkv"])
                        if kg % 2 == 0:
                            S.op("act", lambda e, bk=bk, kg=kg: e.activation(out=KhT[:, kg * 512:(kg + 1) * 512], in_=ps[bk][:, :], func=AF.Copy), r=["ps%d" % bk], w=["KhT"])
                        else:
                            S.op("dve", lambda e, bk=bk, kg=kg: e.tensor_copy(out=KhT[:, kg * 512:(kg + 1) * 512], in_=ps[bk][:, :]), r=["ps%d" % bk], w=["KhT"])
                    for kb4 in range(NCH // 4):
                        bv = nb()
                        def f(e, kb4=kb4, bv=bv):
                            for i in range(4):
                                kb = kb4 * 4 + i
                                for kt in range(4):
                                    ins = e.matmul(ps[bv][:, i * 128:(i + 1) * 128], lhsT=ckvn[:, kt, kb * 128:(kb + 1) * 128], rhs=wkv[:, kt, 128:256], start=(kt == 0), stop=(kt == 3))
                            return ins
                        S.op("pe", f, r=["wkv"], w=["ps%d" % bv])
                        if kb4 % 2 == 0:
                            S.op("dve", lambda e, bv=bv, kb4=kb4: e.tensor_copy(out=Vh[:, kb4 * 4:(kb4 + 1) * 4, :].rearrange("p k d -> p (k d)"), in_=ps[bv][:, :]), r=["ps%d" % bv], w=["Vh"])
                        else:
                            S.op("act", lambda e, bv=bv, kb4=kb4: e.activation(out=Vh[:, kb4 * 4:(kb4 + 1) * 4, :].rearrange("p k d -> p (k d)"), in_=ps[bv][:, :], func=AF.Copy), r=["ps%d" % bv], w=["Vh"])
                    S.flush()
                    for qg in range(NG):
                        b = qg % 2
                        t0 = qg * 512
                        ld_kt(cqg[b], cqn_d, 0, t0, 6, ("cqg", b))
                        S.dma("sp", lambda e, b=b, t0=t0: [e.dma_start(out=csg[b][:, c_, :], in_=cs_d[c_, :, t0:t0 + 512]) for c_ in range(2)], w=[("csg", b)], n=2)
                        S.dma("sp", lambda e, b=b, t0=t0, h=h: [e.dma_start(out=zcg[b][:], in_=zc_d[h * 128:(h + 1) * 128, t0:t0 + 512])], w=[("zcg", b)])
                        bq = nb()
                        mm_acc(ps[bq][:, :], "ps%d" % bq, lambda kt: wq[:, kt, 0:128], lambda kt, b=b: cqg[b][:, kt, :], 6, ["wq", ("cqg", b)])
                        S.op("act", lambda e, bq=bq: e.activation(out=Qn[:], in_=ps[bq][:, :], func=AF.Copy, scale=sc_), r=["ps%d" % bq], w=["Qn"])
                        br = nb()
                        mm_acc(ps[br][0:64, :], "ps%d" % br, lambda kt: wq[:, kt, 128:192], lambda kt, b=b: cqg[b][:, kt, :], 6, ["wq", ("cqg", b)])
                        rope_apply(rtl, ps[br][0:64, :], csg[b][:, 0, :], csg[b][:, 1, :], Qr[0:64, :], "Qr", 1.0, ["ps%d" % br], cskeys=[("csg", b)])
                        S.op("act", lambda e: e.activation(out=Qr[0:64, :], in_=Qr[0:64, :], func=AF.Copy, scale=sc_), r=["Qr"], w=["Qr"])
                        po, pl = 4 + qg % 2, 6 + qg % 2
                        nkb = 4 * qg + 4
                        LA = 2
                        for step in range(nkb + LA):
                            if step < nkb:
                                kb = step
                                bs = kb % 4
                                pi = kb % 6

                                c0 = max(0, kb - 4 * qg) * 128

                                def f(e, bs=bs, kb=kb, c0=c0):
                                    e.matmul(ps[bs][:, c0:], lhsT=KhT[:, kb * 128:(kb + 1) * 128], rhs=Qn[:, c0:], start=True, stop=False)
                                    return e.matmul(ps[bs][:, c0:], lhsT=kr[:, kb * 128:(kb + 1) * 128], rhs=Qr[:, c0:], start=False, stop=True)
                                S.op("pe", f, r=["KhT", "Qn", "Qr"], w=["ps%d" % bs])
                                S.op("act", lambda e, bs=bs, pi=pi, c0=c0: e.activation(out=pT[pi][:, c0:], in_=ps[bs][:, c0:], func=AF.Exp), r=["ps%d" % bs], w=[("pT", pi)])
                                if kb >= 4 * qg:
                                    S.op("pool", lambda e, pi=pi, d_=kb - 4 * qg, c0=c0: e.tensor_tensor(out=pT[pi][:, c0:], in0=pT[pi][:, c0:], in1=mk_b_[:, d_, c0:], op=ALU.mult), r=[("pT", pi), "mk_b"], w=[("pT", pi)])
                                if kb == 0:
                                    S.op("dve", lambda e, pi=pi: e.tensor_copy(out=lacc[:], in_=pT[pi][:]), r=[("pT", pi)], w=["lacc"])
                                else:
                                    S.op("dve", lambda e, pi=pi, c0=c0: e.tensor_tensor(out=lacc[:, c0:], in0=lacc[:, c0:], in1=pT[pi][:, c0:], op=ALU.add), r=[("pT", pi), "lacc"], w=["lacc"])
                            if step >= LA:
                                kb = step - LA
                                pi = kb % 6
                                c0 = max(0, kb - 4 * qg) * 128
                                S.op("pe", lambda e, pi=pi, kb=kb, po=po, nkb=nkb, c0=c0: e.matmul(ps[po][:, c0:], lhsT=Vh[:, kb, :], rhs=pT[pi][:, c0:], start=(kb == 0), stop=(kb == nkb - 1)), r=[("pT", pi), "Vh"], w=["ps%d" % po])
                        S.op("pe", lambda e, pl=pl: e.matmul(ps[pl][:, :], lhsT=ones_f, rhs=lacc[:], start=True, stop=True), r=["lacc"], w=["ps%d" % pl])
                        S.op("dve", lambda e, pl=pl: e.reciprocal(out=rl[:], in_=ps[pl][:, :]), r=["ps%d" % pl], w=["arl"])
                        S.op("dve", lambda e, po=po: e.tensor_tensor(out=osb[:], in0=ps[po][:, :], in1=rl[:], op=ALU.mult), r=["ps%d" % po, "arl"], w=["aosb"])
                        S.op("pool", lambda e, b=b: e.tensor_tensor(out=yc[b][:], in0=osb[:], in1=zcg[b][:], op=ALU.mult), r=["aosb", ("zcg", b)], w=[("yc", b)])
                        S.dma("pool", lambda e, b=b, h=h, t0=t0: [e.dma_start(out=yown1_d[h * 128:(h + 1) * 128, t0:t0 + 512], in_=yc[b][:])], r=[("yc", b)], w=[("yown1_d", h, t0)], slot=("ycst", b))
                S.flush()

        if stop_after >= 5:
            mla_phase()
        if stop_after >= 6:
            S.collective([(lambda e, j=j: e.collective_compute("AllGather", ALU.bypass, replica_groups=RG, ins=[yown1_d[j * 128:(j + 1) * 128, :].opt()], outs=[yall1_d[j * 256:(j + 1) * 256, :].opt()])) for j in range(10)])
            out_phase(wout1_b, 20, x1o_ap, lambda a, b_, t_, w_: x2own_d[a:b_, t_:t_ + w_], True, yall1_d)
            S.collective([lambda e: e.collective_compute("AllGather", ALU.bypass, replica_groups=RG, ins=[ssown_d.opt()], outs=[ssall_d.opt()])])
            final_norm_phase()
        scr = {"hT": hT_d, "yall": yall_d, "yown": yown_d, "yown1": yown1_d, "yall1": yall1_d, "cqn": cqn_d, "lat": lat_d, "zc": zc_d}
        for name in dbg:
            rows = scr[name].shape[0]
            for r0 in range(0, rows, 128):
                r1 = min(rows, r0 + 128)
                S.dma("sp", lambda e, name=name, r0=r0, r1=r1: [e.dma_start(out=dbg[name][r0:r1, :], in_=scr[name][r0:r1, :])], w=[("dbgo", name, r0)], slot="dbgslot")
        S.flush()
    return nc


def _host_inputs(inputs, NTOK):
    f = lambda a: np.ascontiguousarray(np.asarray(a))
    g = lambda k: np.asarray(inputs[k], np.float32)
    colT = lambda v: f(np.asarray(v, np.float32).reshape(-1, 128).T)
    ar = np.arange
    cmat = np.zeros((128, 4, 128), np.float32)
    cmat[:, 0, :] = np.eye(128)
    cmat[:, 1, :] = 1.0
    cmat[:, 2, :] = np.triu(np.ones((128, 128)))
    rot = np.zeros((64, 64), np.float32)
    for i in range(32):
        rot[i + 32, i] = -1.0
        rot[i, i + 32] = 1.0
    cmat[:64, 3, :64] = rot
    amask = np.zeros((128, 4, 512), np.float32)
    for d_ in range(4):
        amask[:, d_, :] = (np.arange(128)[:, None] + d_ * 128 <= np.arange(512)[None, :])
    esel = np.zeros((16, 16, 128), np.float32)
    for h in range(16):
        esel[h, h, :] = 1.0
    inv = (10000.0 ** (-np.arange(32, dtype=np.float32) / 32)).astype(np.float32)
    invf = f(np.concatenate([inv, inv]).reshape(64, 1))
    own0 = [np.concatenate([mm * 512 + ar(512), 1024 + mm * 1024 + ar(1024), 3072 + mm * 256 + ar(256)]) for mm in range(2)]
    own1 = [np.concatenate([mm * 1024 + ar(1024), 2048 + mm * 256 + ar(256)]) for mm in range(2)]
    rows0 = np.concatenate([own0[mm][j * 128:(j + 1) * 128] for j in range(14) for mm in range(2)])
    rows1 = np.concatenate([own1[mm][j * 128:(j + 1) * 128] for j in range(10) for mm in range(2)])
    perm_feat = np.concatenate([mm * 1024 + j * 128 + ar(128) for j in range(8) for mm in range(2)])
    per_m = []
    for m in range(2):
        cols0 = np.concatenate([m * 512 + ar(512), 1024 + m * 512 + ar(512), 2048 + m * 512 + ar(512), 3072 + m * 512 + ar(512),
                                4096 + m * 1024 + ar(1024),
                                6144 + m * 1024 + ar(1024), 6144 + 2048 + m * 256 + ar(256), 6144 + 2560 + m * 256 + ar(256),
                                9216 + m * 16 + ar(16), 9248 + m * 256 + ar(256), 9760 + m * 256 + ar(256)])
        cols1 = np.concatenate([ar(1344), 1344 + m * 1024 + ar(1024), 3392 + m * 256 + ar(256), 3904 + m * 256 + ar(256)])
        xbc_ch = np.concatenate([m * 1024 + ar(1024), 2048 + m * 256 + ar(256), 2560 + m * 256 + ar(256)])
        vec16 = np.zeros((128, 5, 16), np.float32)
        vec16[:, 0, :] = colT(g("mem_norm_w"))
        vec16[:, 1, :] = colT(g("norm0_w"))
        vec16[:, 2, :] = colT(g("norm1_w")[perm_feat])
        vec16[:, 3, :8] = colT(g("final_norm_w")[m * 1024:(m + 1) * 1024])
        vec16[:, 4, :8] = colT(g("ssd_norm_w")[m * 1024:(m + 1) * 1024])
        d = dict(
            w_in0=f(g("w_in0")[:, cols0]), w_out0=f(g("w_out0")[rows0][:, m * 1024:(m + 1) * 1024]), w_in1=f(g("w_in1")[perm_feat][:, cols1]), w_out1=f(g("w_out1")[rows1][:, m * 1024:(m + 1) * 1024]),
            w_uq=f(g("mla_w_uq")[:, m * 1536:(m + 1) * 1536]), w_ukv=f(g("mla_w_ukv")[:, m * 2048:(m + 1) * 2048]),
            mem_k0=f(g("mem_k0")[:, m * 256:(m + 1) * 256]), mem_k1=f(g("mem_k1")[:, m * 256:(m + 1) * 256]),
            mem_v0=f(g("mem_v0")[:, m * 256:(m + 1) * 256]), mem_v1=f(g("mem_v1")[:, m * 256:(m + 1) * 256]),
            vec16=f(vec16),
            dcol=colT(np.repeat(g("ssd_d")[m * 16:(m + 1) * 16], 64)),
            sccw=f(g("sc_conv_w")[:, m * 512:(m + 1) * 512].reshape(3, 4, 128).transpose(2, 1, 0)),
            ssdcw=f(g("ssd_conv_w")[:, xbc_ch].reshape(4, 12, 128).transpose(2, 1, 0)),
            ssdcb=f(g("ssd_conv_b")[xbc_ch].reshape(12, 128).T),
            hvec=f(np.stack([g("ssd_dt_bias")[m * 16:(m + 1) * 16], g("ssd_a_log")[m * 16:(m + 1) * 16]], axis=1)),
            qnw=colT(g("mla_q_norm_w")), kvnw=colT(g("mla_kv_norm_w")),
            cmat=cmat, esel=esel, invf=invf, amask=amask.astype(ml_dtypes.bfloat16))
        per_m.append(d)
    maps = []
    x = np.asarray(inputs["x"])
    mem = np.asarray(inputs["mem"])
    posi = np.asarray(inputs["positions"])
    xT = [f(x[b, :NTOK].T) for b in range(4)]
    mT = [f(mem[b].T) for b in range(4)]
    for c in range(8):
        b, m = c // 2, c % 2
        mp = dict(per_m[m])
        mp["xT"] = xT[b]
        mp["xTo"] = f(xT[b][m * 1024:(m + 1) * 1024])
        mp["memT"] = mT[b]
        mp["pos"] = f(posi[b, :NTOK].reshape(1, NTOK).astype(np.int32))
        maps.append(mp)
    return maps


def kernel(**inputs):
    NTOK = 8192
    nc = build_program(NTOK)
    maps = _host_inputs(inputs, NTOK)
    res = run_bass_kernel_spmd(nc, maps, core_ids=list(range(8)))
    out = np.stack([np.concatenate([np.asarray(res.results[2 * b + m]["outT"]).T for m in range(2)], axis=1) for b in range(4)], axis=0)
    return np.ascontiguousarray(out.astype(np.float32))
```

```python
import os
import numpy as np
import ml_dtypes
from contextlib import ExitStack
import concourse.bass as bass
import concourse.mybir as mybir
from concourse.bass_utils import run_bass_kernel_spmd

F32 = mybir.dt.float32
BF16 = mybir.dt.bfloat16
I32 = mybir.dt.int32
AF = mybir.ActivationFunctionType
ALU = mybir.AluOpType

D = 2048
KT = 16
NMEM = 256
EPS = 1e-6
EVEN_IN = 5136
EVEN_OUT = 3584
EVEN_OWN = 1792
ODD_IN = 2880
ODD_OUT = 2560
ODD_OWN = 1280
C_SCB, C_SCC, C_SCV, C_SCZ = 0, 512, 1024, 1536
C_SSDZ = 2048
C_XBC = 3072
C_DT = 4608
C_MQ0 = 4624
C_MZ0 = 4880
C_CQ, C_CKV, C_KR, C_ZC, C_MQ1, C_MZ1 = 0, 768, 1280, 1344, 2368, 2624

ENGS = ("pe", "act", "dve", "pool", "sp")


class _Op:
    __slots__ = ("eng", "fn", "deps", "is_dma", "slot", "dma_val", "sig", "needed", "prev_dma", "ndma")

    def __init__(self, eng, fn, is_dma=False, slot=None, ndma=1):
        self.eng = eng
        self.fn = fn
        self.deps = []
        self.is_dma = is_dma
        self.slot = slot
        self.dma_val = 0
        self.sig = 0
        self.needed = False
        self.prev_dma = None
        self.ndma = ndma


class Sched:
    def __init__(self, nc, es):
        self.nc = nc
        self.es = es
        self.q = {e: [] for e in ENGS}
        self.last_w = {}
        self.readers = {}
        self.slot_last = {}
        self.slot_tot = {}
        self.slot_sem = {}
        self.sigcnt = {e: 0 for e in ENGS}
        self.esem = {e: es.enter_context(nc.semaphore("s_" + e)) for e in ENGS}
        self.waited = {e: {} for e in ENGS}
        self.nops = 0
        self.cc_sem = es.enter_context(nc.semaphore("s_cc"))
        self.cc_cnt = 0
        self.log = None
        self.cc_scratch = es.enter_context(nc.sbuf_tensor("cc_scratch", [128, 4], F32))

    def collective(self, fns):
        self.flush()
        sem = self.cc_sem
        base = self.cc_cnt
        self.cc_cnt += len(fns)
        with self.nc.Block() as block:
            @block.gpsimd
            def _(e):
                for i, fn in enumerate(fns):
                    fn(e).then_inc(sem, 1)
                    e.wait_ge(sem, base + i + 1)
        scr = self.cc_scratch
        self.op("pool", lambda e: e.memset(scr[:], 0.0), w=["cc_scratch"])
        self.flush()

    def sb(self, es, name, shape, dtype):
        self.nsb = getattr(self, "nsb", 0) + 1
        return es.enter_context(self.nc.sbuf_tensor("%s_%d" % (name, self.nsb), list(shape), dtype))

    def _track(self, o, r, w):
        deps = []
        for k in r:
            d = self.last_w.get(k)
            if d is not None:
                deps.append(d)
        for k in w:
            d = self.last_w.get(k)
            if d is not None:
                deps.append(d)
            deps.extend(self.readers.get(k, ()))
        for k in r:
            self.readers.setdefault(k, []).append(o)
        for k in w:
            self.last_w[k] = o
            self.readers[k] = []
        seen = set()
        for d in deps:
            if d is o or id(d) in seen:
                continue
            seen.add(id(d))
            o.deps.append(d)

    def op(self, eng, fn, r=(), w=()):
        o = _Op(eng, fn)
        self._track(o, r, w)
        self.q[eng].append(o)
        self.nops += 1
        return o

    def dma(self, eng, fn, r=(), w=(), slot=None, n=1):
        if slot is None:
            slot = w[0] if len(w) else r[0]
        o = _Op(eng, fn, is_dma=True, slot=slot, ndma=n)
        self._track(o, r, w)
        o.prev_dma = self.slot_last.get(slot)
        self.slot_last[slot] = o
        self.slot_tot[slot] = self.slot_tot.get(slot, 0) + 16 * n
        o.dma_val = self.slot_tot[slot]
        if slot not in self.slot_sem:
            self.slot_sem[slot] = self.es.enter_context(self.nc.semaphore("d%d" % len(self.slot_sem)))
        self.q[eng].append(o)
        self.nops += 1
        return o

    def flush(self):
        nc = self.nc
        lasts = [self.q[e][-1] for e in ENGS if self.q[e]]
        dmas = [o for o in self.slot_last.values()]
        for e in ENGS:
            b = _Op(e, None)
            b.deps = [d for d in lasts if d.eng != e or d.is_dma] + [d for d in dmas]
            self.q[e].append(b)
        for e in ENGS:
            for o in self.q[e]:
                for d in o.deps:
                    if d.is_dma:
                        continue
                    if d.eng == o.eng and d.eng == "pe":
                        continue
                    if d.sig == 0:
                        d.needed = True
        for e in ENGS:
            for o in self.q[e]:
                if o.needed and o.sig == 0:
                    self.sigcnt[e] += 1
                    o.sig = self.sigcnt[e]
        sched = self

        def emit(eng, e):
            waited = sched.waited[eng]

            def wait(sem, val, key):
                if waited.get(key, 0) >= val:
                    return
                waited[key] = val
                if sched.log is not None:
                    sched.log.append((eng, "W", key, val))
                e.wait_ge(sem, val)

            for o in sched.q[eng]:
                for d in o.deps:
                    if d.is_dma:
                        wait(sched.slot_sem[d.slot], d.dma_val, ("d", d.slot))
                    else:
                        if d.eng == eng and eng == "pe":
                            continue
                        wait(sched.esem[d.eng], d.sig, ("e", d.eng))
                if o.fn is None:
                    continue
                if sched.log is not None:
                    sched.log.append((eng, "DMA" if o.is_dma else "OP", o.slot, o.dma_val if o.is_dma else o.sig, o.needed))
                if o.is_dma:
                    if o.prev_dma is not None:
                        wait(sched.slot_sem[o.slot], o.prev_dma.dma_val, ("d", o.slot))
                    ins = o.fn(e)
                    assert len(ins) == o.ndma, (len(ins), o.ndma)
                    for i in ins:
                        i.then_inc(sched.slot_sem[o.slot], 16)
                else:
                    ins = o.fn(e)
                    if o.needed:
                        ins.then_inc(sched.esem[eng], 1)

        with nc.Block() as block:
            @block.tensor
            def _(e):
                emit("pe", e)

            @block.scalar
            def _(e):
                emit("act", e)

            @block.vector
            def _(e):
                emit("dve", e)

            @block.gpsimd
            def _(e):
                emit("pool", e)

            @block.sync
            def _(e):
                emit("sp", e)

        for e in ENGS:
            self.q[e] = []
        self.last_w = {}
        self.readers = {}
        self.slot_last = {}


class K:
    pass


def build_program(NTOK, debug=(), stop_after=99):
    nc = bass.Bass("TRN2", target_bir_lowering=False)
    NG = NTOK // 512
    NCH = NTOK // 128

    def din(name, shape, dt=F32):
        return nc.dram_tensor(name, list(shape), dt, kind="ExternalInput").ap()

    def dscr(name, shape, dt=BF16):
        return nc.dram_tensor(name, list(shape), dt).ap()

    xT = din("xT", [D, NTOK])
    memT = din("memT", [D, NMEM])
    pos = din("pos", [1, NTOK], I32)
    w_in0 = din("w_in0", [D, EVEN_IN])
    w_out0 = din("w_out0", [EVEN_OUT, 1024])
    xTo = din("xTo", [1024, NTOK])
    w_in1 = din("w_in1", [D, ODD_IN])
    w_out1 = din("w_out1", [ODD_OUT, 1024])
    w_uq = din("w_uq", [768, 1536])
    w_ukv = din("w_ukv", [512, 2048])
    mk = [din("mem_k0", [D, 256]), din("mem_k1", [D, 256])]
    mv = [din("mem_v0", [D, 256]), din("mem_v1", [D, 256])]
    vec16 = din("vec16", [128, 5, 16])
    dcol = din("dcol", [128, 8])
    sccw = din("sccw", [128, 4, 3])
    ssdcw = din("ssdcw", [128, 12, 4])
    ssdcb = din("ssdcb", [128, 12])
    hvec = din("hvec", [16, 2])
    qnw = din("qnw", [128, 6])
    kvnw = din("kvnw", [128, 4])
    cmat = din("cmat", [128, 4, 128])
    esel = din("esel", [16, 16, 128])
    invf = din("invf", [64, 1])
    amask = din("amask", [128, 4, 512], BF16)

    outT = nc.dram_tensor("outT", [1024, NTOK], F32, kind="ExternalOutput").ap()
    H2 = NTOK // 2

    win0_b = dscr("win0_b", [D, EVEN_IN])
    wout0_b = dscr("wout0_b", [EVEN_OUT, 1024])
    win1_b = dscr("win1_b", [D, ODD_IN])
    wout1_b = dscr("wout1_b", [ODD_OUT, 1024])
    wuq_b = dscr("wuq_b", [768, 1536])
    wukv_b = dscr("wukv_b", [512, 2048])
    mk_b = [dscr("mk0_b", [D, 256]), dscr("mk1_b", [D, 256])]
    mv_b = [dscr("mv0_b", [D, 256]), dscr("mv1_b", [D, 256])]
    hT_d = dscr("hT_d", [D, NTOK])
    yall_d = dscr("yall_d", [EVEN_OUT, NTOK])
    yall1_d = dscr("yall1_d", [ODD_OUT, NTOK])
    yown_d = dscr("yown_d", [EVEN_OWN, NTOK])
    yown1_d = dscr("yown1_d", [ODD_OWN, NTOK])
    x1T_d = dscr("x1T_d", [2, D, H2], F32)
    x1own_d = dscr("x1own_d", [2, 1024, H2], F32)
    x2own_d = dscr("x2own_d", [1024, NTOK], F32)
    ssown_d = dscr("ssown_d", [1, NTOK], F32)
    ssall_d = dscr("ssall_d", [2, NTOK], F32)

    def x1g_ap(r0, r1, t0, tw):
        return x1T_d[t0 // H2, r0:r1, (t0 % H2):(t0 % H2) + tw]

    def x1o_ap(r0, r1, t0, tw):
        return x1own_d[t0 // H2, r0:r1, (t0 % H2):(t0 % H2) + tw]
    cqn_d = dscr("cqn_d", [768, NTOK])
    lat_d = dscr("lat_d", [576, NTOK])
    zc_d = dscr("zc_d", [1024, NTOK])
    cs_d = dscr("cs_d", [2, 64, NTOK], F32)

    dbg = {}
    dbgx = {}
    if "sc_dbg" in [d[0] for d in debug]:
        debug = [d for d in debug if d[0] != "sc_dbg"]
        dbgx["sc_dbg"] = True
        for nm, shp in (("work0", [128, 514]), ("work1", [128, 514]), ("acc", [128, 512]), ("acc2", [128, 512]), ("zs", [128, 512]), ("csb", [128, 512]), ("carry", [128, 8, 2]), ("cw", [128, 8, 3])):
            dbgx[nm] = nc.dram_tensor("dbgx_" + nm, shp, F32, kind="ExternalOutput").ap()
    for name, shape, dt in debug:
        dbg[name] = nc.dram_tensor("dbg_" + name, list(shape), dt, kind="ExternalOutput").ap()

    with ExitStack() as es:
        S = Sched(nc, es)
        ps = [es.enter_context(nc.psum_tensor("ps%d" % i, [128, 512], F32)) for i in range(8)]
        cm_f = S.sb(es, "cm_f", [128, 4, 128], F32)
        cm_b = S.sb(es, "cm_b", [128, 4, 128], BF16)
        v16 = S.sb(es, "v16", [128, 5, 16], F32)
        S.dma("sp", lambda e: [e.dma_start(out=cm_f[:], in_=cmat[:, :, :])], w=["cm_f"])
        S.dma("sp", lambda e: [e.dma_start(out=v16[:], in_=vec16[:, :, :])], w=["v16"])
        S.op("dve", lambda e: e.tensor_copy(out=cm_b[:], in_=cm_f[:]), r=["cm_f"], w=["cm_b"])
        ident_f, ones_f, tri_f = cm_f[:, 0, :], cm_f[:, 1, :], cm_f[:, 2, :]
        ident_b, ones_b, tri_b = cm_b[:, 0, :], cm_b[:, 1, :], cm_b[:, 2, :]
        CONST = ["cm_f", "cm_b", "v16"]

        CW = 2568

        def make_cast(pes, ld_eng="sp"):
            fbuf = [S.sb(pes, "cf%d" % i, [128, CW], F32) for i in range(3)]
            bbuf = [S.sb(pes, "cb%d" % i, [128, CW], BF16) for i in range(3)]
            it = [0]

            def cast_item(dst, src, r0, c0, cw):
                i = it[0] % 3
                it[0] += 1
                fb, bb = fbuf[i], bbuf[i]
                S.dma(ld_eng, lambda e: [e.dma_start(out=fb[:, :cw], in_=src[r0:r0 + 128, c0:c0 + cw])], w=[("cf", i)])
                eng = ("dve", "act", "pool")[it[0] % 3]
                if eng == "act":
                    S.op("act", lambda e: e.activation(out=bb[:, :cw], in_=fb[:, :cw], func=AF.Copy), r=[("cf", i)], w=[("cb", i)])
                else:
                    S.op(eng, lambda e: e.tensor_copy(out=bb[:, :cw], in_=fb[:, :cw]), r=[("cf", i)], w=[("cb", i)])
                S.dma("pool", lambda e: [e.dma_start(out=dst[r0:r0 + 128, c0:c0 + cw], in_=bb[:, :cw])], r=[("cb", i)], w=[("wdram", id(dst), r0, c0)], slot=("cbst", i))
            return cast_item

        def cast_items(dst, src, rows, cols):
            return [(dst, src, r0, c0, min(CW, cols - c0)) for r0 in range(0, rows, 128) for c0 in range(0, cols, CW)]

        items_a = cast_items(win0_b, w_in0, D, EVEN_IN)
        for l in range(2):
            items_a += cast_items(mk_b[l], mk[l], D, 256) + cast_items(mv_b[l], mv[l], D, 256)
        items_b = (cast_items(wout0_b, w_out0, EVEN_OUT, 1024) + cast_items(win1_b, w_in1, D, ODD_IN) + cast_items(wout1_b, w_out1, ODD_OUT, 1024)
                   + cast_items(wuq_b, w_uq, 768, 1536) + cast_items(wukv_b, w_ukv, 512, 2048))

        def cast_phase():
            with ExitStack() as pes:
                ci = make_cast(pes)
                for item in items_a:
                    ci(*item)
                S.flush()

        cast_phase()

        def rstd_from_ss(ss_ps, n, out_ap, tmp_ap, keys_r, key_tmp, key_out):
            S.op("act", lambda e: e.activation(out=tmp_ap, in_=ss_ps, func=AF.Sqrt, scale=1.0 / n, bias=EPS), r=keys_r, w=[key_tmp])
            S.op("dve", lambda e: e.reciprocal(out=out_ap, in_=tmp_ap), r=[key_tmp], w=[key_out])

        def wload(wt, wkey, wb, kts, c0, width):
            S.dma("sp", lambda e: [e.dma_start(out=wt[:, kt, :width], in_=wb[kt * 128:(kt + 1) * 128, c0:c0 + width]) for kt in range(kts)],
                  w=[wkey], n=kts)

        def ld_kt(tile_, dr, r0, t0, nk, wkey, tw=512, eng="sp", r=()):
            src_ = dr if callable(dr) else (lambda a, b_, t_, w_: dr[a:b_, t_:t_ + w_])
            S.dma(eng, lambda e: [e.dma_start(out=tile_[:, kt, :], in_=src_(r0 + kt * 128, r0 + (kt + 1) * 128, t0, tw)) for kt in range(nk)], r=list(r), w=[wkey], n=nk)

        def st_kt(tile_, dr, r0, t0, nk, rkey, wkey, slot, tw=512, eng="pool"):
            S.dma(eng, lambda e: [e.dma_start(out=dr[r0 + kt * 128:r0 + (kt + 1) * 128, t0:t0 + tw], in_=tile_[:, kt, :]) for kt in range(nk)], r=[rkey], w=[(wkey, r0, t0)], slot=slot, n=nk)

        kmT = [S.sb(es, "kmT%d" % l, [128, 2, NMEM], BF16) for l in range(2)]
        vm = [S.sb(es, "vm%d" % l, [128, 2, 256], BF16) for l in range(2)]

        def mem_phase():
            with ExitStack() as pes:
                mt = S.sb(pes, "mt", [128, KT, NMEM], F32)
                sq = S.sb(pes, "msq", [128, NMEM], F32)
                rt = S.sb(pes, "mrt", [128, NMEM], F32)
                rs = S.sb(pes, "mrs", [128, NMEM], F32)
                mn = S.sb(pes, "mn", [128, KT, NMEM], BF16)
                wt = [S.sb(pes, "mw%d" % i, [128, KT, 256], BF16) for i in range(2)]
                ld_kt(mt, memT, 0, 0, KT, "mt", tw=NMEM)
                for kt in range(KT):
                    S.op("act", lambda e, kt=kt: e.activation(out=sq[:], in_=mt[:, kt, :], func=AF.Square), r=["mt"], w=["msq"])
                    S.op("pe", lambda e, kt=kt: e.matmul(ps[0][:, :NMEM], lhsT=ones_f, rhs=sq[:], start=(kt == 0), stop=(kt == KT - 1)), r=["msq", "cm_f"], w=["ps0"])
                rstd_from_ss(ps[0][:, :NMEM], D, rs[:], rt[:], ["ps0"], "mrt", "mrs")
                for kt in range(KT):
                    S.op("dve", lambda e, kt=kt: e.scalar_tensor_tensor(out=mn[:, kt, :], in0=mt[:, kt, :], scalar=v16[:, 0, kt:kt + 1], in1=rs[:], op0=ALU.mult, op1=ALU.mult),
                         r=["mt", "mrs", "v16"], w=["mn"])
                for l in range(2):
                    wload(wt[0], "mw0", mk_b[l], KT, 0, 256)
                    wload(wt[1], "mw1", mv_b[l], KT, 0, 256)
                    for h in range(2):
                        def f(e, h=h):
                            for kt in range(KT):
                                ins = e.matmul(ps[1][:, :NMEM], lhsT=wt[0][:, kt, h * 128:(h + 1) * 128], rhs=mn[:, kt, :], start=(kt == 0), stop=(kt == KT - 1))
                            return ins
                        S.op("pe", f, r=["mw0", "mn"], w=["ps1"])
                        S.op("dve", lambda e, h=h, l=l: e.tensor_copy(out=kmT[l][:, h, :], in_=ps[1][:, :NMEM]), r=["ps1"], w=["kmT%d" % l])
                    for mb in range(2):
                        def f(e, mb=mb):
                            for kt in range(KT):
                                ins = e.matmul(ps[2][:, 0:256], lhsT=mn[:, kt, mb * 128:(mb + 1) * 128], rhs=wt[1][:, kt, :], start=(kt == 0), stop=(kt == KT - 1))
                            return ins
                        S.op("pe", f, r=["mw1", "mn"], w=["ps2"])
                        S.op("dve", lambda e, mb=mb, l=l: e.tensor_copy(out=vm[l][:, mb, :], in_=ps[2][:, 0:256]), r=["ps2"], w=["vm%d" % l])
                S.flush()

        mem_phase()

        def norm_phase(src, which):
            with ExitStack() as pes:
                xt = [S.sb(pes, "nx%d" % i, [128, KT, 512], F32) for i in range(2)]
                sq = [S.sb(pes, "nsq%d" % i, [128, 512], BF16) for i in range(2)]
                rt = S.sb(pes, "nrt", [128, 512], F32)
                rs = S.sb(pes, "nrs", [128, 512], F32)
                ht = [S.sb(pes, "nh%d" % i, [128, KT, 512], BF16) for i in range(2)]
                for g in range(NG):
                    b = g % 2
                    t0 = g * 512
                    ld_kt(xt[b], src, 0, t0, KT, ("nx", b))
                    for kt in range(KT):
                        sb_ = kt % 2
                        S.op("act", lambda e, b=b, kt=kt, sb_=sb_: e.activation(out=sq[sb_][:], in_=xt[b][:, kt, :], func=AF.Square), r=[("nx", b)], w=[("nsq", sb_)])
                        S.op("pe", lambda e, kt=kt, sb_=sb_: e.matmul(ps[0][:, :], lhsT=ones_b, rhs=sq[sb_][:], start=(kt == 0), stop=(kt == KT - 1)), r=[("nsq", sb_), "cm_b"], w=["ps0"])
                    rstd_from_ss(ps[0][:, :], D, rs[:], rt[:], ["ps0"], "nrt", "nrs")
                    for kt in range(KT):
                        S.op("dve", lambda e, b=b, kt=kt: e.scalar_tensor_tensor(out=ht[b][:, kt, :], in0=xt[b][:, kt, :], scalar=v16[:, which, kt:kt + 1], in1=rs[:], op0=ALU.mult, op1=ALU.mult),
                             r=[("nx", b), "nrs", "v16"], w=[("nh", b)])
                    st_kt(ht[b], hT_d, 0, t0, KT, ("nh", b), "hT_d", ("nhst", b))
                S.flush()

        norm_phase(xT, 1)


        TWO_PI = 6.283185307179586
        PI = 3.141592653589793
        bank = [0]

        def nb():
            bank[0] = (bank[0] + 1) % 4
            return bank[0]

        def mm_acc(psap, pskey, lhs_fn, rhs_fn, kts, rkeys):
            def f(e):
                for kt in range(kts):
                    ins = e.matmul(psap, lhsT=lhs_fn(kt), rhs=rhs_fn(kt), start=(kt == 0), stop=(kt == kts - 1))
                return ins
            S.op("pe", f, r=rkeys, w=[pskey])

        def load_ht(ht, b, t0):
            ld_kt(ht[b], hT_d, 0, t0, KT, ("ht", b))

        def mem_attn_group(l, ht, b, wts, wb, cq, cz, ym, ymkey, tmp):
            qs, pT, zs, rl, osb = tmp
            for half in range(1):
                wload(wts[half], ("mw", half), wb, KT, cq + half * 256, 256)
                wload(wts[2 + half], ("mw", 2 + half), wb, KT, cz + half * 256, 256)
            for h in range(2):
                wq = wts[h // 2]
                wz = wts[2 + h // 2]
                co = (h % 2) * 128
                bq = nb()
                mm_acc(ps[bq][:, :], "ps%d" % bq, lambda kt, wq=wq, co=co: wq[:, kt, co:co + 128], lambda kt: ht[b][:, kt, :], KT, [("mw", h // 2), ("ht", b)])
                S.op("act", lambda e, bq=bq: e.activation(out=qs[:], in_=ps[bq][:, :], func=AF.Copy, scale=128.0 ** -0.5), r=["ps%d" % bq], w=["mqs"])
                for mb in range(2):
                    bs = nb()
                    S.op("pe", lambda e, bs=bs, mb=mb, h=h: e.matmul(ps[bs][:, :], lhsT=kmT[l][:, h, mb * 128:(mb + 1) * 128], rhs=qs[:], start=True, stop=True),
                         r=["mqs", "kmT%d" % l], w=["ps%d" % bs])
                    S.op("act", lambda e, bs=bs, mb=mb: e.activation(out=pT[mb][:], in_=ps[bs][:, :], func=AF.Exp), r=["ps%d" % bs], w=[("mpT", mb)])
                for mb in range(2):
                    S.op("pe", lambda e, mb=mb, h=h: e.matmul(ps[4][:, :], lhsT=vm[l][:, mb, h * 128:(h + 1) * 128], rhs=pT[mb][:], start=(mb == 0), stop=(mb == 1)),
                         r=[("mpT", mb), "vm%d" % l], w=["ps4"])
                for mb in range(2):
                    S.op("pe", lambda e, mb=mb: e.matmul(ps[5][:, :], lhsT=ones_b, rhs=pT[mb][:], start=(mb == 0), stop=(mb == 1)),
                         r=[("mpT", mb), "cm_b"], w=["ps5"])
                bz = nb()
                mm_acc(ps[bz][:, :], "ps%d" % bz, lambda kt, wz=wz, co=co: wz[:, kt, co:co + 128], lambda kt: ht[b][:, kt, :], KT, [("mw", 2 + h // 2), ("ht", b)])
                S.op("act", lambda e, bz=bz: e.activation(out=zs[:], in_=ps[bz][:, :], func=AF.Silu), r=["ps%d" % bz], w=["mzs"])
                S.op("dve", lambda e: e.reciprocal(out=rl[:], in_=ps[5][:, :]), r=["ps5"], w=["mrl"])
                S.op("dve", lambda e: e.tensor_tensor(out=osb[:], in0=ps[4][:, :], in1=rl[:], op=ALU.mult), r=["ps4", "mrl"], w=["mosb"])
                S.op("pool", lambda e, h=h: e.tensor_tensor(out=ym[:, h, :], in0=osb[:], in1=zs[:], op=ALU.mult), r=["mosb", "mzs"], w=[ymkey])

        def mem_tmp(pes):
            return (S.sb(pes, "mqs", [128, 512], BF16), [S.sb(pes, "mpT%d" % i, [128, 512], BF16) for i in range(2)],
                    S.sb(pes, "mzs", [128, 512], F32), S.sb(pes, "mrl", [128, 512], F32), S.sb(pes, "mosb", [128, 512], F32))

        def sc_mem_phase():
            with ExitStack() as pes:
                ht = [S.sb(pes, "ht%d" % i, [128, KT, 512], BF16) for i in range(2)]
                wts = [S.sb(pes, "wt%d" % i, [128, KT, 256], BF16) for i in range(4)]
                csb = S.sb(pes, "csb", [128, 512], F32)
                work = [S.sb(pes, "work%d" % i, [128, 514], F32) for i in range(2)]
                acc = S.sb(pes, "acc", [128, 512], F32)
                acc2 = S.sb(pes, "acc2", [128, 512], F32)
                zs = S.sb(pes, "zs", [128, 512], F32)
                carry = S.sb(pes, "carry", [128, 4, 2], F32)
                cw = S.sb(pes, "cw", [128, 4, 3], F32)
                ya = [S.sb(pes, "ya%d" % i, [128, 4, 512], BF16) for i in range(2)]
                ym = [S.sb(pes, "ym%d" % i, [128, 2, 512], BF16) for i in range(2)]
                mtmp = mem_tmp(pes)
                ci = make_cast(pes, ld_eng="act")
                per_g = (len(items_b) + NG - 1) // NG
                S.dma("sp", lambda e: [e.dma_start(out=cw[:], in_=sccw[:, :, :])], w=["cw"])
                S.op("pool", lambda e: e.memset(carry[:], 0.0), w=["carry"])
                for g in range(NG):
                    b = g % 2
                    t0 = g * 512
                    load_ht(ht, b, t0)
                    for j in range(4):
                        jj, co = j // 2, (j % 2) * 128
                        if j % 2 == 0:
                            for fi, c_off in enumerate((C_SCC, C_SCV, C_SCB, C_SCZ)):
                                wload(wts[fi], ("mw", fi), win0_b, KT, c_off + jj * 256, 256)
                        wk = work[j % 2]
                        wkk = ("work", j % 2)
                        bc, bv = nb(), nb()
                        mm_acc(ps[bc][:, :], "ps%d" % bc, lambda kt, co=co: wts[0][:, kt, co:co + 128], lambda kt, b=b: ht[b][:, kt, :], KT, [("mw", 0), ("ht", b)])
                        mm_acc(ps[bv][:, :], "ps%d" % bv, lambda kt, co=co: wts[1][:, kt, co:co + 128], lambda kt, b=b: ht[b][:, kt, :], KT, [("mw", 1), ("ht", b)])
                        S.op("act", lambda e, bc=bc: e.activation(out=csb[:], in_=ps[bc][:, :], func=AF.Copy), r=["ps%d" % bc], w=["csb"])
                        S.op("pool", lambda e, wk=wk, j=j: e.tensor_copy(out=wk[:, 0:2], in_=carry[:, j, :]), r=["carry"], w=[wkk])
                        S.op("dve", lambda e, wk=wk, bv=bv: e.tensor_tensor(out=wk[:, 2:514], in0=csb[:], in1=ps[bv][:, :], op=ALU.mult), r=["csb", "ps%d" % bv, wkk], w=[wkk])
                        S.op("pool", lambda e, wk=wk, j=j: e.tensor_copy(out=carry[:, j, :], in_=wk[:, 512:514]), r=[wkk], w=["carry"])
                        S.op("dve", lambda e, wk=wk, j=j: e.tensor_scalar(out=acc[:], in0=wk[:, 2:514], scalar1=cw[:, j, 2:3], scalar2=None, op0=ALU.mult), r=[wkk, "cw"], w=["acc"])
                        S.op("dve", lambda e, wk=wk, j=j: e.scalar_tensor_tensor(out=acc[:], in0=wk[:, 1:513], scalar=cw[:, j, 1:2], in1=acc[:], op0=ALU.mult, op1=ALU.add), r=[wkk, "cw", "acc"], w=["acc"])
                        S.op("dve", lambda e, wk=wk, j=j: e.scalar_tensor_tensor(out=acc[:], in0=wk[:, 0:512], scalar=cw[:, j, 0:1], in1=acc[:], op0=ALU.mult, op1=ALU.add), r=[wkk, "cw", "acc"], w=["acc"])
                        bb, bz = nb(), nb()
                        mm_acc(ps[bb][:, :], "ps%d" % bb, lambda kt, co=co: wts[2][:, kt, co:co + 128], lambda kt, b=b: ht[b][:, kt, :], KT, [("mw", 2), ("ht", b)])
                        mm_acc(ps[bz][:, :], "ps%d" % bz, lambda kt, co=co: wts[3][:, kt, co:co + 128], lambda kt, b=b: ht[b][:, kt, :], KT, [("mw", 3), ("ht", b)])
                        S.op("act", lambda e, bz=bz: e.activation(out=zs[:], in_=ps[bz][:, :], func=AF.Silu), r=["ps%d" % bz], w=["zs"])
                        S.op("dve", lambda e, bb=bb: e.tensor_tensor(out=acc2[:], in0=acc[:], in1=ps[bb][:, :], op=ALU.mult), r=["acc", "ps%d" % bb], w=["acc2"])
                        S.op("pool", lambda e, j=j, b=b: e.tensor_tensor(out=ya[b][:, j, :], in0=acc2[:], in1=zs[:], op=ALU.mult), r=["acc2", "zs"], w=[("ya", b)])
                    st_kt(ya[b], yown_d, 0, t0, 4, ("ya", b), "yown_d", ("yast", b))
                    for item in items_b[g * per_g:(g + 1) * per_g]:
                        ci(*item)
                    mem_attn_group(0, ht, b, wts, win0_b, C_MQ0, C_MZ0, ym[b], ("ym", b), mtmp)
                    st_kt(ym[b], yown_d, 1536, t0, 2, ("ym", b), "yown_d", ("ymst", b))
                S.flush()

        sc_mem_phase()

        def ssd_phase():
            with ExitStack() as pes:
                ht = [S.sb(pes, "sht", [128, KT, 512], BF16)]
                wts = [S.sb(pes, "swt%d" % i, [128, KT, 256], BF16) for i in range(3)]
                zs_t = S.sb(pes, "zs_t", [128, 8, 512], BF16)
                xs_t = S.sb(pes, "xs_t", [128, 8, 512], BF16)
                bm_t = S.sb(pes, "bm_t", [128, 2, 512], BF16)
                cm_t = S.sb(pes, "cm_t", [128, 2, 512], BF16)
                yb_t = S.sb(pes, "yb_t", [128, 8, 512], BF16)
                dtT = S.sb(pes, "dtT", [16, 512], F32)
                dte_ = S.sb(pes, "dtexp", [16, 512], F32)
                dtaT = S.sb(pes, "dtaT", [16, 512], F32)
                acT = S.sb(pes, "acT", [16, 512], F32)
                ones32 = S.sb(pes, "ones32", [16, 128], F32)
                hv = S.sb(pes, "hv", [16, 2], F32)
                aneg = S.sb(pes, "aneg", [16, 1], F32)
                es_sel = S.sb(pes, "es_sel", [16, 16, 128], F32)
                xcar = S.sb(pes, "xcar", [128, 12, 3], F32)
                xcw = S.sb(pes, "xcw", [128, 12, 4], F32)
                xcb = S.sb(pes, "xcb", [128, 12], F32)
                dc = S.sb(pes, "dc", [128, 8], F32)
                wk2 = [S.sb(pes, "wk2_%d" % i, [128, 515], F32) for i in range(2)]
                acc3 = S.sb(pes, "acc3", [128, 512], F32)
                tok32 = S.sb(pes, "tok32", [128, 32], F32)
                xs_tok = S.sb(pes, "xs_tok", [128, 16, 64], BF16)
                bm_tok = S.sb(pes, "bm_tok", [128, 256], BF16)
                seg = [S.sb(pes, "seg%d" % i_, [128, 8, 128], F32) for i_ in range(2)]
                es_ = [S.sb(pes, "es%d" % i_, [128, 8, 128], F32) for i_ in range(2)]
                ea_ = [S.sb(pes, "ea%d" % i_, [128, 8, 128], F32) for i_ in range(2)]
                cbm = [S.sb(pes, "cbm%d" % i_, [128, 128], F32) for i_ in range(2)]
                MT = [S.sb(pes, "MT%d" % i_, [128, 8, 128], BF16) for i_ in range(2)]
                ChT = [S.sb(pes, "ChT%d" % i_, [128, 8, 128], BF16) for i_ in range(2)]
                t8 = [S.sb(pes, "t8%d" % i_, [128, 8], F32) for i_ in range(2)]
                dte8 = [S.sb(pes, "dte8%d" % i_, [128, 8], F32) for i_ in range(2)]
                cd8 = [S.sb(pes, "cd8%d" % i_, [128, 8], F32) for i_ in range(2)]
                xdt_pad = [S.sb(pes, "xdt_pad%d" % i_, [128, 8, 128], BF16) for i_ in range(2)]
                xdt_f = [S.sb(pes, "xdt_f%d" % i_, [128, 8, 64], F32) for i_ in range(2)]
                xdt_end = [S.sb(pes, "xdt_end%d" % i_, [128, 8, 64], BF16) for i_ in range(2)]
                S_f = S.sb(pes, "S_f", [128, 16, 64], F32)
                S_pad = [S.sb(pes, "S_pad%d" % i_, [128, 8, 128], BF16) for i_ in range(2)]
                yv = [S.sb(pes, "yv%d" % i_, [128, 4, 128], F32) for i_ in range(2)]
                yz = [S.sb(pes, "yz%d" % i_, [128, 4, 128], F32) for i_ in range(2)]
                sq4 = [S.sb(pes, "sq4%d" % i_, [128, 4, 128], BF16) for i_ in range(2)]
                rt4 = [S.sb(pes, "rt4%d" % i_, [128, 128], F32) for i_ in range(2)]
                rs4 = [S.sb(pes, "rs4%d" % i_, [128, 128], F32) for i_ in range(2)]
                for t_, src in ((hv, hvec), (es_sel, esel), (xcw, ssdcw), (xcb, ssdcb), (dc, dcol)):
                    S.dma("sp", lambda e, t_=t_, src=src: [e.dma_start(out=t_[:], in_=src)], w=[("c", id(t_))])
                S.op("pool", lambda e: e.memset(xcar[:], 0.0), w=["xcar"])
                S.op("pool", lambda e: e.memset(S_f[:], 0.0), w=[("S_f", 0), ("S_f", 1)])
                for i_ in range(2):
                    S.op("pool", lambda e, i_=i_: e.memset(xdt_pad[i_][:], 0.0), w=[("xdt_pad", i_)])
                    S.op("pool", lambda e, i_=i_: e.memset(S_pad[i_][:], 0.0), w=[("S_pad", i_)])
                S.op("pool", lambda e: e.memset(ones32[:], 1.0), w=["ones32"])
                S.op("act", lambda e: e.activation(out=aneg[:], in_=hv[:, 1:2], func=AF.Exp), r=[("c", id(hv))], w=["aneg0"])
                S.op("dve", lambda e: e.tensor_scalar(out=aneg[:], in0=aneg[:], scalar1=-1.0, scalar2=None, op0=ALU.mult), r=["aneg0"], w=["aneg"])
                S.flush()
                psT = ps[6][:].bitcast(BF16)
                for g in range(NG):
                    t0 = g * 512
                    ld_kt(ht[0], hT_d, 0, t0, KT, ("ht", 0))
                    wi = 0
                    for t in range(8):
                        if t % 2 == 0:
                            wi = (wi + 1) % 3
                            wload(wts[wi], ("sw", wi), win0_b, KT, C_SSDZ + t * 128, 256)
                        co = (t % 2) * 128
                        bz = nb()
                        mm_acc(ps[bz][:, :], "ps%d" % bz, lambda kt, wi=wi, co=co: wts[wi][:, kt, co:co + 128], lambda kt: ht[0][:, kt, :], KT, [("sw", wi), ("ht", 0)])
                        S.op("act", lambda e, bz=bz, t=t: e.activation(out=zs_t[:, t, :], in_=ps[bz][:, :], func=AF.Silu), r=["ps%d" % bz], w=["zs_t"])
                    for t in range(12):
                        if t % 2 == 0:
                            wi = (wi + 1) % 3
                            wload(wts[wi], ("sw", wi), win0_b, KT, C_XBC + t * 128, 256)
                        co = (t % 2) * 128
                        bx = nb()
                        wk = wk2[t % 2]
                        wkk = ("wk2", t % 2)
                        mm_acc(ps[bx][:, :], "ps%d" % bx, lambda kt, wi=wi, co=co: wts[wi][:, kt, co:co + 128], lambda kt: ht[0][:, kt, :], KT, [("sw", wi), ("ht", 0)])
                        S.op("pool", lambda e, wk=wk, t=t: e.tensor_copy(out=wk[:, 0:3], in_=xcar[:, t, :]), r=["xcar"], w=[wkk])
                        S.op("act", lambda e, wk=wk, bx=bx: e.activation(out=wk[:, 3:515], in_=ps[bx][:, :], func=AF.Copy), r=["ps%d" % bx, wkk], w=[wkk])
                        S.op("pool", lambda e, wk=wk, t=t: e.tensor_copy(out=xcar[:, t, :], in_=wk[:, 512:515]), r=[wkk], w=["xcar"])
                        S.op("dve", lambda e, wk=wk, t=t: e.tensor_scalar(out=acc3[:], in0=wk[:, 3:515], scalar1=xcw[:, t, 3:4], scalar2=None, op0=ALU.mult), r=[wkk], w=["acc3"])
                        for k_ in range(3):
                            S.op("dve", lambda e, wk=wk, t=t, k_=k_: e.scalar_tensor_tensor(out=acc3[:], in0=wk[:, k_:k_ + 512], scalar=xcw[:, t, k_:k_ + 1], in1=acc3[:], op0=ALU.mult, op1=ALU.add), r=[wkk, "acc3"], w=["acc3"])
                        if t < 8:
                            dst_ = xs_t[:, t, :]
                            dk = "xs_t"
                        elif t < 10:
                            dst_ = bm_t[:, t - 8, :]
                            dk = "bm_t"
                        else:
                            dst_ = cm_t[:, t - 10, :]
                            dk = "cm_t"
                        S.op("act", lambda e, dst_=dst_, t=t: e.activation(out=dst_, in_=acc3[:], func=AF.Silu, bias=xcb[:, t:t + 1]), r=["acc3"], w=[dk])
                    wi = (wi + 1) % 3
                    wload(wts[wi], ("sw", wi), win0_b, KT, C_DT, 16)
                    bd = nb()
                    mm_acc(ps[bd][0:16, :], "ps%d" % bd, lambda kt, wi=wi: wts[wi][:, kt, 0:16], lambda kt: ht[0][:, kt, :], KT, [("sw", wi), ("ht", 0)])
                    S.op("act", lambda e, bd=bd: e.activation(out=dte_[:], in_=ps[bd][0:16, :], func=AF.Exp, bias=hv[:, 0:1]), r=["ps%d" % bd], w=["dtexp"])
                    S.op("act", lambda e: e.activation(out=dtT[:], in_=dte_[:], func=AF.Ln, bias=1.0), r=["dtexp"], w=["dtT"])
                    S.op("dve", lambda e: e.tensor_scalar(out=dtaT[:], in0=dtT[:], scalar1=aneg[:, 0:1], scalar2=None, op0=ALU.mult), r=["dtT"], w=["dtaT"])
                    for c in range(4):
                        q0 = c * 128
                        S.op("dve", lambda e, q0=q0: e.tensor_tensor_scan(out=acT[:, q0:q0 + 128], data0=ones32[:], data1=dtaT[:, q0:q0 + 128], initial=0.0, op0=ALU.mult, op1=ALU.add), r=["dtaT"], w=["acT"])
                        S.op("pe", lambda e, q0=q0: e.transpose(out=ps[7][:, 0:16], in_=dtT[:, q0:q0 + 128], identity=ident_f[0:16, 0:16]), r=["dtT"], w=["ps7"])
                        S.op("pe", lambda e, q0=q0: e.transpose(out=ps[7][:, 16:32], in_=acT[:, q0:q0 + 128], identity=ident_f[0:16, 0:16]), r=["acT", "ps7"], w=["ps7"])
                        S.op("dve", lambda e: e.tensor_copy(out=tok32[:], in_=ps[7][:, 0:32]), r=["ps7"], w=["tok32"])
                        for q4 in range(2):
                            def f(e, q4=q4, q0=q0):
                                for i in range(4):
                                    ins = e.transpose(out=psT[:, i * 128:(i + 1) * 128], in_=xs_t[:, q4 * 4 + i, q0:q0 + 128], identity=ident_b)
                                return ins
                            S.op("pe", f, r=["xs_t"], w=["ps6"])
                            S.op("act", lambda e, q4=q4: e.activation(out=xs_tok[:, q4 * 8:(q4 + 1) * 8, :].rearrange("p h d -> p (h d)"), in_=psT[:, 0:512], func=AF.Copy), r=["ps6"], w=["xs_tok"])
                        def f(e, q0=q0):
                            for i in range(2):
                                ins = e.transpose(out=psT[:, i * 128:(i + 1) * 128], in_=bm_t[:, i, q0:q0 + 128], identity=ident_b)
                            return ins
                        S.op("pe", f, r=["bm_t"], w=["ps6"])
                        S.op("act", lambda e: e.activation(out=bm_tok[:], in_=psT[:, 0:256], func=AF.Copy), r=["ps6"], w=["bm_tok"])
                        def st1(gh, q0):
                            ba = 2 * gh
                            def f(e):
                                for hh in range(8):
                                    ins = e.matmul(ps[ba + hh // 4][:, (hh % 4) * 128:(hh % 4 + 1) * 128], lhsT=es_sel[:, gh * 8 + hh, :], rhs=acT[:, q0:q0 + 128], start=True, stop=True)
                                return ins
                            S.op("pe", f, r=["acT"], w=["ps%d" % ba, "ps%d" % (ba + 1)])

                        def st2(gh, q0):
                            ba = 2 * gh
                            for hh in range(8):
                                S.op("dve", lambda e, hh=hh: e.tensor_scalar(out=seg[gh][:, hh, :], in0=ps[ba + hh // 4][:, (hh % 4) * 128:(hh % 4 + 1) * 128], scalar1=tok32[:, 16 + gh * 8 + hh:17 + gh * 8 + hh], scalar2=0.0, op0=ALU.subtract, op1=ALU.min),
                                     r=["ps%d" % ba, "ps%d" % (ba + 1), "tok32"], w=[("seg", gh)])

                        def st3(gh, q0):
                            ba = 2 * gh
                            S.op("act", lambda e: e.activation(out=es_[gh][:], in_=seg[gh][:], func=AF.Exp), r=[("seg", gh)], w=[("es", gh)])
                            for hb in range(2):
                                S.op("act", lambda e, hb=hb: e.activation(out=ea_[gh][:, hb * 4:(hb + 1) * 4, :].rearrange("p h q -> p (h q)"), in_=ps[ba + hb][:, :], func=AF.Exp), r=["ps%d" % ba, "ps%d" % (ba + 1)], w=[("ea", gh)])
                            for hb in range(2):
                                S.op("dve", lambda e, hb=hb: e.tensor_tensor(out=t8[gh][:, hb * 4:(hb + 1) * 4], in0=ps[ba + hb][:, :].rearrange("p (h q) -> p h q", q=128)[:, :, 127], in1=tok32[:, 16 + gh * 8 + hb * 4:16 + gh * 8 + hb * 4 + 4], op=ALU.subtract),
                                     r=["ps%d" % ba, "ps%d" % (ba + 1), "tok32"], w=[("t8", gh)])
                            S.op("act", lambda e: e.activation(out=dte8[gh][:], in_=t8[gh][:], func=AF.Exp), r=[("t8", gh)], w=[("dte8", gh)])
                            S.op("act", lambda e: e.activation(out=cd8[gh][:], in_=ea_[gh][:, :, 127], func=AF.Copy), r=[("ea", gh)], w=[("cd8", gh)])

                        def st4(gh, q0):
                            cbo = ps[4][:, gh * 128:(gh + 1) * 128]
                            S.op("pe", lambda e: e.matmul(cbo, lhsT=bm_t[:, gh, q0:q0 + 128], rhs=cm_t[:, gh, q0:q0 + 128], start=True, stop=True), r=["bm_t", "cm_t"], w=["ps4"])
                            S.op("dve", lambda e: e.tensor_tensor(out=cbm[gh][:], in0=cbo, in1=tri_f, op=ALU.mult), r=["ps4"], w=[("cbm", gh)])
                            S.op("dve", lambda e: e.tensor_tensor(out=MT[gh][:], in0=es_[gh][:], in1=cbm[gh][:].unsqueeze(1).broadcast_to([128, 8, 128]), op=ALU.mult), r=[("es", gh), ("cbm", gh)], w=[("MT", gh)])
                            S.op("pool", lambda e: e.tensor_tensor(out=ChT[gh][:], in0=ea_[gh][:], in1=cm_t[:, gh, q0:q0 + 128].unsqueeze(1).broadcast_to([128, 8, 128]), op=ALU.mult), r=[("ea", gh), "cm_t"], w=[("ChT", gh)])

                        def st5(gh, q0):
                            S.op("dve", lambda e: e.tensor_tensor(out=xdt_f[gh][:], in0=xs_tok[:, gh * 8:(gh + 1) * 8, :], in1=tok32[:, gh * 8:(gh + 1) * 8].unsqueeze(2).broadcast_to([128, 8, 64]), op=ALU.mult), r=["xs_tok", "tok32"], w=[("xdt_f", gh)])
                            for ev in range(2):
                                S.op("pool", lambda e, ev=ev: e.tensor_copy(out=xdt_pad[gh][:, ev::2, ev * 64:(ev + 1) * 64], in_=xdt_f[gh][:, ev::2, :]), r=[("xdt_f", gh)], w=[("xdt_pad", gh)])
                                S.op("act", lambda e, ev=ev: e.activation(out=S_pad[gh][:, ev::2, ev * 64:(ev + 1) * 64], in_=S_f[:, gh * 8 + ev:(gh + 1) * 8:2, :], func=AF.Copy), r=[("S_f", gh)], w=[("S_pad", gh)])
                            S.op("dve", lambda e: e.tensor_tensor(out=xdt_end[gh][:], in0=xdt_f[gh][:], in1=dte8[gh][:].unsqueeze(2).broadcast_to([128, 8, 64]), op=ALU.mult), r=[("xdt_f", gh), ("dte8", gh)], w=[("xdt_end", gh)])

                        def st6(gh, q0):
                            by = 5 + gh
                            def f(e):
                                for t in range(4):
                                    o_ = ps[by][:, t * 128:(t + 1) * 128]
                                    e.matmul(o_, lhsT=xdt_pad[gh][:, 2 * t, :], rhs=MT[gh][:, 2 * t, :], start=True, stop=False)
                                    e.matmul(o_, lhsT=xdt_pad[gh][:, 2 * t + 1, :], rhs=MT[gh][:, 2 * t + 1, :], start=False, stop=False)
                                    e.matmul(o_, lhsT=S_pad[gh][:, 2 * t, :], rhs=ChT[gh][:, 2 * t, :], start=False, stop=False)
                                    ins = e.matmul(o_, lhsT=S_pad[gh][:, 2 * t + 1, :], rhs=ChT[gh][:, 2 * t + 1, :], start=False, stop=True)
                                return ins
                            S.op("pe", f, r=[("xdt_pad", gh), ("MT", gh), ("S_pad", gh), ("ChT", gh)], w=["ps%d" % by])
                            S.op("pe", lambda e: e.matmul(ps[7][:, :], lhsT=bm_tok[:, gh * 128:(gh + 1) * 128], rhs=xdt_end[gh][:].rearrange("p h d -> p (h d)"), start=True, stop=True), r=["bm_tok", ("xdt_end", gh)], w=["ps7"])
                            S.op("dve", lambda e: e.tensor_tensor(out=S_f[:, gh * 8:(gh + 1) * 8, :], in0=S_f[:, gh * 8:(gh + 1) * 8, :], in1=cd8[gh][:].unsqueeze(2).broadcast_to([128, 8, 64]), op=ALU.mult), r=[("S_f", gh), ("cd8", gh), ("S_pad", gh)], w=[("S_f", gh)])
                            S.op("dve", lambda e: e.tensor_tensor(out=S_f[:, gh * 8:(gh + 1) * 8, :].rearrange("p h d -> p (h d)"), in0=ps[7][:, :], in1=S_f[:, gh * 8:(gh + 1) * 8, :].rearrange("p h d -> p (h d)"), op=ALU.add), r=[("S_f", gh), "ps7"], w=[("S_f", gh)])

                        def st7(gh, q0):
                            by = 5 + gh
                            for t in range(4):
                                ct = gh * 4 + t
                                S.op("dve", lambda e, t=t, ct=ct: e.scalar_tensor_tensor(out=yv[gh][:, t, :], in0=xs_t[:, ct, q0:q0 + 128], scalar=dc[:, ct:ct + 1], in1=ps[by][:, t * 128:(t + 1) * 128], op0=ALU.mult, op1=ALU.add), r=["xs_t", "ps%d" % by], w=[("yv", gh)])
                            S.op("dve", lambda e: e.tensor_tensor(out=yz[gh][:], in0=yv[gh][:], in1=zs_t[:, gh * 4:(gh + 1) * 4, q0:q0 + 128], op=ALU.mult), r=[("yv", gh), "zs_t"], w=[("yz", gh)])
                            S.op("act", lambda e: e.activation(out=sq4[gh][:], in_=yz[gh][:], func=AF.Square), r=[("yz", gh)], w=[("sq4", gh)])
                            no_ = ps[4][:, 256 + gh * 128:256 + (gh + 1) * 128]
                            def f(e):
                                for t in range(4):
                                    ins = e.matmul(no_, lhsT=ones_b, rhs=sq4[gh][:, t, :], start=(t == 0), stop=(t == 3))
                                return ins
                            S.op("pe", f, r=[("sq4", gh)], w=["ps4"])
                            rstd_from_ss(no_, 512, rs4[gh][:], rt4[gh][:], ["ps4"], ("rt4", gh), ("rs4", gh))
                            for t in range(4):
                                ct = gh * 4 + t
                                S.op("dve", lambda e, t=t, ct=ct: e.scalar_tensor_tensor(out=yb_t[:, ct, q0:q0 + 128], in0=yz[gh][:, t, :], scalar=v16[:, 4, ct:ct + 1], in1=rs4[gh][:], op0=ALU.mult, op1=ALU.mult), r=[("yz", gh), ("rs4", gh)], w=["yb_t"])

                        for st in (st1, st2, st3, st4, st5, st6, st7):
                            for gh in range(2):
                                st(gh, q0)
                    st_kt(yb_t, yown_d, 512, t0, 8, "yb_t", "yown_d", "ybst")
                S.flush()

        if stop_after >= 1.5:
            ssd_phase()

        def out_phase(wb, kto, resid, dst, final, ysrc):
            with ExitStack() as pes:
                yt = [S.sb(pes, "oy%d" % i, [128, kto, 512], BF16) for i in range(2)]
                wts = [S.sb(pes, "ow%d" % i, [128, kto, 256], BF16) for i in range(2)]
                xr = [S.sb(pes, "oxr%d" % i, [128, 512], F32) for i in range(3)]
                ot = [S.sb(pes, "oo%d" % i, [128, 512], F32) for i in range(3)]
                sq = [S.sb(pes, "fsq%d" % i, [128, 512], BF16) for i in range(2)]
                ssr = S.sb(pes, "ssr", [1, 512], F32)
                cnt = 0
                for g in range(NG):
                    b = g % 2
                    t0 = g * 512
                    ld_kt(yt[b], ysrc, 0, t0, kto, ("oy", b), r=["yall_d"])
                    for oc in range(8):
                        if oc % 2 == 0:
                            wi = (oc // 2) % 2
                            wload(wts[wi], ("ow", wi), wb, kto, oc * 128, 256)
                        wt_ = wts[(oc // 2) % 2]
                        co = (oc % 2) * 128
                        i3 = cnt % 3
                        cnt += 1
                        S.dma("sp", lambda e, i3=i3, oc=oc, t0=t0: [e.dma_start(out=xr[i3][:], in_=resid(oc * 128, (oc + 1) * 128, t0, 512))], r=["resid"], w=[("oxr", i3)])
                        bo = nb()
                        mm_acc(ps[bo][:, :], "ps%d" % bo, lambda kt, wt_=wt_, co=co: wt_[:, kt, co:co + 128], lambda kt, b=b: yt[b][:, kt, :], kto, [("ow", (oc // 2) % 2), ("oy", b)])
                        S.op("dve", lambda e, i3=i3, bo=bo: e.tensor_tensor(out=ot[i3][:], in0=ps[bo][:, :], in1=xr[i3][:], op=ALU.add), r=["ps%d" % bo, ("oxr", i3)], w=[("oo", i3)])
                        S.dma(os.environ.get("STQ", "pool"), lambda e, i3=i3, oc=oc, t0=t0: [e.dma_start(out=dst(oc * 128, (oc + 1) * 128, t0, 512), in_=ot[i3][:])], r=[("oo", i3)], w=[("resid_out", oc, t0)], slot=("oost", i3))
                        if final:
                            sb_ = oc % 2
                            S.op("act", lambda e, i3=i3, sb_=sb_: e.activation(out=sq[sb_][:], in_=ot[i3][:], func=AF.Square), r=[("oo", i3)], w=[("fsq", sb_)])
                            S.op("pe", lambda e, oc=oc, sb_=sb_: e.matmul(ps[6][:, :], lhsT=ones_b, rhs=sq[sb_][:], start=(oc == 0), stop=(oc == 7)), r=[("fsq", sb_), "cm_b"], w=["ps6"])
                    if final:
                        S.op("act", lambda e: e.activation(out=ssr[:], in_=ps[6][0:1, :], func=AF.Copy), r=["ps6"], w=["ssr"])
                        S.dma("pool", lambda e, t0=t0: [e.dma_start(out=ssown_d[0:1, t0:t0 + 512], in_=ssr[:])], r=["ssr"], w=[("ssown_d", t0)], slot="ssst")
                S.flush()

        def final_norm_phase():
            with ExitStack() as pes:
                ss2 = [S.sb(pes, "ss2_%d" % i, [2, 512], F32) for i in range(2)]
                rt = S.sb(pes, "frt", [128, 512], F32)
                rs = [S.sb(pes, "frs%d" % i, [128, 512], F32) for i in range(2)]
                xr = [S.sb(pes, "fx%d" % i, [128, 8, 512], F32) for i in range(2)]
                ot = [S.sb(pes, "fo%d" % i, [128, 8, 512], F32) for i in range(2)]
                for g in range(NG):
                    b = g % 2
                    t0 = g * 512
                    S.dma("sp", lambda e, b=b, t0=t0: [e.dma_start(out=ss2[b][:], in_=ssall_d[0:2, t0:t0 + 512])], w=[("ss2", b)])
                    ld_kt(xr[b], x2own_d, 0, t0, 8, ("fx", b))
                    S.op("pe", lambda e, b=b: e.matmul(ps[6][:, :], lhsT=ones_f[0:2, :], rhs=ss2[b][:], start=True, stop=True), r=[("ss2", b)], w=["ps6"])
                    rstd_from_ss(ps[6][:, :], D, rs[b][:], rt[:], ["ps6"], "frt", ("frs", b))
                    for oc in range(8):
                        S.op("dve", lambda e, b=b, oc=oc: e.scalar_tensor_tensor(out=ot[b][:, oc, :], in0=xr[b][:, oc, :], scalar=v16[:, 3, oc:oc + 1], in1=rs[b][:], op0=ALU.mult, op1=ALU.mult), r=[("fx", b), ("frs", b), "v16"], w=[("fo", b)])
                    st_kt(ot[b], outT, 0, t0, 8, ("fo", b), "final_out", ("fost", b))
                S.flush()

        RG = [[0, 1], [2, 3], [4, 5], [6, 7]]
        if stop_after >= 2:
            S.collective([(lambda e, j=j: e.collective_compute("AllGather", ALU.bypass, replica_groups=RG, ins=[yown_d[j * 128:(j + 1) * 128, :].opt()], outs=[yall_d[j * 256:(j + 1) * 256, :].opt()])) for j in range(14)])
            out_phase(wout0_b, 28, lambda a, b_, t_, w_: xTo[a:b_, t_:t_ + w_], x1o_ap, False, yall_d)
            S.collective([(lambda e, th=th, j=j: e.collective_compute("AllGather", ALU.bypass, replica_groups=RG, ins=[x1own_d[th, j * 128:(j + 1) * 128, :].opt()], outs=[x1T_d[th, j * 256:(j + 1) * 256, :].opt()])) for th in range(2) for j in range(8)])
        if stop_after >= 3:
            norm_phase(x1g_ap, 2)

        def rope_apply(pes_tiles, t_ps, cos2, sin2, out_bf, okey, scale, rkeys, cskeys=("cs",)):
            t_sb, o1, o2 = pes_tiles
            S.op("act", lambda e: e.activation(out=t_sb[:], in_=t_ps, func=AF.Copy), r=rkeys, w=["t_sb"])
            S.op("pe", lambda e: e.matmul(ps[7][0:64, :], lhsT=cm_f[0:64, 3, 0:64], rhs=t_sb[:], start=True, stop=True), r=["t_sb"], w=["ps7"])
            S.op("dve", lambda e: e.tensor_tensor(out=o1[:], in0=t_sb[:], in1=cos2, op=ALU.mult), r=["t_sb"] + list(cskeys), w=["o1"])
            S.op("dve", lambda e: e.tensor_tensor(out=o2[:], in0=ps[7][0:64, :], in1=sin2, op=ALU.mult), r=["ps7"] + list(cskeys), w=["o2"])
            S.op("dve", lambda e: e.scalar_tensor_tensor(out=out_bf, in0=o1[:], scalar=scale, in1=o2[:], op0=ALU.mult, op1=ALU.add), r=["o1", "o2"], w=[okey])

        def l1_in_phase():
            with ExitStack() as pes:
                ht2 = [S.sb(pes, "ht%d" % i, [128, KT, 512], BF16) for i in range(2)]
                wts = [S.sb(pes, "wt%d" % i, [128, KT, 256], BF16) for i in range(4)]
                cq = S.sb(pes, "cq", [128, 6, 512], F32)
                sq = [S.sb(pes, "lsq%d" % i, [128, 512], BF16) for i in range(2)]
                rt = S.sb(pes, "lrt", [128, 512], F32)
                rs = S.sb(pes, "lrs", [128, 512], F32)
                cqn_t = S.sb(pes, "cqn_t", [128, 6, 512], BF16)
                zc_t = S.sb(pes, "zc_t", [128, 8, 512], BF16)
                ym = S.sb(pes, "ym", [128, 2, 512], BF16)
                nwq = S.sb(pes, "nwq", [128, 6], F32)
                nwk = S.sb(pes, "nwk", [128, 4], F32)
                ivf = S.sb(pes, "ivf", [64, 1], F32)
                posi = S.sb(pes, "posi", [1, 512], I32)
                posf = S.sb(pes, "posf", [1, 512], F32)
                a1 = S.sb(pes, "a1", [64, 512], F32)
                a2 = S.sb(pes, "a2", [64, 512], F32)
                a3 = S.sb(pes, "a3", [64, 512], F32)
                ki = S.sb(pes, "ki", [64, 512], I32)
                cs = S.sb(pes, "cs", [64, 2, 512], F32)
                rtl = (S.sb(pes, "t_sb", [64, 512], F32), S.sb(pes, "o1", [64, 512], F32), S.sb(pes, "o2", [64, 512], F32))
                krb = S.sb(pes, "krb", [64, 512], BF16)
                mtmp = mem_tmp(pes)
                for t_, src in ((nwq, qnw), (nwk, kvnw), (ivf, invf)):
                    S.dma("sp", lambda e, t_=t_, src=src: [e.dma_start(out=t_[:], in_=src)], w=[("c", id(t_))])
                S.flush()
                for g in range(NG):
                    t0 = g * 512
                    hb_ = g % 2
                    ht = [ht2[hb_]]
                    hk = ("ht", hb_)
                    ld_kt(ht[0], hT_d, 0, t0, KT, hk)
                    for (c0, nt, nw, dst, r0) in ((C_CQ, 6, nwq, cqn_d, 0), (C_CKV, 4, nwk, lat_d, 0)):
                        for t in range(nt):
                            if t % 2 == 0:
                                wload(wts[(t // 2) % 4], ("mw", (t // 2) % 4), win1_b, KT, c0 + t * 128, 256)
                            wt_ = wts[(t // 2) % 4]
                            co = (t % 2) * 128
                            bq = nb()
                            mm_acc(ps[bq][:, :], "ps%d" % bq, lambda kt, wt_=wt_, co=co: wt_[:, kt, co:co + 128], lambda kt, htg=ht[0]: htg[:, kt, :], KT, [("mw", (t // 2) % 4), hk])
                            S.op("act", lambda e, bq=bq, t=t: e.activation(out=cq[:, t, :], in_=ps[bq][:, :], func=AF.Copy), r=["ps%d" % bq], w=["cq"])
                            sb_ = t % 2
                            S.op("act", lambda e, t=t, sb_=sb_: e.activation(out=sq[sb_][:], in_=cq[:, t, :], func=AF.Square), r=["cq"], w=[("lsq", sb_)])
                            S.op("pe", lambda e, t=t, sb_=sb_, nt=nt: e.matmul(ps[6][:, :], lhsT=ones_b, rhs=sq[sb_][:], start=(t == 0), stop=(t == nt - 1)), r=[("lsq", sb_)], w=["ps6"])
                        rstd_from_ss(ps[6][:, :], nt * 128, rs[:], rt[:], ["ps6"], "lrt", "lrs")
                        for t in range(nt):
                            S.op("dve", lambda e, t=t, nw=nw: e.scalar_tensor_tensor(out=cqn_t[:, t, :], in0=cq[:, t, :], scalar=nw[:, t:t + 1], in1=rs[:], op0=ALU.mult, op1=ALU.mult), r=["cq", "lrs"], w=["cqn_t"])
                        st_kt(cqn_t, dst, 0, t0, nt, "cqn_t", ("d", id(dst)), "cqnst")
                    S.dma("sp", lambda e, t0=t0: [e.dma_start(out=posi[:], in_=pos[0:1, t0:t0 + 512])], w=["posi"])
                    S.op("dve", lambda e: e.tensor_copy(out=posf[:], in_=posi[:]), r=["posi"], w=["posf"])
                    S.op("pe", lambda e: e.matmul(ps[5][0:64, :], lhsT=ones_f[0:1, 0:64], rhs=posf[:], start=True, stop=True), r=["posf"], w=["ps5"])
                    C1 = 6.28125
                    C2 = TWO_PI - C1
                    PIS = 3.1415925
                    S.op("dve", lambda e: e.tensor_scalar(out=a1[:], in0=ps[5][0:64, :], scalar1=ivf[:, 0:1], scalar2=None, op0=ALU.mult), r=["ps5"], w=["a1"])
                    S.op("dve", lambda e: e.tensor_scalar(out=a2[:], in0=a1[:], scalar1=1.0 / TWO_PI, scalar2=None, op0=ALU.mult), r=["a1"], w=["a2"])
                    S.op("dve", lambda e: e.tensor_copy(out=ki[:], in_=a2[:]), r=["a2"], w=["ki"])
                    S.op("dve", lambda e: e.tensor_copy(out=a2[:], in_=ki[:]), r=["ki"], w=["a2"])
                    S.op("dve", lambda e: e.scalar_tensor_tensor(out=a1[:], in0=a2[:], scalar=-C1, in1=a1[:], op0=ALU.mult, op1=ALU.add), r=["a2", "a1"], w=["a1"])
                    S.op("dve", lambda e: e.scalar_tensor_tensor(out=a1[:], in0=a2[:], scalar=-C2, in1=a1[:], op0=ALU.mult, op1=ALU.add), r=["a2", "a1"], w=["a1"])

                    def fold(t_, key):
                        S.op("dve", lambda e: e.tensor_scalar(out=a2[:], in0=t_[:], scalar1=PI, scalar2=-TWO_PI, op0=ALU.is_gt, op1=ALU.mult), r=[key], w=["a2"])
                        S.op("dve", lambda e: e.tensor_tensor(out=t_[:], in0=t_[:], in1=a2[:], op=ALU.add), r=[key, "a2"], w=[key])
                        S.op("dve", lambda e: e.tensor_scalar(out=a2[:], in0=t_[:], scalar1=-PI, scalar2=TWO_PI, op0=ALU.is_lt, op1=ALU.mult), r=[key], w=["a2"])
                        S.op("dve", lambda e: e.tensor_tensor(out=t_[:], in0=t_[:], in1=a2[:], op=ALU.add), r=[key, "a2"], w=[key])
                        S.op("dve", lambda e: e.tensor_scalar(out=t_[:], in0=t_[:], scalar1=PIS, scalar2=-PIS, op0=ALU.min, op1=ALU.max), r=[key], w=[key])
                    fold(a1, "a1")
                    S.op("act", lambda e: e.activation(out=cs[:, 1, :], in_=a1[:], func=AF.Sin), r=["a1"], w=["cs"])
                    S.op("dve", lambda e: e.tensor_scalar(out=a3[:], in0=a1[:], scalar1=PI / 2, scalar2=None, op0=ALU.add), r=["a1"], w=["a3"])
                    fold(a3, "a3")
                    S.op("act", lambda e: e.activation(out=cs[:, 0, :], in_=a3[:], func=AF.Sin), r=["a3"], w=["cs"])
                    S.dma("pool", lambda e, t0=t0: [e.dma_start(out=cs_d[c_, :, t0:t0 + 512], in_=cs[:, c_, :]) for c_ in range(2)], r=["cs"], w=[("cs_d", t0)], slot="csst", n=2)
                    wload(wts[0], ("mw", 0), win1_b, KT, C_KR, 64)
                    bk = nb()
                    mm_acc(ps[bk][0:64, :], "ps%d" % bk, lambda kt: wts[0][:, kt, 0:64], lambda kt, htg=ht[0]: htg[:, kt, :], KT, [("mw", 0), hk])
                    rope_apply(rtl, ps[bk][0:64, :], cs[:, 0, :], cs[:, 1, :], krb[:], "krb", 1.0, ["ps%d" % bk])
                    S.dma("pool", lambda e, t0=t0: [e.dma_start(out=lat_d[512:576, t0:t0 + 512], in_=krb[:])], r=["krb"], w=[("lat_kr", t0)], slot="krst")
                    for t in range(8):
                        if t % 2 == 0:
                            wload(wts[(t // 2) % 4], ("mw", (t // 2) % 4), win1_b, KT, C_ZC + t * 128, 256)
                        wt_ = wts[(t // 2) % 4]
                        co = (t % 2) * 128
                        bz = nb()
                        mm_acc(ps[bz][:, :], "ps%d" % bz, lambda kt, wt_=wt_, co=co: wt_[:, kt, co:co + 128], lambda kt, htg=ht[0]: htg[:, kt, :], KT, [("mw", (t // 2) % 4), hk])
                        S.op("act", lambda e, bz=bz, t=t: e.activation(out=zc_t[:, t, :], in_=ps[bz][:, :], func=AF.Silu), r=["ps%d" % bz], w=["zc_t"])
                    st_kt(zc_t, zc_d, 0, t0, 8, "zc_t", "zc_d", "zcst")
                    mem_attn_group(1, ht2, hb_, wts, win1_b, C_MQ1, C_MZ1, ym, "ym1", mtmp)
                    st_kt(ym, yown1_d, 1024, t0, 2, "ym1", "yown1_d", "ym1st")
                S.flush()

        if stop_after >= 4:
            l1_in_phase()

        def mla_phase():
            with ExitStack() as pes:
                ckvn = S.sb(pes, "ckvn", [128, 4, NTOK], BF16)
                kr = S.sb(pes, "kr", [128, NTOK], BF16)
                KhT = S.sb(pes, "KhT", [128, NTOK], BF16)
                Vh = S.sb(pes, "Vh", [128, NCH, 128], BF16)
                mk_b_ = S.sb(pes, "mk_b", [128, 4, 512], BF16)
                wkv = S.sb(pes, "wkv", [128, 4, 256], BF16)
                wq = S.sb(pes, "wq", [128, 6, 192], BF16)
                cqg = [S.sb(pes, "cqg%d" % i, [128, 6, 512], BF16) for i in range(2)]
                csg = [S.sb(pes, "csg%d" % i, [64, 2, 512], F32) for i in range(2)]
                zcg = [S.sb(pes, "zcg%d" % i, [128, 512], BF16) for i in range(2)]
                Qn = S.sb(pes, "Qn", [128, 512], BF16)
                Qr = S.sb(pes, "Qr", [128, 512], BF16)
                lacc = S.sb(pes, "lacc", [128, 512], F32)
                pT = [S.sb(pes, "pT%d" % i, [128, 512], BF16) for i in range(6)]
                rl = S.sb(pes, "arl", [128, 512], F32)
                osb = S.sb(pes, "aosb", [128, 512], F32)
                yc = [S.sb(pes, "yc%d" % i, [128, 512], BF16) for i in range(2)]
                rtl = (S.sb(pes, "t_sb", [64, 512], F32), S.sb(pes, "o1", [64, 512], F32), S.sb(pes, "o2", [64, 512], F32))
                S.dma("sp", lambda e: [e.dma_start(out=mk_b_[:], in_=amask[:, :, :])], w=["mk_b"])
                for rt_ in range(4):
                    S.dma("sp", lambda e, rt_=rt_: [e.dma_start(out=ckvn[:, rt_, :], in_=lat_d[rt_ * 128:(rt_ + 1) * 128, :])], w=[("ckvn", rt_)])
                S.op("pool", lambda e: e.memset(kr[64:128, :], 0.0), w=["kr_pad"])
                S.op("pool", lambda e: e.memset(Qr[64:128, :], 0.0), w=["Qr_pad"])
                S.dma("sp", lambda e: [e.dma_start(out=kr[0:64, :], in_=lat_d[512:576, :])], w=["kr"])
                S.flush()
                sc_ = 192.0 ** -0.5
                it = 0
                for h in range(8):
                    S.dma("sp", lambda e, h=h: [e.dma_start(out=wkv[:, kt, :], in_=wukv_b[kt * 128:(kt + 1) * 128, h * 256:(h + 1) * 256]) for kt in range(4)], w=["wkv"], n=4)
                    S.dma("sp", lambda e, h=h: [e.dma_start(out=wq[:, kt, :], in_=wuq_b[kt * 128:(kt + 1) * 128, h * 192:(h + 1) * 192]) for kt in range(6)], w=["wq"], n=6)
                    for kg in range(NG):
                        bk = nb()
                        mm_acc(ps[bk][:, :], "ps%d" % bk, lambda kt: wkv[:, kt, 0:128], lambda kt, kg=kg: ckvn[:, kt, kg * 512:(kg + 1) * 512], 4, ["wkv"])
                        if kg % 2 == 0:
                            S.op("act", lambda e, bk=bk, kg=kg: e.activation(out=KhT[:, kg * 512:(kg + 1) * 512], in_=ps[bk][:, :], func=AF.Copy), r=["ps%d" % bk], w=["KhT"])
                        else:
                            S.op("dve", lambda e, bk=bk, kg=kg: e.tensor_copy(out=KhT[:, kg * 512:(kg + 1) * 512], in_=ps[bk][:, :]), r=["ps%d" % bk], w=["KhT"])
                    for kb4 in range(NCH // 4):
                        bv = nb()
                        def f(e, kb4=kb4, bv=bv):
                            for i in range(4):
                                kb = kb4 * 4 + i
                                for kt in range(4):
                                    ins = e.matmul(ps[bv][:, i * 128:(i + 1) * 128], lhsT=ckvn[:, kt, kb * 128:(kb + 1) * 128], rhs=wkv[:, kt, 128:256], start=(kt == 0), stop=(kt == 3))
                            return ins
                        S.op("pe", f, r=["wkv"], w=["ps%d" % bv])
                        if kb4 % 2 == 0:
                            S.op("dve", lambda e, bv=bv, kb4=kb4: e.tensor_copy(out=Vh[:, kb4 * 4:(kb4 + 1) * 4, :].rearrange("p k d -> p (k d)"), in_=ps[bv][:, :]), r=["ps%d" % bv], w=["Vh"])
                        else:
                            S.op("act", lambda e, bv=bv, kb4=kb4: e.activation(out=Vh[:, kb4 * 4:(kb4 + 1) * 4, :].rearrange("p k d -> p (k d)"), in_=ps[bv][:, :], func=AF.Copy), r=["ps%d" % bv], w=["Vh"])
                    S.flush()
                    for qg in range(NG):
                        b = qg % 2
                        t0 = qg * 512
                        ld_kt(cqg[b], cqn_d, 0, t0, 6, ("cqg", b))
                        S.dma("sp", lambda e, b=b, t0=t0: [e.dma_start(out=csg[b][:, c_, :], in_=cs_d[c_, :, t0:t0 + 512]) for c_ in range(2)], w=[("csg", b)], n=2)
                        S.dma("sp", lambda e, b=b, t0=t0, h=h: [e.dma_start(out=zcg[b][:], in_=zc_d[h * 128:(h + 1) * 128, t0:t0 + 512])], w=[("zcg", b)])
                        bq = nb()
                        mm_acc(ps[bq][:, :], "ps%d" % bq, lambda kt: wq[:, kt, 0:128], lambda kt, b=b: cqg[b][:, kt, :], 6, ["wq", ("cqg", b)])
                        S.op("act", lambda e, bq=bq: e.activation(out=Qn[:], in_=ps[bq][:, :], func=AF.Copy, scale=sc_), r=["ps%d" % bq], w=["Qn"])
                        br = nb()
                        mm_acc(ps[br][0:64, :], "ps%d" % br, lambda kt: wq[:, kt, 128:192], lambda kt, b=b: cqg[b][:, kt, :], 6, ["wq", ("cqg", b)])
                        rope_apply(rtl, ps[br][0:64, :], csg[b][:, 0, :], csg[b][:, 1, :], Qr[0:64, :], "Qr", 1.0, ["ps%d" % br], cskeys=[("csg", b)])
                        S.op("act", lambda e: e.activation(out=Qr[0:64, :], in_=Qr[0:64, :], func=AF.Copy, scale=sc_), r=["Qr"], w=["Qr"])
                        po, pl = 4 + qg % 2, 6 + qg % 2
                        nkb = 4 * qg + 4
                        LA = 2
                        for step in range(nkb + LA):
                            if step < nkb:
                                kb = step
                                bs = kb % 4
                                pi = kb % 6

                                c0 = max(0, kb - 4 * qg) * 128

                                def f(e, bs=bs, kb=kb, c0=c0):
                                    e.matmul(ps[bs][:, c0:], lhsT=KhT[:, kb * 128:(kb + 1) * 128], rhs=Qn[:, c0:], start=True, stop=False)
                                    return e.matmul(ps[bs][:, c0:], lhsT=kr[:, kb * 128:(kb + 1) * 128], rhs=Qr[:, c0:], start=False, stop=True)
                                S.op("pe", f, r=["KhT", "Qn", "Qr"], w=["ps%d" % bs])
                                S.op("act", lambda e, bs=bs, pi=pi, c0=c0: e.activation(out=pT[pi][:, c0:], in_=ps[bs][:, c0:], func=AF.Exp), r=["ps%d" % bs], w=[("pT", pi)])
                                if kb >= 4 * qg:
                                    S.op("pool", lambda e, pi=pi, d_=kb - 4 * qg, c0=c0: e.tensor_tensor(out=pT[pi][:, c0:], in0=pT[pi][:, c0:], in1=mk_b_[:, d_, c0:], op=ALU.mult), r=[("pT", pi), "mk_b"], w=[("pT", pi)])
                                if kb == 0:
                                    S.op("dve", lambda e, pi=pi: e.tensor_copy(out=lacc[:], in_=pT[pi][:]), r=[("pT", pi)], w=["lacc"])
                                else:
                                    S.op("dve", lambda e, pi=pi, c0=c0: e.tensor_tensor(out=lacc[:, c0:], in0=lacc[:, c0:], in1=pT[pi][:, c0:], op=ALU.add), r=[("pT", pi), "lacc"], w=["lacc"])
                            if step >= LA:
                                kb = step - LA
                                pi = kb % 6
                                c0 = max(0, kb - 4 * qg) * 128
                                S.op("pe", lambda e, pi=pi, kb=kb, po=po, nkb=nkb, c0=c0: e.matmul(ps[po][:, c0:], lhsT=Vh[:, kb, :], rhs=pT[pi][:, c0:], start=(kb == 0), stop=(kb == nkb - 1)), r=[("pT", pi), "Vh"], w=["ps%d" % po])
                        S.op("pe", lambda e, pl=pl: e.matmul(ps[pl][:, :], lhsT=ones_f, rhs=lacc[:], start=True, stop=True), r=["lacc"], w=["ps%d" % pl])
                        S.op("dve", lambda e, pl=pl: e.reciprocal(out=rl[:], in_=ps[pl][:, :]), r=["ps%d" % pl], w=["arl"])
                        S.op("dve", lambda e, po=po: e.tensor_tensor(out=osb[:], in0=ps[po][:, :], in1=rl[:], op=ALU.mult), r=["ps%d" % po, "arl"], w=["aosb"])
                        S.op("pool", lambda e, b=b: e.tensor_tensor(out=yc[b][:], in0=osb[:], in1=zcg[b][:], op=ALU.mult), r=["aosb", ("zcg", b)], w=[("yc", b)])
                        S.dma("pool", lambda e, b=b, h=h, t0=t0: [e.dma_start(out=yown1_d[h * 128:(h + 1) * 128, t0:t0 + 512], in_=yc[b][:])], r=[("yc", b)], w=[("yown1_d", h, t0)], slot=("ycst", b))
                S.flush()

        if stop_after >= 5:
            mla_phase()
        if stop_after >= 6:
            S.collective([(lambda e, j=j: e.collective_compute("AllGather", ALU.bypass, replica_groups=RG, ins=[yown1_d[j * 128:(j + 1) * 128, :].opt()], outs=[yall1_d[j * 256:(j + 1) * 256, :].opt()])) for j in range(10)])
            out_phase(wout1_b, 20, x1o_ap, lambda a, b_, t_, w_: x2own_d[a:b_, t_:t_ + w_], True, yall1_d)
            S.collective([lambda e: e.collective_compute("AllGather", ALU.bypass, replica_groups=RG, ins=[ssown_d.opt()], outs=[ssall_d.opt()])])
            final_norm_phase()
        scr = {"hT": hT_d, "yall": yall_d, "yown": yown_d, "yown1": yown1_d, "yall1": yall1_d, "cqn": cqn_d, "lat": lat_d, "zc": zc_d}
        for name in dbg:
            rows = scr[name].shape[0]
            for r0 in range(0, rows, 128):
                r1 = min(rows, r0 + 128)
                S.dma("sp", lambda e, name=name, r0=r0, r1=r1: [e.dma_start(out=dbg[name][r0:r1, :], in_=scr[name][r0:r1, :])], w=[("dbgo", name, r0)], slot="dbgslot")
        S.flush()
    return nc


def _host_inputs(inputs, NTOK):
    f = lambda a: np.ascontiguousarray(np.asarray(a))
    g = lambda k: np.asarray(inputs[k], np.float32)
    colT = lambda v: f(np.asarray(v, np.float32).reshape(-1, 128).T)
    ar = np.arange
    cmat = np.zeros((128, 4, 128), np.float32)
    cmat[:, 0, :] = np.eye(128)
    cmat[:, 1, :] = 1.0
    cmat[:, 2, :] = np.triu(np.ones((128, 128)))
    rot = np.zeros((64, 64), np.float32)
    for i in range(32):
        rot[i + 32, i] = -1.0
        rot[i, i + 32] = 1.0
    cmat[:64, 3, :64] = rot
    amask = np.zeros((128, 4, 512), np.float32)
    for d_ in range(4):
        amask[:, d_, :] = (np.arange(128)[:, None] + d_ * 128 <= np.arange(512)[None, :])
    esel = np.zeros((16, 16, 128), np.float32)
    for h in range(16):
        esel[h, h, :] = 1.0
    inv = (10000.0 ** (-np.arange(32, dtype=np.float32) / 32)).astype(np.float32)
    invf = f(np.concatenate([inv, inv]).reshape(64, 1))
    own0 = [np.concatenate([mm * 512 + ar(512), 1024 + mm * 1024 + ar(1024), 3072 + mm * 256 + ar(256)]) for mm in range(2)]
    own1 = [np.concatenate([mm * 1024 + ar(1024), 2048 + mm * 256 + ar(256)]) for mm in range(2)]
    rows0 = np.concatenate([own0[mm][j * 128:(j + 1) * 128] for j in range(14) for mm in range(2)])
    rows1 = np.concatenate([own1[mm][j * 128:(j + 1) * 128] for j in range(10) for mm in range(2)])
    perm_feat = np.concatenate([mm * 1024 + j * 128 + ar(128) for j in range(8) for mm in range(2)])
    per_m = []
    for m in range(2):
        cols0 = np.concatenate([m * 512 + ar(512), 1024 + m * 512 + ar(512), 2048 + m * 512 + ar(512), 3072 + m * 512 + ar(512),
                                4096 + m * 1024 + ar(1024),
                                6144 + m * 1024 + ar(1024), 6144 + 2048 + m * 256 + ar(256), 6144 + 2560 + m * 256 + ar(256),
                                9216 + m * 16 + ar(16), 9248 + m * 256 + ar(256), 9760 + m * 256 + ar(256)])
        cols1 = np.concatenate([ar(1344), 1344 + m * 1024 + ar(1024), 3392 + m * 256 + ar(256), 3904 + m * 256 + ar(256)])
        xbc_ch = np.concatenate([m * 1024 + ar(1024), 2048 + m * 256 + ar(256), 2560 + m * 256 + ar(256)])
        vec16 = np.zeros((128, 5, 16), np.float32)
        vec16[:, 0, :] = colT(g("mem_norm_w"))
        vec16[:, 1, :] = colT(g("norm0_w"))
        vec16[:, 2, :] = colT(g("norm1_w")[perm_feat])
        vec16[:, 3, :8] = colT(g("final_norm_w")[m * 1024:(m + 1) * 1024])
        vec16[:, 4, :8] = colT(g("ssd_norm_w")[m * 1024:(m + 1) * 1024])
        d = dict(
            w_in0=f(g("w_in0")[:, cols0]), w_out0=f(g("w_out0")[rows0][:, m * 1024:(m + 1) * 1024]), w_in1=f(g("w_in1")[perm_feat][:, cols1]), w_out1=f(g("w_out1")[rows1][:, m * 1024:(m + 1) * 1024]),
            w_uq=f(g("mla_w_uq")[:, m * 1536:(m + 1) * 1536]), w_ukv=f(g("mla_w_ukv")[:, m * 2048:(m + 1) * 2048]),
            mem_k0=f(g("mem_k0")[:, m * 256:(m + 1) * 256]), mem_k1=f(g("mem_k1")[:, m * 256:(m + 1) * 256]),
            mem_v0=f(g("mem_v0")[:, m * 256:(m + 1) * 256]), mem_v1=f(g("mem_v1")[:, m * 256:(m + 1) * 256]),
            vec16=f(vec16),
            dcol=colT(np.repeat(g("ssd_d")[m * 16:(m + 1) * 16], 64)),
            sccw=f(g("sc_conv_w")[:, m * 512:(m + 1) * 512].reshape(3, 4, 128).transpose(2, 1, 0)),
            ssdcw=f(g("ssd_conv_w")[:, xbc_ch].reshape(4, 12, 128).transpose(2, 1, 0)),
            ssdcb=f(g("ssd_conv_b")[xbc_ch].reshape(12, 128).T),
            hvec=f(np.stack([g("ssd_dt_bias")[m * 16:(m + 1) * 16], g("ssd_a_log")[m * 16:(m + 1) * 16]], axis=1)),
            qnw=colT(g("mla_q_norm_w")), kvnw=colT(g("mla_kv_norm_w")),
            cmat=cmat, esel=esel, invf=invf, amask=amask.astype(ml_dtypes.bfloat16))
        per_m.append(d)
    maps = []
    x = np.asarray(inputs["x"])
    mem = np.asarray(inputs["mem"])
    posi = np.asarray(inputs["positions"])
    xT = [f(x[b, :NTOK].T) for b in range(4)]
    mT = [f(mem[b].T) for b in range(4)]
    for c in range(8):
        b, m = c // 2, c % 2
        mp = dict(per_m[m])
        mp["xT"] = xT[b]
        mp["xTo"] = f(xT[b][m * 1024:(m + 1) * 1024])
        mp["memT"] = mT[b]
        mp["pos"] = f(posi[b, :NTOK].reshape(1, NTOK).astype(np.int32))
        maps.append(mp)
    return maps


def kernel(**inputs):
    NTOK = 8192
    nc = build_program(NTOK)
    maps = _host_inputs(inputs, NTOK)
    res = run_bass_kernel_spmd(nc, maps, core_ids=list(range(8)))
    out = np.stack([np.concatenate([np.asarray(res.results[2 * b + m]["outT"]).T for m in range(2)], axis=1) for b in range(4)], axis=0)
    return np.ascontiguousarray(out.astype(np.float32))
```

```python
import os
import numpy as np
import ml_dtypes
from contextlib import ExitStack
import concourse.bass as bass
import concourse.mybir as mybir
from concourse.bass_utils import run_bass_kernel_spmd

F32 = mybir.dt.float32
BF16 = mybir.dt.bfloat16
I32 = mybir.dt.int32
AF = mybir.ActivationFunctionType
ALU = mybir.AluOpType

D = 2048
KT = 16
NMEM = 256
EPS = 1e-6
EVEN_IN = 5136
EVEN_OUT = 3584
EVEN_OWN = 1792
ODD_IN = 2880
ODD_OUT = 2560
ODD_OWN = 1280
C_SCB, C_SCC, C_SCV, C_SCZ = 0, 512, 1024, 1536
C_SSDZ = 2048
C_XBC = 3072
C_DT = 4608
C_MQ0 = 4624
C_MZ0 = 4880
C_CQ, C_CKV, C_KR, C_ZC, C_MQ1, C_MZ1 = 0, 768, 1280, 1344, 2368, 2624

ENGS = ("pe", "act", "dve", "pool", "sp")


class _Op:
    __slots__ = ("eng", "fn", "deps", "is_dma", "slot", "dma_val", "sig", "needed", "prev_dma", "ndma")

    def __init__(self, eng, fn, is_dma=False, slot=None, ndma=1):
        self.eng = eng
        self.fn = fn
        self.deps = []
        self.is_dma = is_dma
        self.slot = slot
        self.dma_val = 0
        self.sig = 0
        self.needed = False
        self.prev_dma = None
        self.ndma = ndma


class Sched:
    def __init__(self, nc, es):
        self.nc = nc
        self.es = es
        self.q = {e: [] for e in ENGS}
        self.last_w = {}
        self.readers = {}
        self.slot_last = {}
        self.slot_tot = {}
        self.slot_sem = {}
        self.sigcnt = {e: 0 for e in ENGS}
        self.esem = {e: es.enter_context(nc.semaphore("s_" + e)) for e in ENGS}
        self.waited = {e: {} for e in ENGS}
        self.nops = 0
        self.cc_sem = es.enter_context(nc.semaphore("s_cc"))
        self.cc_cnt = 0
        self.log = None
        self.cc_scratch = es.enter_context(nc.sbuf_tensor("cc_scratch", [128, 4], F32))

    def collective(self, fns):
        self.flush()
        sem = self.cc_sem
        base = self.cc_cnt
        self.cc_cnt += len(fns)
        with self.nc.Block() as block:
            @block.gpsimd
            def _(e):
                for i, fn in enumerate(fns):
                    fn(e).then_inc(sem, 1)
                    e.wait_ge(sem, base + i + 1)
        scr = self.cc_scratch
        self.op("pool", lambda e: e.memset(scr[:], 0.0), w=["cc_scratch"])
        self.flush()

    def sb(self, es, name, shape, dtype):
        self.nsb = getattr(self, "nsb", 0) + 1
        return es.enter_context(self.nc.sbuf_tensor("%s_%d" % (name, self.nsb), list(shape), dtype))

    def _track(self, o, r, w):
        deps = []
        for k in r:
            d = self.last_w.get(k)
            if d is not None:
                deps.append(d)
        for k in w:
            d = self.last_w.get(k)
            if d is not None:
                deps.append(d)
            deps.extend(self.readers.get(k, ()))
        for k in r:
            self.readers.setdefault(k, []).append(o)
        for k in w:
            self.last_w[k] = o
            self.readers[k] = []
        seen = set()
        for d in deps:
            if d is o or id(d) in seen:
                continue
            seen.add(id(d))
            o.deps.append(d)

    def op(self, eng, fn, r=(), w=()):
        o = _Op(eng, fn)
        self._track(o, r, w)
        self.q[eng].append(o)
        self.nops += 1
        return o

    def dma(self, eng, fn, r=(), w=(), slot=None, n=1):
        if slot is None:
            slot = w[0] if len(w) else r[0]
        o = _Op(eng, fn, is_dma=True, slot=slot, ndma=n)
        self._track(o, r, w)
        o.prev_dma = self.slot_last.get(slot)
        self.slot_last[slot] = o
        self.slot_tot[slot] = self.slot_tot.get(slot, 0) + 16 * n
        o.dma_val = self.slot_tot[slot]
        if slot not in self.slot_sem:
            self.slot_sem[slot] = self.es.enter_context(self.nc.semaphore("d%d" % len(self.slot_sem)))
        self.q[eng].append(o)
        self.nops += 1
        return o

    def flush(self):
        nc = self.nc
        lasts = [self.q[e][-1] for e in ENGS if self.q[e]]
        dmas = [o for o in self.slot_last.values()]
        for e in ENGS:
            b = _Op(e, None)
            b.deps = [d for d in lasts if d.eng != e or d.is_dma] + [d for d in dmas]
            self.q[e].append(b)
        for e in ENGS:
            for o in self.q[e]:
                for d in o.deps:
                    if d.is_dma:
                        continue
                    if d.eng == o.eng and d.eng == "pe":
                        continue
                    if d.sig == 0:
                        d.needed = True
        for e in ENGS:
            for o in self.q[e]:
                if o.needed and o.sig == 0:
                    self.sigcnt[e] += 1
                    o.sig = self.sigcnt[e]
        sched = self

        def emit(eng, e):
            waited = sched.waited[eng]

            def wait(sem, val, key):
                if waited.get(key, 0) >= val:
                    return
                waited[key] = val
                if sched.log is not None:
                    sched.log.append((eng, "W", key, val))
                e.wait_ge(sem, val)

            for o in sched.q[eng]:
                for d in o.deps:
                    if d.is_dma:
                        wait(sched.slot_sem[d.slot], d.dma_val, ("d", d.slot))
                    else:
                        if d.eng == eng and eng == "pe":
                            continue
                        wait(sched.esem[d.eng], d.sig, ("e", d.eng))
                if o.fn is None:
                    continue
                if sched.log is not None:
                    sched.log.append((eng, "DMA" if o.is_dma else "OP", o.slot, o.dma_val if o.is_dma else o.sig, o.needed))
                if o.is_dma:
                    if o.prev_dma is not None:
                        wait(sched.slot_sem[o.slot], o.prev_dma.dma_val, ("d", o.slot))
                    ins = o.fn(e)
                    assert len(ins) == o.ndma, (len(ins), o.ndma)
                    for i in ins:
                        i.then_inc(sched.slot_sem[o.slot], 16)
                else:
                    ins = o.fn(e)
                    if o.needed:
                        ins.then_inc(sched.esem[eng], 1)

        with nc.Block() as block:
            @block.tensor
            def _(e):
                emit("pe", e)

            @block.scalar
            def _(e):
                emit("act", e)

            @block.vector
            def _(e):
                emit("dve", e)

            @block.gpsimd
            def _(e):
                emit("pool", e)

            @block.sync
            def _(e):
                emit("sp", e)

        for e in ENGS:
            self.q[e] = []
        self.last_w = {}
        self.readers = {}
        self.slot_last = {}


class K:
    pass


def build_program(NTOK, debug=(), stop_after=99):
    nc = bass.Bass("TRN2", target_bir_lowering=False)
    NG = NTOK // 512
    NCH = NTOK // 128

    def din(name, shape, dt=F32):
        return nc.dram_tensor(name, list(shape), dt, kind="ExternalInput").ap()

    def dscr(name, shape, dt=BF16):
        return nc.dram_tensor(name, list(shape), dt).ap()

    xT = din("xT", [D, NTOK])
    memT = din("memT", [D, NMEM])
    pos = din("pos", [1, NTOK], I32)
    w_in0 = din("w_in0", [D, EVEN_IN])
    w_out0 = din("w_out0", [EVEN_OUT, 1024])
    xTo = din("xTo", [1024, NTOK])
    w_in1 = din("w_in1", [D, ODD_IN])
    w_out1 = din("w_out1", [ODD_OUT, 1024])
    w_uq = din("w_uq", [768, 1536])
    w_ukv = din("w_ukv", [512, 2048])
    mk = [din("mem_k0", [D, 256]), din("mem_k1", [D, 256])]
    mv = [din("mem_v0", [D, 256]), din("mem_v1", [D, 256])]
    vec16 = din("vec16", [128, 5, 16])
    dcol = din("dcol", [128, 8])
    sccw = din("sccw", [128, 4, 3])
    ssdcw = din("ssdcw", [128, 12, 4])
    ssdcb = din("ssdcb", [128, 12])
    hvec = din("hvec", [16, 2])
    qnw = din("qnw", [128, 6])
    kvnw = din("kvnw", [128, 4])
    cmat = din("cmat", [128, 4, 128])
    esel = din("esel", [16, 16, 128])
    invf = din("invf", [64, 1])
    amask = din("amask", [128, 4, 512], BF16)

    outT = nc.dram_tensor("outT", [1024, NTOK], F32, kind="ExternalOutput").ap()
    H2 = NTOK // 2

    win0_b = dscr("win0_b", [D, EVEN_IN])
    wout0_b = dscr("wout0_b", [EVEN_OUT, 1024])
    win1_b = dscr("win1_b", [D, ODD_IN])
    wout1_b = dscr("wout1_b", [ODD_OUT, 1024])
    wuq_b = dscr("wuq_b", [768, 1536])
    wukv_b = dscr("wukv_b", [512, 2048])
    mk_b = [dscr("mk0_b", [D, 256]), dscr("mk1_b", [D, 256])]
    mv_b = [dscr("mv0_b", [D, 256]), dscr("mv1_b", [D, 256])]
    hT_d = dscr("hT_d", [D, NTOK])
    yall_d = dscr("yall_d", [EVEN_OUT, NTOK])
    yall1_d = dscr("yall1_d", [ODD_OUT, NTOK])
    yown_d = dscr("yown_d", [EVEN_OWN, NTOK])
    yown1_d = dscr("yown1_d", [ODD_OWN, NTOK])
    x1T_d = dscr("x1T_d", [2, D, H2], F32)
    x1own_d = dscr("x1own_d", [2, 1024, H2], F32)
    x2own_d = dscr("x2own_d", [1024, NTOK], F32)
    ssown_d = dscr("ssown_d", [1, NTOK], F32)
    ssall_d = dscr("ssall_d", [2, NTOK], F32)

    def x1g_ap(r0, r1, t0, tw):
        return x1T_d[t0 // H2, r0:r1, (t0 % H2):(t0 % H2) + tw]

    def x1o_ap(r0, r1, t0, tw):
        return x1own_d[t0 // H2, r0:r1, (t0 % H2):(t0 % H2) + tw]
    cqn_d = dscr("cqn_d", [768, NTOK])
    lat_d = dscr("lat_d", [576, NTOK])
    zc_d = dscr("zc_d", [1024, NTOK])
    cs_d = dscr("cs_d", [2, 64, NTOK], F32)

    dbg = {}
    dbgx = {}
    if "sc_dbg" in [d[0] for d in debug]:
        debug = [d for d in debug if d[0] != "sc_dbg"]
        dbgx["sc_dbg"] = True
        for nm, shp in (("work0", [128, 514]), ("work1", [128, 514]), ("acc", [128, 512]), ("acc2", [128, 512]), ("zs", [128, 512]), ("csb", [128, 512]), ("carry", [128, 8, 2]), ("cw", [128, 8, 3])):
            dbgx[nm] = nc.dram_tensor("dbgx_" + nm, shp, F32, kind="ExternalOutput").ap()
    for name, shape, dt in debug:
        dbg[name] = nc.dram_tensor("dbg_" + name, list(shape), dt, kind="ExternalOutput").ap()

    with ExitStack() as es:
        S = Sched(nc, es)
        ps = [es.enter_context(nc.psum_tensor("ps%d" % i, [128, 512], F32)) for i in range(8)]
        cm_f = S.sb(es, "cm_f", [128, 4, 128], F32)
        cm_b = S.sb(es, "cm_b", [128, 4, 128], BF16)
        v16 = S.sb(es, "v16", [128, 5, 16], F32)
        S.dma("sp", lambda e: [e.dma_start(out=cm_f[:], in_=cmat[:, :, :])], w=["cm_f"])
        S.dma("sp", lambda e: [e.dma_start(out=v16[:], in_=vec16[:, :, :])], w=["v16"])
        S.op("dve", lambda e: e.tensor_copy(out=cm_b[:], in_=cm_f[:]), r=["cm_f"], w=["cm_b"])
        ident_f, ones_f, tri_f = cm_f[:, 0, :], cm_f[:, 1, :], cm_f[:, 2, :]
        ident_b, ones_b, tri_b = cm_b[:, 0, :], cm_b[:, 1, :], cm_b[:, 2, :]
        CONST = ["cm_f", "cm_b", "v16"]

        CW = 2568

        def make_cast(pes, ld_eng="sp"):
            fbuf = [S.sb(pes, "cf%d" % i, [128, CW], F32) for i in range(3)]
            bbuf = [S.sb(pes, "cb%d" % i, [128, CW], BF16) for i in range(3)]
            it = [0]

            def cast_item(dst, src, r0, c0, cw):
                i = it[0] % 3
                it[0] += 1
                fb, bb = fbuf[i], bbuf[i]
                S.dma(ld_eng, lambda e: [e.dma_start(out=fb[:, :cw], in_=src[r0:r0 + 128, c0:c0 + cw])], w=[("cf", i)])
                eng = ("dve", "act", "pool")[it[0] % 3]
                if eng == "act":
                    S.op("act", lambda e: e.activation(out=bb[:, :cw], in_=fb[:, :cw], func=AF.Copy), r=[("cf", i)], w=[("cb", i)])
                else:
                    S.op(eng, lambda e: e.tensor_copy(out=bb[:, :cw], in_=fb[:, :cw]), r=[("cf", i)], w=[("cb", i)])
                S.dma("pool", lambda e: [e.dma_start(out=dst[r0:r0 + 128, c0:c0 + cw], in_=bb[:, :cw])], r=[("cb", i)], w=[("wdram", id(dst), r0, c0)], slot=("cbst", i))
            return cast_item

        def cast_items(dst, src, rows, cols):
            return [(dst, src, r0, c0, min(CW, cols - c0)) for r0 in range(0, rows, 128) for c0 in range(0, cols, CW)]

        items_a = cast_items(win0_b, w_in0, D, EVEN_IN)
        for l in range(2):
            items_a += cast_items(mk_b[l], mk[l], D, 256) + cast_items(mv_b[l], mv[l], D, 256)
        items_b = (cast_items(wout0_b, w_out0, EVEN_OUT, 1024) + cast_items(win1_b, w_in1, D, ODD_IN) + cast_items(wout1_b, w_out1, ODD_OUT, 1024)
                   + cast_items(wuq_b, w_uq, 768, 1536) + cast_items(wukv_b, w_ukv, 512, 2048))

        def cast_phase():
            with ExitStack() as pes:
                ci = make_cast(pes)
                for item in items_a:
                    ci(*item)
                S.flush()

        cast_phase()

        def rstd_from_ss(ss_ps, n, out_ap, tmp_ap, keys_r, key_tmp, key_out):
            S.op("act", lambda e: e.activation(out=tmp_ap, in_=ss_ps, func=AF.Sqrt, scale=1.0 / n, bias=EPS), r=keys_r, w=[key_tmp])
            S.op("dve", lambda e: e.reciprocal(out=out_ap, in_=tmp_ap), r=[key_tmp], w=[key_out])

        def wload(wt, wkey, wb, kts, c0, width):
            S.dma("sp", lambda e: [e.dma_start(out=wt[:, kt, :width], in_=wb[kt * 128:(kt + 1) * 128, c0:c0 + width]) for kt in range(kts)],
                  w=[wkey], n=kts)

        def ld_kt(tile_, dr, r0, t0, nk, wkey, tw=512, eng="sp", r=()):
            src_ = dr if callable(dr) else (lambda a, b_, t_, w_: dr[a:b_, t_:t_ + w_])
            S.dma(eng, lambda e: [e.dma_start(out=tile_[:, kt, :], in_=src_(r0 + kt * 128, r0 + (kt + 1) * 128, t0, tw)) for kt in range(nk)], r=list(r), w=[wkey], n=nk)

        def st_kt(tile_, dr, r0, t0, nk, rkey, wkey, slot, tw=512, eng="pool"):
            S.dma(eng, lambda e: [e.dma_start(out=dr[r0 + kt * 128:r0 + (kt + 1) * 128, t0:t0 + tw], in_=tile_[:, kt, :]) for kt in range(nk)], r=[rkey], w=[(wkey, r0, t0)], slot=slot, n=nk)

        kmT = [S.sb(es, "kmT%d" % l, [128, 2, NMEM], BF16) for l in range(2)]
        vm = [S.sb(es, "vm%d" % l, [128, 2, 256], BF16) for l in range(2)]

        def mem_phase():
            with ExitStack() as pes:
                mt = S.sb(pes, "mt", [128, KT, NMEM], F32)
                sq = S.sb(pes, "msq", [128, NMEM], F32)
                rt = S.sb(pes, "mrt", [128, NMEM], F32)
                rs = S.sb(pes, "mrs", [128, NMEM], F32)
                mn = S.sb(pes, "mn", [128, KT, NMEM], BF16)
                wt = [S.sb(pes, "mw%d" % i, [128, KT, 256], BF16) for i in range(2)]
                ld_kt(mt, memT, 0, 0, KT, "mt", tw=NMEM)
                for kt in range(KT):
                    S.op("act", lambda e, kt=kt: e.activation(out=sq[:], in_=mt[:, kt, :], func=AF.Square), r=["mt"], w=["msq"])
                    S.op("pe", lambda e, kt=kt: e.matmul(ps[0][:, :NMEM], lhsT=ones_f, rhs=sq[:], start=(kt == 0), stop=(kt == KT - 1)), r=["msq", "cm_f"], w=["ps0"])
                rstd_from_ss(ps[0][:, :NMEM], D, rs[:], rt[:], ["ps0"], "mrt", "mrs")
                for kt in range(KT):
                    S.op("dve", lambda e, kt=kt: e.scalar_tensor_tensor(out=mn[:, kt, :], in0=mt[:, kt, :], scalar=v16[:, 0, kt:kt + 1], in1=rs[:], op0=ALU.mult, op1=ALU.mult),
                         r=["mt", "mrs", "v16"], w=["mn"])
                for l in range(2):
                    wload(wt[0], "mw0", mk_b[l], KT, 0, 256)
                    wload(wt[1], "mw1", mv_b[l], KT, 0, 256)
                    for h in range(2):
                        def f(e, h=h):
                            for kt in range(KT):
                                ins = e.matmul(ps[1][:, :NMEM], lhsT=wt[0][:, kt, h * 128:(h + 1) * 128], rhs=mn[:, kt, :], start=(kt == 0), stop=(kt == KT - 1))
                            return ins
                        S.op("pe", f, r=["mw0", "mn"], w=["ps1"])
                        S.op("dve", lambda e, h=h, l=l: e.tensor_copy(out=kmT[l][:, h, :], in_=ps[1][:, :NMEM]), r=["ps1"], w=["kmT%d" % l])
                    for mb in range(2):
                        def f(e, mb=mb):
                            for kt in range(KT):
                                ins = e.matmul(ps[2][:, 0:256], lhsT=mn[:, kt, mb * 128:(mb + 1) * 128], rhs=wt[1][:, kt, :], start=(kt == 0), stop=(kt == KT - 1))
                            return ins
                        S.op("pe", f, r=["mw1", "mn"], w=["ps2"])
                        S.op("dve", lambda e, mb=mb, l=l: e.tensor_copy(out=vm[l][:, mb, :], in_=ps[2][:, 0:256]), r=["ps2"], w=["vm%d" % l])
                S.flush()

        mem_phase()

        def norm_phase(src, which):
            with ExitStack() as pes:
                xt = [S.sb(pes, "nx%d" % i, [128, KT, 512], F32) for i in range(2)]
                sq = [S.sb(pes, "nsq%d" % i, [128, 512], BF16) for i in range(2)]
                rt = S.sb(pes, "nrt", [128, 512], F32)
                rs = S.sb(pes, "nrs", [128, 512], F32)
                ht = [S.sb(pes, "nh%d" % i, [128, KT, 512], BF16) for i in range(2)]
                for g in range(NG):
                    b = g % 2
                    t0 = g * 512
                    ld_kt(xt[b], src, 0, t0, KT, ("nx", b))
                    for kt in range(KT):
                        sb_ = kt % 2
                        S.op("act", lambda e, b=b, kt=kt, sb_=sb_: e.activation(out=sq[sb_][:], in_=xt[b][:, kt, :], func=AF.Square), r=[("nx", b)], w=[("nsq", sb_)])
                        S.op("pe", lambda e, kt=kt, sb_=sb_: e.matmul(ps[0][:, :], lhsT=ones_b, rhs=sq[sb_][:], start=(kt == 0), stop=(kt == KT - 1)), r=[("nsq", sb_), "cm_b"], w=["ps0"])
                    rstd_from_ss(ps[0][:, :], D, rs[:], rt[:], ["ps0"], "nrt", "nrs")
                    for kt in range(KT):
                        S.op("dve", lambda e, b=b, kt=kt: e.scalar_tensor_tensor(out=ht[b][:, kt, :], in0=xt[b][:, kt, :], scalar=v16[:, which, kt:kt + 1], in1=rs[:], op0=ALU.mult, op1=ALU.mult),
                             r=[("nx", b), "nrs", "v16"], w=[("nh", b)])
                    st_kt(ht[b], hT_d, 0, t0, KT, ("nh", b), "hT_d", ("nhst", b))
                S.flush()

        norm_phase(xT, 1)


        TWO_PI = 6.283185307179586
        PI = 3.141592653589793
        bank = [0]

        def nb():
            bank[0] = (bank[0] + 1) % 4
            return bank[0]

        def mm_acc(psap, pskey, lhs_fn, rhs_fn, kts, rkeys):
            def f(e):
                for kt in range(kts):
                    ins = e.matmul(psap, lhsT=lhs_fn(kt), rhs=rhs_fn(kt), start=(kt == 0), stop=(kt == kts - 1))
                return ins
            S.op("pe", f, r=rkeys, w=[pskey])

        def load_ht(ht, b, t0):
            ld_kt(ht[b], hT_d, 0, t0, KT, ("ht", b))

        def mem_attn_group(l, ht, b, wts, wb, cq, cz, ym, ymkey, tmp):
            qs, pT, zs, rl, osb = tmp
            for half in range(1):
                wload(wts[half], ("mw", half), wb, KT, cq + half * 256, 256)
                wload(wts[2 + half], ("mw", 2 + half), wb, KT, cz + half * 256, 256)
            for h in range(2):
                wq = wts[h // 2]
                wz = wts[2 + h // 2]
                co = (h % 2) * 128
                bq = nb()
                mm_acc(ps[bq][:, :], "ps%d" % bq, lambda kt, wq=wq, co=co: wq[:, kt, co:co + 128], lambda kt: ht[b][:, kt, :], KT, [("mw", h // 2), ("ht", b)])
                S.op("act", lambda e, bq=bq: e.activation(out=qs[:], in_=ps[bq][:, :], func=AF.Copy, scale=128.0 ** -0.5), r=["ps%d" % bq], w=["mqs"])
                for mb in range(2):
                    bs = nb()
                    S.op("pe", lambda e, bs=bs, mb=mb, h=h: e.matmul(ps[bs][:, :], lhsT=kmT[l][:, h, mb * 128:(mb + 1) * 128], rhs=qs[:], start=True, stop=True),
                         r=["mqs", "kmT%d" % l], w=["ps%d" % bs])
                    S.op("act", lambda e, bs=bs, mb=mb: e.activation(out=pT[mb][:], in_=ps[bs][:, :], func=AF.Exp), r=["ps%d" % bs], w=[("mpT", mb)])
                for mb in range(2):
                    S.op("pe", lambda e, mb=mb, h=h: e.matmul(ps[4][:, :], lhsT=vm[l][:, mb, h * 128:(h + 1) * 128], rhs=pT[mb][:], start=(mb == 0), stop=(mb == 1)),
                         r=[("mpT", mb), "vm%d" % l], w=["ps4"])
                for mb in range(2):
                    S.op("pe", lambda e, mb=mb: e.matmul(ps[5][:, :], lhsT=ones_b, rhs=pT[mb][:], start=(mb == 0), stop=(mb == 1)),
                         r=[("mpT", mb), "cm_b"], w=["ps5"])
                bz = nb()
                mm_acc(ps[bz][:, :], "ps%d" % bz, lambda kt, wz=wz, co=co: wz[:, kt, co:co + 128], lambda kt: ht[b][:, kt, :], KT, [("mw", 2 + h // 2), ("ht", b)])
                S.op("act", lambda e, bz=bz: e.activation(out=zs[:], in_=ps[bz][:, :], func=AF.Silu), r=["ps%d" % bz], w=["mzs"])
                S.op("dve", lambda e: e.reciprocal(out=rl[:], in_=ps[5][:, :]), r=["ps5"], w=["mrl"])
                S.op("dve", lambda e: e.tensor_tensor(out=osb[:], in0=ps[4][:, :], in1=rl[:], op=ALU.mult), r=["ps4", "mrl"], w=["mosb"])
                S.op("pool", lambda e, h=h: e.tensor_tensor(out=ym[:, h, :], in0=osb[:], in1=zs[:], op=ALU.mult), r=["mosb", "mzs"], w=[ymkey])

        def mem_tmp(pes):
            return (S.sb(pes, "mqs", [128, 512], BF16), [S.sb(pes, "mpT%d" % i, [128, 512], BF16) for i in range(2)],
                    S.sb(pes, "mzs", [128, 512], F32), S.sb(pes, "mrl", [128, 512], F32), S.sb(pes, "mosb", [128, 512], F32))

        def sc_mem_phase():
            with ExitStack() as pes:
                ht = [S.sb(pes, "ht%d" % i, [128, KT, 512], BF16) for i in range(2)]
                wts = [S.sb(pes, "wt%d" % i, [128, KT, 256], BF16) for i in range(4)]
                csb = S.sb(pes, "csb", [128, 512], F32)
                work = [S.sb(pes, "work%d" % i, [128, 514], F32) for i in range(2)]
                acc = S.sb(pes, "acc", [128, 512], F32)
                acc2 = S.sb(pes, "acc2", [128, 512], F32)
                zs = S.sb(pes, "zs", [128, 512], F32)
                carry = S.sb(pes, "carry", [128, 4, 2], F32)
                cw = S.sb(pes, "cw", [128, 4, 3], F32)
                ya = [S.sb(pes, "ya%d" % i, [128, 4, 512], BF16) for i in range(2)]
                ym = [S.sb(pes, "ym%d" % i, [128, 2, 512], BF16) for i in range(2)]
                mtmp = mem_tmp(pes)
                ci = make_cast(pes, ld_eng="act")
                per_g = (len(items_b) + NG - 1) // NG
                S.dma("sp", lambda e: [e.dma_start(out=cw[:], in_=sccw[:, :, :])], w=["cw"])
                S.op("pool", lambda e: e.memset(carry[:], 0.0), w=["carry"])
                for g in range(NG):
                    b = g % 2
                    t0 = g * 512
                    load_ht(ht, b, t0)
                    for j in range(4):
                        jj, co = j // 2, (j % 2) * 128
                        if j % 2 == 0:
                            for fi, c_off in enumerate((C_SCC, C_SCV, C_SCB, C_SCZ)):
                                wload(wts[fi], ("mw", fi), win0_b, KT, c_off + jj * 256, 256)
                        wk = work[j % 2]
                        wkk = ("work", j % 2)
                        bc, bv = nb(), nb()
                        mm_acc(ps[bc][:, :], "ps%d" % bc, lambda kt, co=co: wts[0][:, kt, co:co + 128], lambda kt, b=b: ht[b][:, kt, :], KT, [("mw", 0), ("ht", b)])
                        mm_acc(ps[bv][:, :], "ps%d" % bv, lambda kt, co=co: wts[1][:, kt, co:co + 128], lambda kt, b=b: ht[b][:, kt, :], KT, [("mw", 1), ("ht", b)])
                        S.op("act", lambda e, bc=bc: e.activation(out=csb[:], in_=ps[bc][:, :], func=AF.Copy), r=["ps%d" % bc], w=["csb"])
                        S.op("pool", lambda e, wk=wk, j=j: e.tensor_copy(out=wk[:, 0:2], in_=carry[:, j, :]), r=["carry"], w=[wkk])
                        S.op("dve", lambda e, wk=wk, bv=bv: e.tensor_tensor(out=wk[:, 2:514], in0=csb[:], in1=ps[bv][:, :], op=ALU.mult), r=["csb", "ps%d" % bv, wkk], w=[wkk])
                        S.op("pool", lambda e, wk=wk, j=j: e.tensor_copy(out=carry[:, j, :], in_=wk[:, 512:514]), r=[wkk], w=["carry"])
                        S.op("dve", lambda e, wk=wk, j=j: e.tensor_scalar(out=acc[:], in0=wk[:, 2:514], scalar1=cw[:, j, 2:3], scalar2=None, op0=ALU.mult), r=[wkk, "cw"], w=["acc"])
                        S.op("dve", lambda e, wk=wk, j=j: e.scalar_tensor_tensor(out=acc[:], in0=wk[:, 1:513], scalar=cw[:, j, 1:2], in1=acc[:], op0=ALU.mult, op1=ALU.add), r=[wkk, "cw", "acc"], w=["acc"])
                        S.op("dve", lambda e, wk=wk, j=j: e.scalar_tensor_tensor(out=acc[:], in0=wk[:, 0:512], scalar=cw[:, j, 0:1], in1=acc[:], op0=ALU.mult, op1=ALU.add), r=[wkk, "cw", "acc"], w=["acc"])
                        bb, bz = nb(), nb()
                        mm_acc(ps[bb][:, :], "ps%d" % bb, lambda kt, co=co: wts[2][:, kt, co:co + 128], lambda kt, b=b: ht[b][:, kt, :], KT, [("mw", 2), ("ht", b)])
                        mm_acc(ps[bz][:, :], "ps%d" % bz, lambda kt, co=co: wts[3][:, kt, co:co + 128], lambda kt, b=b: ht[b][:, kt, :], KT, [("mw", 3), ("ht", b)])
                        S.op("act", lambda e, bz=bz: e.activation(out=zs[:], in_=ps[bz][:, :], func=AF.Silu), r=["ps%d" % bz], w=["zs"])
                        S.op("dve", lambda e, bb=bb: e.tensor_tensor(out=acc2[:], in0=acc[:], in1=ps[bb][:, :], op=ALU.mult), r=["acc", "ps%d" % bb], w=["acc2"])
                        S.op("pool", lambda e, j=j, b=b: e.tensor_tensor(out=ya[b][:, j, :], in0=acc2[:], in1=zs[:], op=ALU.mult), r=["acc2", "zs"], w=[("ya", b)])
                    st_kt(ya[b], yown_d, 0, t0, 4, ("ya", b), "yown_d", ("yast", b))
                    for item in items_b[g * per_g:(g + 1) * per_g]:
                        ci(*item)
                    mem_attn_group(0, ht, b, wts, win0_b, C_MQ0, C_MZ0, ym[b], ("ym", b), mtmp)
                    st_kt(ym[b], yown_d, 1536, t0, 2, ("ym", b), "yown_d", ("ymst", b))
                S.flush()

        sc_mem_phase()

        def ssd_phase():
            with ExitStack() as pes:
                ht = [S.sb(pes, "sht", [128, KT, 512], BF16)]
                wts = [S.sb(pes, "swt%d" % i, [128, KT, 256], BF16) for i in range(3)]
                zs_t = S.sb(pes, "zs_t", [128, 8, 512], BF16)
                xs_t = S.sb(pes, "xs_t", [128, 8, 512], BF16)
                bm_t = S.sb(pes, "bm_t", [128, 2, 512], BF16)
                cm_t = S.sb(pes, "cm_t", [128, 2, 512], BF16)
                yb_t = S.sb(pes, "yb_t", [128, 8, 512], BF16)
                dtT = S.sb(pes, "dtT", [16, 512], F32)
                dte_ = S.sb(pes, "dtexp", [16, 512], F32)
                dtaT = S.sb(pes, "dtaT", [16, 512], F32)
                acT = S.sb(pes, "acT", [16, 512], F32)
                ones32 = S.sb(pes, "ones32", [16, 128], F32)
                hv = S.sb(pes, "hv", [16, 2], F32)
                aneg = S.sb(pes, "aneg", [16, 1], F32)
                es_sel = S.sb(pes, "es_sel", [16, 16, 128], F32)
                xcar = S.sb(pes, "xcar", [128, 12, 3], F32)
                xcw = S.sb(pes, "xcw", [128, 12, 4], F32)
                xcb = S.sb(pes, "xcb", [128, 12], F32)
                dc = S.sb(pes, "dc", [128, 8], F32)
                wk2 = [S.sb(pes, "wk2_%d" % i, [128, 515], F32) for i in range(2)]
                acc3 = S.sb(pes, "acc3", [128, 512], F32)
                tok32 = S.sb(pes, "tok32", [128, 32], F32)
                xs_tok = S.sb(pes, "xs_tok", [128, 16, 64], BF16)
                bm_tok = S.sb(pes, "bm_tok", [128, 256], BF16)
                seg = [S.sb(pes, "seg%d" % i_, [128, 8, 128], F32) for i_ in range(2)]
                es_ = [S.sb(pes, "es%d" % i_, [128, 8, 128], F32) for i_ in range(2)]
                ea_ = [S.sb(pes, "ea%d" % i_, [128, 8, 128], F32) for i_ in range(2)]
                cbm = [S.sb(pes, "cbm%d" % i_, [128, 128], F32) for i_ in range(2)]
                MT = [S.sb(pes, "MT%d" % i_, [128, 8, 128], BF16) for i_ in range(2)]
                ChT = [S.sb(pes, "ChT%d" % i_, [128, 8, 128], BF16) for i_ in range(2)]
                t8 = [S.sb(pes, "t8%d" % i_, [128, 8], F32) for i_ in range(2)]
                dte8 = [S.sb(pes, "dte8%d" % i_, [128, 8], F32) for i_ in range(2)]
                cd8 = [S.sb(pes, "cd8%d" % i_, [128, 8], F32) for i_ in range(2)]
                xdt_pad = [S.sb(pes, "xdt_pad%d" % i_, [128, 8, 128], BF16) for i_ in range(2)]
                xdt_f = [S.sb(pes, "xdt_f%d" % i_, [128, 8, 64], F32) for i_ in range(2)]
                xdt_end = [S.sb(pes, "xdt_end%d" % i_, [128, 8, 64], BF16) for i_ in range(2)]
                S_f = S.sb(pes, "S_f", [128, 16, 64], F32)
                S_pad = [S.sb(pes, "S_pad%d" % i_, [128, 8, 128], BF16) for i_ in range(2)]
                yv = [S.sb(pes, "yv%d" % i_, [128, 4, 128], F32) for i_ in range(2)]
                yz = [S.sb(pes, "yz%d" % i_, [128, 4, 128], F32) for i_ in range(2)]
                sq4 = [S.sb(pes, "sq4%d" % i_, [128, 4, 128], BF16) for i_ in range(2)]
                rt4 = [S.sb(pes, "rt4%d" % i_, [128, 128], F32) for i_ in range(2)]
                rs4 = [S.sb(pes, "rs4%d" % i_, [128, 128], F32) for i_ in range(2)]
                for t_, src in ((hv, hvec), (es_sel, esel), (xcw, ssdcw), (xcb, ssdcb), (dc, dcol)):
                    S.dma("sp", lambda e, t_=t_, src=src: [e.dma_start(out=t_[:], in_=src)], w=[("c", id(t_))])
                S.op("pool", lambda e: e.memset(xcar[:], 0.0), w=["xcar"])
                S.op("pool", lambda e: e.memset(S_f[:], 0.0), w=[("S_f", 0), ("S_f", 1)])
                for i_ in range(2):
                    S.op("pool", lambda e, i_=i_: e.memset(xdt_pad[i_][:], 0.0), w=[("xdt_pad", i_)])
                    S.op("pool", lambda e, i_=i_: e.memset(S_pad[i_][:], 0.0), w=[("S_pad", i_)])
                S.op("pool", lambda e: e.memset(ones32[:], 1.0), w=["ones32"])
                S.op("act", lambda e: e.activation(out=aneg[:], in_=hv[:, 1:2], func=AF.Exp), r=[("c", id(hv))], w=["aneg0"])
                S.op("dve", lambda e: e.tensor_scalar(out=aneg[:], in0=aneg[:], scalar1=-1.0, scalar2=None, op0=ALU.mult), r=["aneg0"], w=["aneg"])
                S.flush()
                psT = ps[6][:].bitcast(BF16)
                for g in range(NG):
                    t0 = g * 512
                    ld_kt(ht[0], hT_d, 0, t0, KT, ("ht", 0))
                    wi = 0
                    for t in range(8):
                        if t % 2 == 0:
                            wi = (wi + 1) % 3
                            wload(wts[wi], ("sw", wi), win0_b, KT, C_SSDZ + t * 128, 256)
                        co = (t % 2) * 128
                        bz = nb()
                        mm_acc(ps[bz][:, :], "ps%d" % bz, lambda kt, wi=wi, co=co: wts[wi][:, kt, co:co + 128], lambda kt: ht[0][:, kt, :], KT, [("sw", wi), ("ht", 0)])
                        S.op("act", lambda e, bz=bz, t=t: e.activation(out=zs_t[:, t, :], in_=ps[bz][:, :], func=AF.Silu), r=["ps%d" % bz], w=["zs_t"])
                    for t in range(12):
                        if t % 2 == 0:
                            wi = (wi + 1) % 3
                            wload(wts[wi], ("sw", wi), win0_b, KT, C_XBC + t * 128, 256)
                        co = (t % 2) * 128
                        bx = nb()
                        wk = wk2[t % 2]
                        wkk = ("wk2", t % 2)
                        mm_acc(ps[bx][:, :], "ps%d" % bx, lambda kt, wi=wi, co=co: wts[wi][:, kt, co:co + 128], lambda kt: ht[0][:, kt, :], KT, [("sw", wi), ("ht", 0)])
                        S.op("pool", lambda e, wk=wk, t=t: e.tensor_copy(out=wk[:, 0:3], in_=xcar[:, t, :]), r=["xcar"], w=[wkk])
                        S.op("act", lambda e, wk=wk, bx=bx: e.activation(out=wk[:, 3:515], in_=ps[bx][:, :], func=AF.Copy), r=["ps%d" % bx, wkk], w=[wkk])
                        S.op("pool", lambda e, wk=wk, t=t: e.tensor_copy(out=xcar[:, t, :], in_=wk[:, 512:515]), r=[wkk], w=["xcar"])
                        S.op("dve", lambda e, wk=wk, t=t: e.tensor_scalar(out=acc3[:], in0=wk[:, 3:515], scalar1=xcw[:, t, 3:4], scalar2=None, op0=ALU.mult), r=[wkk], w=["acc3"])
                        for k_ in range(3):
                            S.op("dve", lambda e, wk=wk, t=t, k_=k_: e.scalar_tensor_tensor(out=acc3[:], in0=wk[:, k_:k_ + 512], scalar=xcw[:, t, k_:k_ + 1], in1=acc3[:], op0=ALU.mult, op1=ALU.add), r=[wkk, "acc3"], w=["acc3"])
                        if t < 8:
                            dst_ = xs_t[:, t, :]
                            dk = "xs_t"
                        elif t < 10:
                            dst_ = bm_t[:, t - 8, :]
                            dk = "bm_t"
                        else:
                            dst_ = cm_t[:, t - 10, :]
                            dk = "cm_t"
                        S.op("act", lambda e, dst_=dst_, t=t: e.activation(out=dst_, in_=acc3[:], func=AF.Silu, bias=xcb[:, t:t + 1]), r=["acc3"], w=[dk])
                    wi = (wi + 1) % 3
                    wload(wts[wi], ("sw", wi), win0_b, KT, C_DT, 16)
                    bd = nb()
                    mm_acc(ps[bd][0:16, :], "ps%d" % bd, lambda kt, wi=wi: wts[wi][:, kt, 0:16], lambda kt: ht[0][:, kt, :], KT, [("sw", wi), ("ht", 0)])
                    S.op("act", lambda e, bd=bd: e.activation(out=dte_[:], in_=ps[bd][0:16, :], func=AF.Exp, bias=hv[:, 0:1]), r=["ps%d" % bd], w=["dtexp"])
                    S.op("act", lambda e: e.activation(out=dtT[:], in_=dte_[:], func=AF.Ln, bias=1.0), r=["dtexp"], w=["dtT"])
                    S.op("dve", lambda e: e.tensor_scalar(out=dtaT[:], in0=dtT[:], scalar1=aneg[:, 0:1], scalar2=None, op0=ALU.mult), r=["dtT"], w=["dtaT"])
                    for c in range(4):
                        q0 = c * 128
                        S.op("dve", lambda e, q0=q0: e.tensor_tensor_scan(out=acT[:, q0:q0 + 128], data0=ones32[:], data1=dtaT[:, q0:q0 + 128], initial=0.0, op0=ALU.mult, op1=ALU.add), r=["dtaT"], w=["acT"])
                        S.op("pe", lambda e, q0=q0: e.transpose(out=ps[7][:, 0:16], in_=dtT[:, q0:q0 + 128], identity=ident_f[0:16, 0:16]), r=["dtT"], w=["ps7"])
                        S.op("pe", lambda e, q0=q0: e.transpose(out=ps[7][:, 16:32], in_=acT[:, q0:q0 + 128], identity=ident_f[0:16, 0:16]), r=["acT", "ps7"], w=["ps7"])
                        S.op("dve", lambda e: e.tensor_copy(out=tok32[:], in_=ps[7][:, 0:32]), r=["ps7"], w=["tok32"])
                        for q4 in range(2):
                            def f(e, q4=q4, q0=q0):
                                for i in range(4):
                                    ins = e.transpose(out=psT[:, i * 128:(i + 1) * 128], in_=xs_t[:, q4 * 4 + i, q0:q0 + 128], identity=ident_b)
                                return ins
                            S.op("pe", f, r=["xs_t"], w=["ps6"])
                            S.op("act", lambda e, q4=q4: e.activation(out=xs_tok[:, q4 * 8:(q4 + 1) * 8, :].rearrange("p h d -> p (h d)"), in_=psT[:, 0:512], func=AF.Copy), r=["ps6"], w=["xs_tok"])
                        def f(e, q0=q0):
                            for i in range(2):
                                ins = e.transpose(out=psT[:, i * 128:(i + 1) * 128], in_=bm_t[:, i, q0:q0 + 128], identity=ident_b)
                            return ins
                        S.op("pe", f, r=["bm_t"], w=["ps6"])
                        S.op("act", lambda e: e.activation(out=bm_tok[:], in_=psT[:, 0:256], func=AF.Copy), r=["ps6"], w=["bm_tok"])
                        def st1(gh, q0):
                            ba = 2 * gh
                            def f(e):
                                for hh in range(8):
                                    ins = e.matmul(ps[ba + hh // 4][:, (hh % 4) * 128:(hh % 4 + 1) * 128], lhsT=es_sel[:, gh * 8 + hh, :], rhs=acT[:, q0:q0 + 128], start=True, stop=True)
                                return ins
                            S.op("pe", f, r=["acT"], w=["ps%d" % ba, "ps%d" % (ba + 1)])

                        def st2(gh, q0):
                            ba = 2 * gh
                            for hh in range(8):
                                S.op("dve", lambda e, hh=hh: e.tensor_scalar(out=seg[gh][:, hh, :], in0=ps[ba + hh // 4][:, (hh % 4) * 128:(hh % 4 + 1) * 128], scalar1=tok32[:, 16 + gh * 8 + hh:17 + gh * 8 + hh], scalar2=0.0, op0=ALU.subtract, op1=ALU.min),
                                     r=["ps%d" % ba, "ps%d" % (ba + 1), "tok32"], w=[("seg", gh)])

                        def st3(gh, q0):
                            ba = 2 * gh
                            S.op("act", lambda e: e.activation(out=es_[gh][:], in_=seg[gh][:], func=AF.Exp), r=[("seg", gh)], w=[("es", gh)])
                            for hb in range(2):
                                S.op("act", lambda e, hb=hb: e.activation(out=ea_[gh][:, hb * 4:(hb + 1) * 4, :].rearrange("p h q -> p (h q)"), in_=ps[ba + hb][:, :], func=AF.Exp), r=["ps%d" % ba, "ps%d" % (ba + 1)], w=[("ea", gh)])
                            for hb in range(2):
                                S.op("dve", lambda e, hb=hb: e.tensor_tensor(out=t8[gh][:, hb * 4:(hb + 1) * 4], in0=ps[ba + hb][:, :].rearrange("p (h q) -> p h q", q=128)[:, :, 127], in1=tok32[:, 16 + gh * 8 + hb * 4:16 + gh * 8 + hb * 4 + 4], op=ALU.subtract),
                                     r=["ps%d" % ba, "ps%d" % (ba + 1), "tok32"], w=[("t8", gh)])
                            S.op("act", lambda e: e.activation(out=dte8[gh][:], in_=t8[gh][:], func=AF.Exp), r=[("t8", gh)], w=[("dte8", gh)])
                            S.op("act", lambda e: e.activation(out=cd8[gh][:], in_=ea_[gh][:, :, 127], func=AF.Copy), r=[("ea", gh)], w=[("cd8", gh)])

                        def st4(gh, q0):
                            cbo = ps[4][:, gh * 128:(gh + 1) * 128]
                            S.op("pe", lambda e: e.matmul(cbo, lhsT=bm_t[:, gh, q0:q0 + 128], rhs=cm_t[:, gh, q0:q0 + 128], start=True, stop=True), r=["bm_t", "cm_t"], w=["ps4"])
                            S.op("dve", lambda e: e.tensor_tensor(out=cbm[gh][:], in0=cbo, in1=tri_f, op=ALU.mult), r=["ps4"], w=[("cbm", gh)])
                            S.op("dve", lambda e: e.tensor_tensor(out=MT[gh][:], in0=es_[gh][:], in1=cbm[gh][:].unsqueeze(1).broadcast_to([128, 8, 128]), op=ALU.mult), r=[("es", gh), ("cbm", gh)], w=[("MT", gh)])
                            S.op("pool", lambda e: e.tensor_tensor(out=ChT[gh][:], in0=ea_[gh][:], in1=cm_t[:, gh, q0:q0 + 128].unsqueeze(1).broadcast_to([128, 8, 128]), op=ALU.mult), r=[("ea", gh), "cm_t"], w=[("ChT", gh)])

                        def st5(gh, q0):
                            S.op("dve", lambda e: e.tensor_tensor(out=xdt_f[gh][:], in0=xs_tok[:, gh * 8:(gh + 1) * 8, :], in1=tok32[:, gh * 8:(gh + 1) * 8].unsqueeze(2).broadcast_to([128, 8, 64]), op=ALU.mult), r=["xs_tok", "tok32"], w=[("xdt_f", gh)])
                            for ev in range(2):
                                S.op("pool", lambda e, ev=ev: e.tensor_copy(out=xdt_pad[gh][:, ev::2, ev * 64:(ev + 1) * 64], in_=xdt_f[gh][:, ev::2, :]), r=[("xdt_f", gh)], w=[("xdt_pad", gh)])
                                S.op("act", lambda e, ev=ev: e.activation(out=S_pad[gh][:, ev::2, ev * 64:(ev + 1) * 64], in_=S_f[:, gh * 8 + ev:(gh + 1) * 8:2, :], func=AF.Copy), r=[("S_f", gh)], w=[("S_pad", gh)])
                            S.op("dve", lambda e: e.tensor_tensor(out=xdt_end[gh][:], in0=xdt_f[gh][:], in1=dte8[gh][:].unsqueeze(2).broadcast_to([128, 8, 64]), op=ALU.mult), r=[("xdt_f", gh), ("dte8", gh)], w=[("xdt_end", gh)])

                        def st6(gh, q0):
                            by = 5 + gh
                            def f(e):
                                for t in range(4):
                                    o_ = ps[by][:, t * 128:(t + 1) * 128]
                                    e.matmul(o_, lhsT=xdt_pad[gh][:, 2 * t, :], rhs=MT[gh][:, 2 * t, :], start=True, stop=False)
                                    e.matmul(o_, lhsT=xdt_pad[gh][:, 2 * t + 1, :], rhs=MT[gh][:, 2 * t + 1, :], start=False, stop=False)
                                    e.matmul(o_, lhsT=S_pad[gh][:, 2 * t, :], rhs=ChT[gh][:, 2 * t, :], start=False, stop=False)
                                    ins = e.matmul(o_, lhsT=S_pad[gh][:, 2 * t + 1, :], rhs=ChT[gh][:, 2 * t + 1, :], start=False, stop=True)
                                return ins
                            S.op("pe", f, r=[("xdt_pad", gh), ("MT", gh), ("S_pad", gh), ("ChT", gh)], w=["ps%d" % by])
                            S.op("pe", lambda e: e.matmul(ps[7][:, :], lhsT=bm_tok[:, gh * 128:(gh + 1) * 128], rhs=xdt_end[gh][:].rearrange("p h d -> p (h d)"), start=True, stop=True), r=["bm_tok", ("xdt_end", gh)], w=["ps7"])
                            S.op("dve", lambda e: e.tensor_tensor(out=S_f[:, gh * 8:(gh + 1) * 8, :], in0=S_f[:, gh * 8:(gh + 1) * 8, :], in1=cd8[gh][:].unsqueeze(2).broadcast_to([128, 8, 64]), op=ALU.mult), r=[("S_f", gh), ("cd8", gh), ("S_pad", gh)], w=[("S_f", gh)])
                            S.op("dve", lambda e: e.tensor_tensor(out=S_f[:, gh * 8:(gh + 1) * 8, :].rearrange("p h d -> p (h d)"), in0=ps[7][:, :], in1=S_f[:, gh * 8:(gh + 1) * 8, :].rearrange("p h d -> p (h d)"), op=ALU.add), r=[("S_f", gh), "ps7"], w=[("S_f", gh)])

                        def st7(gh, q0):
                            by = 5 + gh
                            for t in range(4):
                                ct = gh * 4 + t
                                S.op("dve", lambda e, t=t, ct=ct: e.scalar_tensor_tensor(out=yv[gh][:, t, :], in0=xs_t[:, ct, q0:q0 + 128], scalar=dc[:, ct:ct + 1], in1=ps[by][:, t * 128:(t + 1) * 128], op0=ALU.mult, op1=ALU.add), r=["xs_t", "ps%d" % by], w=[("yv", gh)])
                            S.op("dve", lambda e: e.tensor_tensor(out=yz[gh][:], in0=yv[gh][:], in1=zs_t[:, gh * 4:(gh + 1) * 4, q0:q0 + 128], op=ALU.mult), r=[("yv", gh), "zs_t"], w=[("yz", gh)])
                            S.op("act", lambda e: e.activation(out=sq4[gh][:], in_=yz[gh][:], func=AF.Square), r=[("yz", gh)], w=[("sq4", gh)])
                            no_ = ps[4][:, 256 + gh * 128:256 + (gh + 1) * 128]
                            def f(e):
                                for t in range(4):
                                    ins = e.matmul(no_, lhsT=ones_b, rhs=sq4[gh][:, t, :], start=(t == 0), stop=(t == 3))
                                return ins
                            S.op("pe", f, r=[("sq4", gh)], w=["ps4"])
                            rstd_from_ss(no_, 512, rs4[gh][:], rt4[gh][:], ["ps4"], ("rt4", gh), ("rs4", gh))
                            for t in range(4):
                                ct = gh * 4 + t
                                S.op("dve", lambda e, t=t, ct=ct: e.scalar_tensor_tensor(out=yb_t[:, ct, q0:q0 + 128], in0=yz[gh][:, t, :], scalar=v16[:, 4, ct:ct + 1], in1=rs4[gh][:], op0=ALU.mult, op1=ALU.mult), r=[("yz", gh), ("rs4", gh)], w=["yb_t"])

                        for st in (st1, st2, st3, st4, st5, st6, st7):
                            for gh in range(2):
                                st(gh, q0)
                    st_kt(yb_t, yown_d, 512, t0, 8, "yb_t", "yown_d", "ybst")
                S.flush()

        if stop_after >= 1.5:
            ssd_phase()

        def out_phase(wb, kto, resid, dst, final, ysrc):
            with ExitStack() as pes:
                yt = [S.sb(pes, "oy%d" % i, [128, kto, 512], BF16) for i in range(2)]
                wts = [S.sb(pes, "ow%d" % i, [128, kto, 256], BF16) for i in range(2)]
                xr = [S.sb(pes, "oxr%d" % i, [128, 512], F32) for i in range(3)]
                ot = [S.sb(pes, "oo%d" % i, [128, 512], F32) for i in range(3)]
                sq = [S.sb(pes, "fsq%d" % i, [128, 512], BF16) for i in range(2)]
                ssr = S.sb(pes, "ssr", [1, 512], F32)
                cnt = 0
                for g in range(NG):
                    b = g % 2
                    t0 = g * 512
                    ld_kt(yt[b], ysrc, 0, t0, kto, ("oy", b), r=["yall_d"])
                    for oc in range(8):
                        if oc % 2 == 0:
                            wi = (oc // 2) % 2
                            wload(wts[wi], ("ow", wi), wb, kto, oc * 128, 256)
                        wt_ = wts[(oc // 2) % 2]
                        co = (oc % 2) * 128
                        i3 = cnt % 3
                        cnt += 1
                        S.dma("sp", lambda e, i3=i3, oc=oc, t0=t0: [e.dma_start(out=xr[i3][:], in_=resid(oc * 128, (oc + 1) * 128, t0, 512))], r=["resid"], w=[("oxr", i3)])
                        bo = nb()
                        mm_acc(ps[bo][:, :], "ps%d" % bo, lambda kt, wt_=wt_, co=co: wt_[:, kt, co:co + 128], lambda kt, b=b: yt[b][:, kt, :], kto, [("ow", (oc // 2) % 2), ("oy", b)])
                        S.op("dve", lambda e, i3=i3, bo=bo: e.tensor_tensor(out=ot[i3][:], in0=ps[bo][:, :], in1=xr[i3][:], op=ALU.add), r=["ps%d" % bo, ("oxr", i3)], w=[("oo", i3)])
                        S.dma(os.environ.get("STQ", "pool"), lambda e, i3=i3, oc=oc, t0=t0: [e.dma_start(out=dst(oc * 128, (oc + 1) * 128, t0, 512), in_=ot[i3][:])], r=[("oo", i3)], w=[("resid_out", oc, t0)], slot=("oost", i3))
                        if final:
                            sb_ = oc % 2
                            S.op("act", lambda e, i3=i3, sb_=sb_: e.activation(out=sq[sb_][:], in_=ot[i3][:], func=AF.Square), r=[("oo", i3)], w=[("fsq", sb_)])
                            S.op("pe", lambda e, oc=oc, sb_=sb_: e.matmul(ps[6][:, :], lhsT=ones_b, rhs=sq[sb_][:], start=(oc == 0), stop=(oc == 7)), r=[("fsq", sb_), "cm_b"], w=["ps6"])
                    if final:
                        S.op("act", lambda e: e.activation(out=ssr[:], in_=ps[6][0:1, :], func=AF.Copy), r=["ps6"], w=["ssr"])
                        S.dma("pool", lambda e, t0=t0: [e.dma_start(out=ssown_d[0:1, t0:t0 + 512], in_=ssr[:])], r=["ssr"], w=[("ssown_d", t0)], slot="ssst")
                S.flush()

        def final_norm_phase():
            with ExitStack() as pes:
                ss2 = [S.sb(pes, "ss2_%d" % i, [2, 512], F32) for i in range(2)]
                rt = S.sb(pes, "frt", [128, 512], F32)
                rs = [S.sb(pes, "frs%d" % i, [128, 512], F32) for i in range(2)]
                xr = [S.sb(pes, "fx%d" % i, [128, 8, 512], F32) for i in range(2)]
                ot = [S.sb(pes, "fo%d" % i, [128, 8, 512], F32) for i in range(2)]
                for g in range(NG):
                    b = g % 2
                    t0 = g * 512
                    S.dma("sp", lambda e, b=b, t0=t0: [e.dma_start(out=ss2[b][:], in_=ssall_d[0:2, t0:t0 + 512])], w=[("ss2", b)])
                    ld_kt(xr[b], x2own_d, 0, t0, 8, ("fx", b))
                    S.op("pe", lambda e, b=b: e.matmul(ps[6][:, :], lhsT=ones_f[0:2, :], rhs=ss2[b][:], start=True, stop=True), r=[("ss2", b)], w=["ps6"])
                    rstd_from_ss(ps[6][:, :], D, rs[b][:], rt[:], ["ps6"], "frt", ("frs", b))
                    for oc in range(8):
                        S.op("dve", lambda e, b=b, oc=oc: e.scalar_tensor_tensor(out=ot[b][:, oc, :], in0=xr[b][:, oc, :], scalar=v16[:, 3, oc:oc + 1], in1=rs[b][:], op0=ALU.mult, op1=ALU.mult), r=[("fx", b), ("frs", b), "v16"], w=[("fo", b)])
                    st_kt(ot[b], outT, 0, t0, 8, ("fo", b), "final_out", ("fost", b))
                S.flush()

        RG = [[0, 1], [2, 3], [4, 5], [6, 7]]
        if stop_after >= 2:
            S.collective([(lambda e, j=j: e.collective_compute("AllGather", ALU.bypass, replica_groups=RG, ins=[yown_d[j * 128:(j + 1) * 128, :].opt()], outs=[yall_d[j * 256:(j + 1) * 256, :].opt()])) for j in range(14)])
            out_phase(wout0_b, 28, lambda a, b_, t_, w_: xTo[a:b_, t_:t_ + w_], x1o_ap, False, yall_d)
            S.collective([(lambda e, th=th, j=j: e.collective_compute("AllGather", ALU.bypass, replica_groups=RG, ins=[x1own_d[th, j * 128:(j + 1) * 128, :].opt()], outs=[x1T_d[th, j * 256:(j + 1) * 256, :].opt()])) for th in range(2) for j in range(8)])
        if stop_after >= 3:
            norm_phase(x1g_ap, 2)

        def rope_apply(pes_tiles, t_ps, cos2, sin2, out_bf, okey, scale, rkeys, cskeys=("cs",)):
            t_sb, o1, o2 = pes_tiles
            S.op("act", lambda e: e.activation(out=t_sb[:], in_=t_ps, func=AF.Copy), r=rkeys, w=["t_sb"])
            S.op("pe", lambda e: e.matmul(ps[7][0:64, :], lhsT=cm_f[0:64, 3, 0:64], rhs=t_sb[:], start=True, stop=True), r=["t_sb"], w=["ps7"])
            S.op("dve", lambda e: e.tensor_tensor(out=o1[:], in0=t_sb[:], in1=cos2, op=ALU.mult), r=["t_sb"] + list(cskeys), w=["o1"])
            S.op("dve", lambda e: e.tensor_tensor(out=o2[:], in0=ps[7][0:64, :], in1=sin2, op=ALU.mult), r=["ps7"] + list(cskeys), w=["o2"])
            S.op("dve", lambda e: e.scalar_tensor_tensor(out=out_bf, in0=o1[:], scalar=scale, in1=o2[:], op0=ALU.mult, op1=ALU.add), r=["o1", "o2"], w=[okey])

        def l1_in_phase():
            with ExitStack() as pes:
                ht2 = [S.sb(pes, "ht%d" % i, [128, KT, 512], BF16) for i in range(2)]
                wts = [S.sb(pes, "wt%d" % i, [128, KT, 256], BF16) for i in range(4)]
                cq = S.sb(pes, "cq", [128, 6, 512], F32)
                sq = [S.sb(pes, "lsq%d" % i, [128, 512], BF16) for i in range(2)]
                rt = S.sb(pes, "lrt", [128, 512], F32)
                rs = S.sb(pes, "lrs", [128, 512], F32)
                cqn_t = S.sb(pes, "cqn_t", [128, 6, 512], BF16)
                zc_t = S.sb(pes, "zc_t", [128, 8, 512], BF16)
                ym = S.sb(pes, "ym", [128, 2, 512], BF16)
                nwq = S.sb(pes, "nwq", [128, 6], F32)
                nwk = S.sb(pes, "nwk", [128, 4], F32)
                ivf = S.sb(pes, "ivf", [64, 1], F32)
                posi = S.sb(pes, "posi", [1, 512], I32)
                posf = S.sb(pes, "posf", [1, 512], F32)
                a1 = S.sb(pes, "a1", [64, 512], F32)
                a2 = S.sb(pes, "a2", [64, 512], F32)
                a3 = S.sb(pes, "a3", [64, 512], F32)
                ki = S.sb(pes, "ki", [64, 512], I32)
                cs = S.sb(pes, "cs", [64, 2, 512], F32)
                rtl = (S.sb(pes, "t_sb", [64, 512], F32), S.sb(pes, "o1", [64, 512], F32), S.sb(pes, "o2", [64, 512], F32))
                krb = S.sb(pes, "krb", [64, 512], BF16)
                mtmp = mem_tmp(pes)
                for t_, src in ((nwq, qnw), (nwk, kvnw), (ivf, invf)):
                    S.dma("sp", lambda e, t_=t_, src=src: [e.dma_start(out=t_[:], in_=src)], w=[("c", id(t_))])
                S.flush()
                for g in range(NG):
                    t0 = g * 512
                    hb_ = g % 2
                    ht = [ht2[hb_]]
                    hk = ("ht", hb_)
                    ld_kt(ht[0], hT_d, 0, t0, KT, hk)
                    for (c0, nt, nw, dst, r0) in ((C_CQ, 6, nwq, cqn_d, 0), (C_CKV, 4, nwk, lat_d, 0)):
                        for t in range(nt):
                            if t % 2 == 0:
                                wload(wts[(t // 2) % 4], ("mw", (t // 2) % 4), win1_b, KT, c0 + t * 128, 256)
                            wt_ = wts[(t // 2) % 4]
                            co = (t % 2) * 128
                            bq = nb()
                            mm_acc(ps[bq][:, :], "ps%d" % bq, lambda kt, wt_=wt_, co=co: wt_[:, kt, co:co + 128], lambda kt, htg=ht[0]: htg[:, kt, :], KT, [("mw", (t // 2) % 4), hk])
                            S.op("act", lambda e, bq=bq, t=t: e.activation(out=cq[:, t, :], in_=ps[bq][:, :], func=AF.Copy), r=["ps%d" % bq], w=["cq"])
                            sb_ = t % 2
                            S.op("act", lambda e, t=t, sb_=sb_: e.activation(out=sq[sb_][:], in_=cq[:, t, :], func=AF.Square), r=["cq"], w=[("lsq", sb_)])
                            S.op("pe", lambda e, t=t, sb_=sb_, nt=nt: e.matmul(ps[6][:, :], lhsT=ones_b, rhs=sq[sb_][:], start=(t == 0), stop=(t == nt - 1)), r=[("lsq", sb_)], w=["ps6"])
                        rstd_from_ss(ps[6][:, :], nt * 128, rs[:], rt[:], ["ps6"], "lrt", "lrs")
                        for t in range(nt):
                            S.op("dve", lambda e, t=t, nw=nw: e.scalar_tensor_tensor(out=cqn_t[:, t, :], in0=cq[:, t, :], scalar=nw[:, t:t + 1], in1=rs[:], op0=ALU.mult, op1=ALU.mult), r=["cq", "lrs"], w=["cqn_t"])
                        st_kt(cqn_t, dst, 0, t0, nt, "cqn_t", ("d", id(dst)), "cqnst")
                    S.dma("sp", lambda e, t0=t0: [e.dma_start(out=posi[:], in_=pos[0:1, t0:t0 + 512])], w=["posi"])
                    S.op("dve", lambda e: e.tensor_copy(out=posf[:], in_=posi[:]), r=["posi"], w=["posf"])
                    S.op("pe", lambda e: e.matmul(ps[5][0:64, :], lhsT=ones_f[0:1, 0:64], rhs=posf[:], start=True, stop=True), r=["posf"], w=["ps5"])
                    C1 = 6.28125
                    C2 = TWO_PI - C1
                    PIS = 3.1415925
                    S.op("dve", lambda e: e.tensor_scalar(out=a1[:], in0=ps[5][0:64, :], scalar1=ivf[:, 0:1], scalar2=None, op0=ALU.mult), r=["ps5"], w=["a1"])
                    S.op("dve", lambda e: e.tensor_scalar(out=a2[:], in0=a1[:], scalar1=1.0 / TWO_PI, scalar2=None, op0=ALU.mult), r=["a1"], w=["a2"])
                    S.op("dve", lambda e: e.tensor_copy(out=ki[:], in_=a2[:]), r=["a2"], w=["ki"])
                    S.op("dve", lambda e: e.tensor_copy(out=a2[:], in_=ki[:]), r=["ki"], w=["a2"])
                    S.op("dve", lambda e: e.scalar_tensor_tensor(out=a1[:], in0=a2[:], scalar=-C1, in1=a1[:], op0=ALU.mult, op1=ALU.add), r=["a2", "a1"], w=["a1"])
                    S.op("dve", lambda e: e.scalar_tensor_tensor(out=a1[:], in0=a2[:], scalar=-C2, in1=a1[:], op0=ALU.mult, op1=ALU.add), r=["a2", "a1"], w=["a1"])

                    def fold(t_, key):
                        S.op("dve", lambda e: e.tensor_scalar(out=a2[:], in0=t_[:], scalar1=PI, scalar2=-TWO_PI, op0=ALU.is_gt, op1=ALU.mult), r=[key], w=["a2"])
                        S.op("dve", lambda e: e.tensor_tensor(out=t_[:], in0=t_[:], in1=a2[:], op=ALU.add), r=[key, "a2"], w=[key])
                        S.op("dve", lambda e: e.tensor_scalar(out=a2[:], in0=t_[:], scalar1=-PI, scalar2=TWO_PI, op0=ALU.is_lt, op1=ALU.mult), r=[key], w=["a2"])
                        S.op("dve", lambda e: e.tensor_tensor(out=t_[:], in0=t_[:], in1=a2[:], op=ALU.add), r=[key, "a2"], w=[key])
                        S.op("dve", lambda e: e.tensor_scalar(out=t_[:], in0=t_[:], scalar1=PIS, scalar2=-PIS, op0=ALU.min, op1=ALU.max), r=[key], w=[key])
                    fold(a1, "a1")
                    S.op("act", lambda e: e.activation(out=cs[:, 1, :], in_=a1[:], func=AF.Sin), r=["a1"], w=["cs"])
                    S.op("dve", lambda e: e.tensor_scalar(out=a3[:], in0=a1[:], scalar1=PI / 2, scalar2=None, op0=ALU.add), r=["a1"], w=["a3"])
                    fold(a3, "a3")
                    S.op("act", lambda e: e.activation(out=cs[:, 0, :], in_=a3[:], func=AF.Sin), r=["a3"], w=["cs"])
                    S.dma("pool", lambda e, t0=t0: [e.dma_start(out=cs_d[c_, :, t0:t0 + 512], in_=cs[:, c_, :]) for c_ in range(2)], r=["cs"], w=[("cs_d", t0)], slot="csst", n=2)
                    wload(wts[0], ("mw", 0), win1_b, KT, C_KR, 64)
                    bk = nb()
                    mm_acc(ps[bk][0:64, :], "ps%d" % bk, lambda kt: wts[0][:, kt, 0:64], lambda kt, htg=ht[0]: htg[:, kt, :], KT, [("mw", 0), hk])
                    rope_apply(rtl, ps[bk][0:64, :], cs[:, 0, :], cs[:, 1, :], krb[:], "krb", 1.0, ["ps%d" % bk])
                    S.dma("pool", lambda e, t0=t0: [e.dma_start(out=lat_d[512:576, t0:t0 + 512], in_=krb[:])], r=["krb"], w=[("lat_kr", t0)], slot="krst")
                    for t in range(8):
                        if t % 2 == 0:
                            wload(wts[(t // 2) % 4], ("mw", (t // 2) % 4), win1_b, KT, C_ZC + t * 128, 256)
                        wt_ = wts[(t // 2) % 4]
                        co = (t % 2) * 128
                        bz = nb()
                        mm_acc(ps[bz][:, :], "ps%d" % bz, lambda kt, wt_=wt_, co=co: wt_[:, kt, co:co + 128], lambda kt, htg=ht[0]: htg[:, kt, :], KT, [("mw", (t // 2) % 4), hk])
                        S.op("act", lambda e, bz=bz, t=t: e.activation(out=zc_t[:, t, :], in_=ps[bz][:, :], func=AF.Silu), r=["ps%d" % bz], w=["zc_t"])
                    st_kt(zc_t, zc_d, 0, t0, 8, "zc_t", "zc_d", "zcst")
                    mem_attn_group(1, ht2, hb_, wts, win1_b, C_MQ1, C_MZ1, ym, "ym1", mtmp)
                    st_kt(ym, yown1_d, 1024, t0, 2, "ym1", "yown1_d", "ym1st")
                S.flush()

        if stop_after >= 4:
            l1_in_phase()

        def mla_phase():
            with ExitStack() as pes:
                ckvn = S.sb(pes, "ckvn", [128, 4, NTOK], BF16)
                kr = S.sb(pes, "kr", [128, NTOK], BF16)
                KhT = S.sb(pes, "KhT", [128, NTOK], BF16)
                Vh = S.sb(pes, "Vh", [128, NCH, 128], BF16)
                mk_b_ = S.sb(pes, "mk_b", [128, 4, 512], BF16)
                wkv = S.sb(pes, "wkv", [128, 4, 256], BF16)
                wq = S.sb(pes, "wq", [128, 6, 192], BF16)
                cqg = [S.sb(pes, "cqg%d" % i, [128, 6, 512], BF16) for i in range(2)]
                csg = [S.sb(pes, "csg%d" % i, [64, 2, 512], F32) for i in range(2)]
                zcg = [S.sb(pes, "zcg%d" % i, [128, 512], BF16) for i in range(2)]
                Qn = S.sb(pes, "Qn", [128, 512], BF16)
                Qr = S.sb(pes, "Qr", [128, 512], BF16)
                lacc = S.sb(pes, "lacc", [128, 512], F32)
                lacc_b = S.sb(pes, "lacc_b", [128, 512], BF16)
                pT = [S.sb(pes, "pT%d" % i, [128, 512], BF16) for i in range(6)]
                rl = S.sb(pes, "arl", [128, 512], F32)
                osb = S.sb(pes, "aosb", [128, 512], F32)
                yc = [S.sb(pes, "yc%d" % i, [128, 512], BF16) for i in range(2)]
                rtl = (S.sb(pes, "t_sb", [64, 512], F32), S.sb(pes, "o1", [64, 512], F32), S.sb(pes, "o2", [64, 512], F32))
                S.dma("sp", lambda e: [e.dma_start(out=mk_b_[:], in_=amask[:, :, :])], w=["mk_b"])
                for rt_ in range(4):
                    S.dma("sp", lambda e, rt_=rt_: [e.dma_start(out=ckvn[:, rt_, :], in_=lat_d[rt_ * 128:(rt_ + 1) * 128, :])], w=[("ckvn", rt_)])
                S.op("pool", lambda e: e.memset(kr[64:128, :], 0.0), w=["kr_pad"])
                S.op("pool", lambda e: e.memset(Qr[64:128, :], 0.0), w=["Qr_pad"])
                S.dma("sp", lambda e: [e.dma_start(out=kr[0:64, :], in_=lat_d[512:576, :])], w=["kr"])
                S.flush()
                sc_ = 192.0 ** -0.5
                it = 0
                for h in range(8):
                    S.dma("sp", lambda e, h=h: [e.dma_start(out=wkv[:, kt, :], in_=wukv_b[kt * 128:(kt + 1) * 128, h * 256:(h + 1) * 256]) for kt in range(4)], w=["wkv"], n=4)
                    S.dma("sp", lambda e, h=h: [e.dma_start(out=wq[:, kt, :], in_=wuq_b[kt * 128:(kt + 1) * 128, h * 192:(h + 1) * 192]) for kt in range(6)], w=["wq"], n=6)
                    for kg in range(NG):
                        bk = nb()
                        mm_acc(ps[bk][:, :], "ps%d" % bk, lambda kt: wkv[:, kt, 0:128], lambda kt, kg=kg: ckvn[:, kt, kg * 512:(kg + 1) * 512], 4, ["wkv"])
                        if kg % 2 == 0:
                            S.op("act", lambda e, bk=bk, kg=kg: e.activation(out=KhT[:, kg * 512:(kg + 1) * 512], in_=ps[bk][:, :], func=AF.Copy), r=["ps%d" % bk], w=["KhT"])
                        else:
                            S.op("dve", lambda e, bk=bk, kg=kg: e.tensor_copy(out=KhT[:, kg * 512:(kg + 1) * 512], in_=ps[bk][:, :]), r=["ps%d" % bk], w=["KhT"])
                    for kb4 in range(NCH // 4):
                        bv = nb()
                        def f(e, kb4=kb4, bv=bv):
                            for i in range(4):
                                kb = kb4 * 4 + i
                                for kt in range(4):
                                    ins = e.matmul(ps[bv][:, i * 128:(i + 1) * 128], lhsT=ckvn[:, kt, kb * 128:(kb + 1) * 128], rhs=wkv[:, kt, 128:256], start=(kt == 0), stop=(kt == 3))
                            return ins
                        S.op("pe", f, r=["wkv"], w=["ps%d" % bv])
                        if kb4 % 2 == 0:
                            S.op("dve", lambda e, bv=bv, kb4=kb4: e.tensor_copy(out=Vh[:, kb4 * 4:(kb4 + 1) * 4, :].rearrange("p k d -> p (k d)"), in_=ps[bv][:, :]), r=["ps%d" % bv], w=["Vh"])
                        else:
                            S.op("act", lambda e, bv=bv, kb4=kb4: e.activation(out=Vh[:, kb4 * 4:(kb4 + 1) * 4, :].rearrange("p k d -> p (k d)"), in_=ps[bv][:, :], func=AF.Copy), r=["ps%d" % bv], w=["Vh"])
                    S.flush()
                    for qg in range(NG):
                        b = qg % 2
                        t0 = qg * 512
                        ld_kt(cqg[b], cqn_d, 0, t0, 6, ("cqg", b))
                        S.dma("sp", lambda e, b=b, t0=t0: [e.dma_start(out=csg[b][:, c_, :], in_=cs_d[c_, :, t0:t0 + 512]) for c_ in range(2)], w=[("csg", b)], n=2)
                        S.dma("sp", lambda e, b=b, t0=t0, h=h: [e.dma_start(out=zcg[b][:], in_=zc_d[h * 128:(h + 1) * 128, t0:t0 + 512])], w=[("zcg", b)])
                        bq = nb()
                        mm_acc(ps[bq][:, :], "ps%d" % bq, lambda kt: wq[:, kt, 0:128], lambda kt, b=b: cqg[b][:, kt, :], 6, ["wq", ("cqg", b)])
                        S.op("act", lambda e, bq=bq: e.activation(out=Qn[:], in_=ps[bq][:, :], func=AF.Copy, scale=sc_), r=["ps%d" % bq], w=["Qn"])
                        br = nb()
                        mm_acc(ps[br][0:64, :], "ps%d" % br, lambda kt: wq[:, kt, 128:192], lambda kt, b=b: cqg[b][:, kt, :], 6, ["wq", ("cqg", b)])
                        rope_apply(rtl, ps[br][0:64, :], csg[b][:, 0, :], csg[b][:, 1, :], Qr[0:64, :], "Qr", 1.0, ["ps%d" % br], cskeys=[("csg", b)])
                        S.op("act", lambda e: e.activation(out=Qr[0:64, :], in_=Qr[0:64, :], func=AF.Copy, scale=sc_), r=["Qr"], w=["Qr"])
                        po, pl = 4 + qg % 2, 6 + qg % 2
                        nkb = 4 * qg + 4
                        LA = 2
                        for step in range(nkb + LA):
                            if step < nkb:
                                kb = step
                                bs = kb % 4
                                pi = kb % 6

                                c0 = max(0, kb - 4 * qg) * 128

                                def f(e, bs=bs, kb=kb, c0=c0):
                                    e.matmul(ps[bs][:, c0:], lhsT=KhT[:, kb * 128:(kb + 1) * 128], rhs=Qn[:, c0:], start=True, stop=False)
                                    return e.matmul(ps[bs][:, c0:], lhsT=kr[:, kb * 128:(kb + 1) * 128], rhs=Qr[:, c0:], start=False, stop=True)
                                S.op("pe", f, r=["KhT", "Qn", "Qr"], w=["ps%d" % bs])
                                S.op("act", lambda e, bs=bs, pi=pi, c0=c0: e.activation(out=pT[pi][:, c0:], in_=ps[bs][:, c0:], func=AF.Exp), r=["ps%d" % bs], w=[("pT", pi)])
                                if kb >= 4 * qg:
                                    S.op("pool", lambda e, pi=pi, d_=kb - 4 * qg, c0=c0: e.tensor_tensor(out=pT[pi][:, c0:], in0=pT[pi][:, c0:], in1=mk_b_[:, d_, c0:], op=ALU.mult), r=[("pT", pi), "mk_b"], w=[("pT", pi)])
                                if kb == 0:
                                    S.op("dve", lambda e, pi=pi: e.tensor_copy(out=lacc[:], in_=pT[pi][:]), r=[("pT", pi)], w=["lacc"])
                                else:
                                    S.op("dve", lambda e, pi=pi, c0=c0: e.tensor_tensor(out=lacc[:, c0:], in0=lacc[:, c0:], in1=pT[pi][:, c0:], op=ALU.add), r=[("pT", pi), "lacc"], w=["lacc"])
                            if step >= LA:
                                kb = step - LA
                                pi = kb % 6
                                c0 = max(0, kb - 4 * qg) * 128
                                S.op("pe", lambda e, pi=pi, kb=kb, po=po, nkb=nkb, c0=c0: e.matmul(ps[po][:, c0:], lhsT=Vh[:, kb, :], rhs=pT[pi][:, c0:], start=(kb == 0), stop=(kb == nkb - 1)), r=[("pT", pi), "Vh"], w=["ps%d" % po])
                        S.op("dve", lambda e: e.tensor_copy(out=lacc_b[:], in_=lacc[:]), r=["lacc"], w=["lacc_b"])
                        S.op("pe", lambda e, pl=pl: e.matmul(ps[pl][:, :], lhsT=ones_b, rhs=lacc_b[:], start=True, stop=True), r=["lacc_b"], w=["ps%d" % pl])
                        S.op("dve", lambda e, pl=pl: e.reciprocal(out=rl[:], in_=ps[pl][:, :]), r=["ps%d" % pl], w=["arl"])
                        S.op("dve", lambda e, po=po: e.tensor_tensor(out=osb[:], in0=ps[po][:, :], in1=rl[:], op=ALU.mult), r=["ps%d" % po, "arl"], w=["aosb"])
                        S.op("pool", lambda e, b=b: e.tensor_tensor(out=yc[b][:], in0=osb[:], in1=zcg[b][:], op=ALU.mult), r=["aosb", ("zcg", b)], w=[("yc", b)])
                        S.dma("pool", lambda e, b=b, h=h, t0=t0: [e.dma_start(out=yown1_d[h * 128:(h + 1) * 128, t0:t0 + 512], in_=yc[b][:])], r=[("yc", b)], w=[("yown1_d", h, t0)], slot=("ycst", b))
                S.flush()

        if stop_after >= 5:
            mla_phase()
        if stop_after >= 6:
            S.collective([(lambda e, j=j: e.collective_compute("AllGather", ALU.bypass, replica_groups=RG, ins=[yown1_d[j * 128:(j + 1) * 128, :].opt()], outs=[yall1_d[j * 256:(j + 1) * 256, :].opt()])) for j in range(10)])
            out_phase(wout1_b, 20, x1o_ap, lambda a, b_, t_, w_: x2own_d[a:b_, t_:t_ + w_], True, yall1_d)
            S.collective([lambda e: e.collective_compute("AllGather", ALU.bypass, replica_groups=RG, ins=[ssown_d.opt()], outs=[ssall_d.opt()])])
            final_norm_phase()
        scr = {"hT": hT_d, "yall": yall_d, "yown": yown_d, "yown1": yown1_d, "yall1": yall1_d, "cqn": cqn_d, "lat": lat_d, "zc": zc_d}
        for name in dbg:
            rows = scr[name].shape[0]
            for r0 in range(0, rows, 128):
                r1 = min(rows, r0 + 128)
                S.dma("sp", lambda e, name=name, r0=r0, r1=r1: [e.dma_start(out=dbg[name][r0:r1, :], in_=scr[name][r0:r1, :])], w=[("dbgo", name, r0)], slot="dbgslot")
        S.flush()
    return nc


def _host_inputs(inputs, NTOK):
    f = lambda a: np.ascontiguousarray(np.asarray(a))
    g = lambda k: np.asarray(inputs[k], np.float32)
    colT = lambda v: f(np.asarray(v, np.float32).reshape(-1, 128).T)
    ar = np.arange
    cmat = np.zeros((128, 4, 128), np.float32)
    cmat[:, 0, :] = np.eye(128)
    cmat[:, 1, :] = 1.0
    cmat[:, 2, :] = np.triu(np.ones((128, 128)))
    rot = np.zeros((64, 64), np.float32)
    for i in range(32):
        rot[i + 32, i] = -1.0
        rot[i, i + 32] = 1.0
    cmat[:64, 3, :64] = rot
    amask = np.zeros((128, 4, 512), np.float32)
    for d_ in range(4):
        amask[:, d_, :] = (np.arange(128)[:, None] + d_ * 128 <= np.arange(512)[None, :])
    esel = np.zeros((16, 16, 128), np.float32)
    for h in range(16):
        esel[h, h, :] = 1.0
    inv = (10000.0 ** (-np.arange(32, dtype=np.float32) / 32)).astype(np.float32)
    invf = f(np.concatenate([inv, inv]).reshape(64, 1))
    own0 = [np.concatenate([mm * 512 + ar(512), 1024 + mm * 1024 + ar(1024), 3072 + mm * 256 + ar(256)]) for mm in range(2)]
    own1 = [np.concatenate([mm * 1024 + ar(1024), 2048 + mm * 256 + ar(256)]) for mm in range(2)]
    rows0 = np.concatenate([own0[mm][j * 128:(j + 1) * 128] for j in range(14) for mm in range(2)])
    rows1 = np.concatenate([own1[mm][j * 128:(j + 1) * 128] for j in range(10) for mm in range(2)])
    perm_feat = np.concatenate([mm * 1024 + j * 128 + ar(128) for j in range(8) for mm in range(2)])
    per_m = []
    for m in range(2):
        cols0 = np.concatenate([m * 512 + ar(512), 1024 + m * 512 + ar(512), 2048 + m * 512 + ar(512), 3072 + m * 512 + ar(512),
                                4096 + m * 1024 + ar(1024),
                                6144 + m * 1024 + ar(1024), 6144 + 2048 + m * 256 + ar(256), 6144 + 2560 + m * 256 + ar(256),
                                9216 + m * 16 + ar(16), 9248 + m * 256 + ar(256), 9760 + m * 256 + ar(256)])
        cols1 = np.concatenate([ar(1344), 1344 + m * 1024 + ar(1024), 3392 + m * 256 + ar(256), 3904 + m * 256 + ar(256)])
        xbc_ch = np.concatenate([m * 1024 + ar(1024), 2048 + m * 256 + ar(256), 2560 + m * 256 + ar(256)])
        vec16 = np.zeros((128, 5, 16), np.float32)
        vec16[:, 0, :] = colT(g("mem_norm_w"))
        vec16[:, 1, :] = colT(g("norm0_w"))
        vec16[:, 2, :] = colT(g("norm1_w")[perm_feat])
        vec16[:, 3, :8] = colT(g("final_norm_w")[m * 1024:(m + 1) * 1024])
        vec16[:, 4, :8] = colT(g("ssd_norm_w")[m * 1024:(m + 1) * 1024])
        d = dict(
            w_in0=f(g("w_in0")[:, cols0]), w_out0=f(g("w_out0")[rows0][:, m * 1024:(m + 1) * 1024]), w_in1=f(g("w_in1")[perm_feat][:, cols1]), w_out1=f(g("w_out1")[rows1][:, m * 1024:(m + 1) * 1024]),
            w_uq=f(g("mla_w_uq")[:, m * 1536:(m + 1) * 1536]), w_ukv=f(g("mla_w_ukv")[:, m * 2048:(m + 1) * 2048]),
            mem_k0=f(g("mem_k0")[:, m * 256:(m + 1) * 256]), mem_k1=f(g("mem_k1")[:, m * 256:(m + 1) * 256]),
            mem_v0=f(g("mem_v0")[:, m * 256:(m + 1) * 256]), mem_v1=f(g("mem_v1")[:, m * 256:(m + 1) * 256]),
            vec16=f(vec16),
            dcol=colT(np.repeat(g("ssd_d")[m * 16:(m + 1) * 16], 64)),
            sccw=f(g("sc_conv_w")[:, m * 512:(m + 1) * 512].reshape(3, 4, 128).transpose(2, 1, 0)),
            ssdcw=f(g("ssd_conv_w")[:, xbc_ch].reshape(4, 12, 128).transpose(2, 1, 0)),
            ssdcb=f(g("ssd_conv_b")[xbc_ch].reshape(12, 128).T),
            hvec=f(np.stack([g("ssd_dt_bias")[m * 16:(m + 1) * 16], g("ssd_a_log")[m * 16:(m + 1) * 16]], axis=1)),
            qnw=colT(g("mla_q_norm_w")), kvnw=colT(g("mla_kv_norm_w")),
            cmat=cmat, esel=esel, invf=invf, amask=amask.astype(ml_dtypes.bfloat16))
        per_m.append(d)
    maps = []
    x = np.asarray(inputs["x"])
    mem = np.asarray(inputs["mem"])
    posi = np.asarray(inputs["positions"])
    xT = [f(x[b, :NTOK].T) for b in range(4)]
    mT = [f(mem[b].T) for b in range(4)]
    for c in range(8):
        b, m = c // 2, c % 2
        mp = dict(per_m[m])
        mp["xT"] = xT[b]
        mp["xTo"] = f(xT[b][m * 1024:(m + 1) * 1024])
        mp["memT"] = mT[b]
        mp["pos"] = f(posi[b, :NTOK].reshape(1, NTOK).astype(np.int32))
        maps.append(mp)
    return maps


def kernel(**inputs):
    NTOK = 8192
    nc = build_program(NTOK)
    maps = _host_inputs(inputs, NTOK)
    res = run_bass_kernel_spmd(nc, maps, core_ids=list(range(8)))
    out = np.stack([np.concatenate([np.asarray(res.results[2 * b + m]["outT"]).T for m in range(2)], axis=1) for b in range(4)], axis=0)
    return np.ascontiguousarray(out.astype(np.float32))
```
